# Optimizing a Trainium2 kernel written in Bass

```python
import math
import jax, jax.numpy as jnp
from jax import lax
import numpy as np

D_MODEL = 1024
BATCH = 32
SEQ = 2048
DEPTH = 2
DEC_BATCH = 16
DEC_SEQ = 32
PAST_LEN = 4096

CHUNK = 64
N_A = DEPTH // 2
N_B = DEPTH - N_A

RET_HEADS = 4
RET_DK = D_MODEL // RET_HEADS
RET_DV = D_MODEL // RET_HEADS
RET_QK_W = RET_HEADS * RET_DK
RET_W = RET_HEADS * RET_DV

DIFF_HEADS = 4
DIFF_DQK = D_MODEL // (2 * DIFF_HEADS)
DIFF_DV = 2 * DIFF_DQK
DIFF_QW = DIFF_HEADS * 2 * DIFF_DQK
DIFF_VW = DIFF_HEADS * DIFF_DV

MEM_LEN = 256
MEM_HEADS = 4
MEM_HD = D_MODEL // 8
MEM_W = MEM_HEADS * MEM_HD

MIX_W = RET_W + MEM_W
SPLIT_A = (RET_QK_W, 2 * RET_QK_W, 2 * RET_QK_W + RET_W, 2 * RET_QK_W + 2 * RET_W, 2 * RET_QK_W + 2 * RET_W + MEM_W)
IN_A = 2 * RET_QK_W + 2 * RET_W + 2 * MEM_W
SPLIT_B = (DIFF_QW, DIFF_QW + DIFF_VW, DIFF_QW + DIFF_VW + MEM_W)
IN_B = DIFF_QW + DIFF_VW + 2 * MEM_W

Q_BLOCK = 128
ROPE_BASE = 10000.0
EPS = 1e-6

kernel_name = "yoco_retention_diffattn_stream_step"


def rms_norm(x, gain=None, eps=EPS):
    xf = x.astype(jnp.float32)
    y = xf * lax.rsqrt(jnp.mean(jnp.square(xf), axis=-1, keepdims=True) + eps)
    if gain is not None:
        y = y * gain.astype(jnp.float32)
    return y.astype(x.dtype)


def rope(x, pos):
    half = x.shape[-1] // 2
    inv = ROPE_BASE ** (-jnp.arange(half, dtype=jnp.float32) / half)
    ang = pos.astype(jnp.float32)[:, None] * inv[None, :]
    cos = jnp.cos(ang)[None, :, None, :]
    sin = jnp.sin(ang)[None, :, None, :]
    x1, x2 = x[..., :half], x[..., half:]
    return jnp.concatenate([x1 * cos - x2 * sin, x2 * cos + x1 * sin], axis=-1)


def ret_log_decay():
    return jnp.log(1.0 - 2.0 ** (-5.0 - jnp.arange(RET_HEADS, dtype=jnp.float32)))


def retention_block(q, k, v, S, log_g):
    L = q.shape[1]
    idx = jnp.arange(L, dtype=jnp.float32)
    diff = idx[:, None] - idx[None, :]
    decay = jnp.where(diff >= 0, jnp.exp(log_g[:, None, None] * jnp.maximum(diff, 0.0)[None]), 0.0)
    scores = jnp.einsum('blhd,bmhd->bhlm', q, k) * decay[None]
    inner = jnp.einsum('bhlm,bmhe->blhe', scores, v)
    cross = jnp.einsum('blhd,bhde->blhe', q, S) * jnp.exp(log_g[None, :, None] * (idx + 1.0)[:, None, None])
    kw = k * jnp.exp(log_g[:, None] * (L - 1.0 - idx)[None, :]).T[None, :, :, None]
    S_new = jnp.exp(log_g * L)[None, :, None, None] * S + jnp.einsum('blhd,blhe->bhde', kw, v)
    return inner + cross, S_new


def retention_seq(q, k, v, S0, log_g):
    B, T = q.shape[:2]
    if T <= CHUNK:
        return retention_block(q, k, v, S0, log_g)
    nc = T // CHUNK

    def to_chunks(a):
        return jnp.swapaxes(a.reshape(B, nc, CHUNK, *a.shape[2:]), 0, 1)

    def step(S, qkv):
        o, S = retention_block(qkv[0], qkv[1], qkv[2], S, log_g)
        return S, o

    S, o = lax.scan(step, S0, (to_chunks(q), to_chunks(k), to_chunks(v)))
    return jnp.swapaxes(o, 0, 1).reshape(B, T, *o.shape[3:]), S


def mem_attention(q, mk, mv):
    s = jnp.einsum('bthd,bmhd->bhtm', q, mk).astype(jnp.float32) * (MEM_HD ** -0.5)
    p = jax.nn.softmax(s, axis=-1)
    return jnp.einsum('bhtm,bmhd->bthd', p.astype(mv.dtype), mv)


def memory_kv(mem, norm_mem, w_mem_kv):
    B, M = mem.shape[:2]
    base = rms_norm(mem)
    kv = jnp.einsum('bmd,ld,lde->lbme', base, norm_mem.astype(base.dtype), w_mem_kv)
    nl = kv.shape[0]
    mk = kv[..., :MEM_W].reshape(nl, B, M, MEM_HEADS, MEM_HD)
    mv = kv[..., MEM_W:].reshape(nl, B, M, MEM_HEADS, MEM_HD)
    return mk, mv


def retention_layer(x, pos, S0, mk, mv, norm_g, w_in, w_out):
    B, T, _ = x.shape
    h = rms_norm(x, norm_g)
    q, k, v, g, qm, gm = jnp.split(h @ w_in, SPLIT_A, axis=-1)
    q = rope(q.astype(jnp.float32).reshape(B, T, RET_HEADS, RET_DK), pos)
    k = rope(k.astype(jnp.float32).reshape(B, T, RET_HEADS, RET_DK), pos) * (RET_DK ** -0.5)
    v = v.astype(jnp.float32).reshape(B, T, RET_HEADS, RET_DV)
    o, S = retention_seq(q, k, v, S0.astype(jnp.float32), ret_log_decay())
    o = rms_norm(o).reshape(B, T, RET_W).astype(x.dtype) * jax.nn.silu(g)
    m = mem_attention(qm.reshape(B, T, MEM_HEADS, MEM_HD), mk, mv).reshape(B, T, MEM_W).astype(x.dtype) * jax.nn.silu(gm)
    return x + jnp.concatenate([o, m], axis=-1) @ w_out, S


def diff_attend(q, k, v, lam, mask):
    s = jnp.einsum('bqhcd,bkhcd->bhcqk', q, k).astype(jnp.float32) * (DIFF_DQK ** -0.5)
    if mask is not None:
        s = jnp.where(mask, s, -jnp.inf)
    p = jax.nn.softmax(s, axis=-1)
    a = p[:, :, 0] - lam * p[:, :, 1]
    return jnp.einsum('bhqk,bkhe->bqhe', a.astype(v.dtype), v)


def diff_attend_blocks(q, k, v, lam):
    B, T = q.shape[:2]
    nb = T // Q_BLOCK
    qb = jnp.moveaxis(q.reshape(B, nb, Q_BLOCK, *q.shape[2:]), 1, 0)
    key_chunk = jnp.arange(T) // CHUNK

    def block(args):
        qi, i = args
        q_chunk = (i * Q_BLOCK + jnp.arange(Q_BLOCK)) // CHUNK
        return diff_attend(qi, k, v, lam, key_chunk[None, :] <= q_chunk[:, None])

    o = lax.map(block, (qb, jnp.arange(nb)))
    return jnp.moveaxis(o, 0, 1).reshape(B, T, *o.shape[3:])


def diff_layer(x, k_all, v_all, mk, mv, norm_g, w_in, lam_p, subln_g, w_out, lam_init, blocked):
    B, T, _ = x.shape
    h = rms_norm(x, norm_g)
    q, g, qm, gm = jnp.split(h @ w_in, SPLIT_B, axis=-1)
    q = q.reshape(B, T, DIFF_HEADS, 2, DIFF_DQK)
    lp = lam_p.astype(jnp.float32)
    lam = jnp.exp(jnp.sum(lp[0] * lp[1])) - jnp.exp(jnp.sum(lp[2] * lp[3])) + lam_init
    if blocked:
        o = diff_attend_blocks(q, k_all, v_all, lam)
    else:
        o = diff_attend(q, k_all, v_all, lam, None)
    o = rms_norm(o, subln_g) * (1.0 - lam_init)
    o = o.reshape(B, T, DIFF_VW).astype(x.dtype) * jax.nn.silu(g)
    m = mem_attention(qm.reshape(B, T, MEM_HEADS, MEM_HD), mk, mv).reshape(B, T, MEM_W).astype(x.dtype) * jax.nn.silu(gm)
    return x + jnp.concatenate([o, m], axis=-1) @ w_out


def shared_kv(h, norm_kv, w_kv):
    B, T, _ = h.shape
    kv = rms_norm(h, norm_kv) @ w_kv
    k = kv[..., :DIFF_QW].reshape(B, T, DIFF_HEADS, 2, DIFF_DQK)
    v = kv[..., DIFF_QW:].reshape(B, T, DIFF_HEADS, DIFF_DV)
    return k, v


def trunk(x, pos, S_in, mem_k, mem_v, past_k, past_v, norm_a, w_in_a, w_out_a, norm_kv, w_kv,
          norm_b, w_in_b, diff_lambda, subln_b, w_out_b, norm_f, blocked):
    h = x
    S_out = []
    for l in range(N_A):
        h, S = retention_layer(h, pos, S_in[l], mem_k[l], mem_v[l], norm_a[l], w_in_a[l], w_out_a[l])
        S_out.append(S.astype(x.dtype))
    k_new, v_new = shared_kv(h, norm_kv, w_kv)
    if past_k is None:
        k_all, v_all = k_new, v_new
    else:
        k_all = jnp.concatenate([past_k.astype(k_new.dtype), k_new], axis=1)
        v_all = jnp.concatenate([past_v.astype(v_new.dtype), v_new], axis=1)
    for j in range(N_B):
        l = N_A + j
        lam_init = 0.8 - 0.6 * math.exp(-0.3 * l)
        h = diff_layer(h, k_all, v_all, mem_k[l], mem_v[l], norm_b[j], w_in_b[j], diff_lambda[j],
                       subln_b[j], w_out_b[j], lam_init, blocked)
    return rms_norm(h, norm_f), jnp.stack(S_out), k_new, v_new


def setup_inputs(seed: int = 0) -> dict:
    key = jax.random.key(seed)
    ks = jax.random.split(key, 21)
    f32 = jnp.float32

    def nrm(k, shape, s):
        return jax.random.normal(k, shape, f32) * s

    def gain(k, shape):
        return 1.0 + 0.02 * jax.random.normal(k, shape, f32)

    return {
        "x_prompt": nrm(ks[0], (BATCH, SEQ, D_MODEL), 1.0),
        "x_sample": nrm(ks[1], (DEC_BATCH, DEC_SEQ, D_MODEL), 1.0),
        "state_ret": nrm(ks[2], (N_A, DEC_BATCH, RET_HEADS, RET_DK, RET_DV), 0.5),
        "cache_k": nrm(ks[3], (DEC_BATCH, PAST_LEN, DIFF_HEADS, 2, DIFF_DQK), 1.0),
        "cache_v": nrm(ks[4], (DEC_BATCH, PAST_LEN, DIFF_HEADS, DIFF_DV), 1.0),
        "cache_mem_k": nrm(ks[5], (DEPTH, DEC_BATCH, MEM_LEN, MEM_HEADS, MEM_HD), 1.0),
        "cache_mem_v": nrm(ks[6], (DEPTH, DEC_BATCH, MEM_LEN, MEM_HEADS, MEM_HD), 1.0),
        "mem_prompt": nrm(ks[7], (BATCH, MEM_LEN, D_MODEL), 1.0),
        "norm_a": gain(ks[8], (N_A, D_MODEL)),
        "w_in_a": nrm(ks[9], (N_A, D_MODEL, IN_A), D_MODEL ** -0.5),
        "w_out_a": nrm(ks[10], (N_A, MIX_W, D_MODEL), MIX_W ** -0.5),
        "norm_kv": gain(ks[11], (D_MODEL,)),
        "w_kv": nrm(ks[12], (D_MODEL, DIFF_QW + DIFF_VW), D_MODEL ** -0.5),
        "norm_b": gain(ks[13], (N_B, D_MODEL)),
        "w_in_b": nrm(ks[14], (N_B, D_MODEL, IN_B), D_MODEL ** -0.5),
        "diff_lambda": nrm(ks[15], (N_B, 4, DIFF_DQK), 0.1),
        "subln_b": gain(ks[16], (N_B, DIFF_DV)),
        "w_out_b": nrm(ks[17], (N_B, MIX_W, D_MODEL), MIX_W ** -0.5),
        "norm_mem": gain(ks[18], (DEPTH, D_MODEL)),
        "w_mem_kv": nrm(ks[19], (DEPTH, D_MODEL, 2 * MEM_W), D_MODEL ** -0.5),
        "norm_f": gain(ks[20], (D_MODEL,)),
    }


def reference(x_prompt, x_sample, state_ret, cache_k, cache_v, cache_mem_k, cache_mem_v, mem_prompt,
              norm_a, w_in_a, w_out_a, norm_kv, w_kv, norm_b, w_in_b, diff_lambda, subln_b, w_out_b,
              norm_mem, w_mem_kv, norm_f):
    Bp, Tp = x_prompt.shape[:2]
    mem_k_prompt, mem_v_prompt = memory_kv(mem_prompt, norm_mem, w_mem_kv)
    S0 = jnp.zeros((N_A, Bp, RET_HEADS, RET_DK, RET_DV), jnp.float32)
    y_prompt, state_ret_prompt, k_prompt, v_prompt = trunk(
        x_prompt, jnp.arange(Tp), S0, mem_k_prompt, mem_v_prompt, None, None,
        norm_a, w_in_a, w_out_a, norm_kv, w_kv, norm_b, w_in_b, diff_lambda, subln_b, w_out_b, norm_f, True)
    Ts = x_sample.shape[1]
    y_sample, state_ret_sample, k_sample, v_sample = trunk(
        x_sample, PAST_LEN + jnp.arange(Ts), state_ret, cache_mem_k, cache_mem_v, cache_k, cache_v,
        norm_a, w_in_a, w_out_a, norm_kv, w_kv, norm_b, w_in_b, diff_lambda, subln_b, w_out_b, norm_f, False)
    return (y_prompt, y_sample, state_ret_prompt, k_prompt, v_prompt, mem_k_prompt, mem_v_prompt,
            state_ret_sample, k_sample, v_sample)
```

```python
import math
from contextlib import ExitStack

import numpy as np
import ml_dtypes

import concourse.bass as bass
import concourse.mybir as mybir
from concourse.bass_utils import run_bass_kernel_spmd

F32 = mybir.dt.float32
BF16 = mybir.dt.bfloat16
AF = mybir.ActivationFunctionType
ALU = mybir.AluOpType

D = 1024
SEQ = 2048
NCORES = 8
PAST = 4096
EPS = 1e-6
LAM_INIT = 0.8 - 0.6 * math.exp(-0.3 * 1)
SC_ATT = 128 ** -0.5
NKT_S = 33
INLOOP_CONV = True
MAX_DMA_INFLIGHT = 16
PIPE_NEXT = True
CONV_AT = (0, 4, 0, 0)


class Buf:
    __slots__ = ("name", "writers", "readers", "overlaps")

    def __init__(self, name):
        self.name = name
        self.writers = {}
        self.readers = {}
        self.overlaps = []


def link(a_list, b_list):
    for a in a_list:
        for b in b_list:
            a.overlaps.append(b)
            b.overlaps.append(a)


class Op:
    __slots__ = ("eng", "fn", "deps", "raw", "is_dma", "key", "dcount", "needs_inc", "inc_val", "pos")

    def __init__(self, eng, fn, is_dma, key):
        self.eng = eng
        self.fn = fn
        self.deps = set()
        self.raw = set()
        self.is_dma = is_dma
        self.key = key
        self.dcount = 0
        self.needs_inc = False
        self.inc_val = 0
        self.pos = 0


class Sched:
    ENGS = ("pe", "act", "dve", "pool", "sp")

    def __init__(self):
        self.ops = {e: [] for e in self.ENGS}
        self.last_dma = {}
        self.dma_hist = []
        self.dma_counts = {}
        self.dry = False
        self.n = 0

    def add(self, eng, fn, reads=(), writes=(), key=None, pw=()):
        if self.dry:
            return None
        is_dma = key is not None
        op = Op(eng, fn, is_dma, key)
        self.n += 1
        op.pos = self.n
        rset = []
        for b in reads:
            rset.append(b)
            rset.extend(b.overlaps)
        wset = []
        for b in writes:
            wset.append(b)
            wset.extend(b.overlaps)
        pset = []
        for b in pw:
            pset.append(b)
            wset.extend(b.overlaps)
        for b in rset:
            for w in b.writers.values():
                op.deps.add(w)
                op.raw.add(w)
        for b in wset:
            for w in b.writers.values():
                op.deps.add(w)
            for r in b.readers.values():
                op.deps.add(r)
        for b in pset:
            for r in b.readers.values():
                op.deps.add(r)
        if is_dma:
            self.dma_hist.append(op)
            if len(self.dma_hist) > MAX_DMA_INFLIGHT:
                old_ = self.dma_hist[-1 - MAX_DMA_INFLIGHT]
                op.deps.add(old_)
            prev = self.last_dma.get(key)
            if prev is not None:
                op.deps.add(prev)
                op.raw.add(prev)
            self.last_dma[key] = op
            self.dma_counts[key] = self.dma_counts.get(key, 0) + 1
            op.dcount = self.dma_counts[key]
        rk = ("dma", key) if is_dma else eng
        for b in reads:
            b.readers[rk] = op
        for b in writes:
            b.writers = {rk: op}
            b.readers = {}
        for b in pw:
            b.writers[rk] = op
        op.deps.discard(op)
        self.ops[eng].append(op)
        return op

    def emit(self, nc, es):
        for e in self.ENGS:
            for op in self.ops[e]:
                need = set()
                for d in op.deps:
                    if d.is_dma:
                        need.add(d)
                    elif d.eng == op.eng and not op.is_dma:
                        if op.eng != "pe":
                            need.add(d)
                    else:
                        need.add(d)
                op.deps = need
                for d in need:
                    if not d.is_dma:
                        d.needs_inc = True
        sems = {}
        for e in self.ENGS:
            sems[e] = es.enter_context(nc.semaphore("sem_" + e))
            c = 0
            for op in self.ops[e]:
                if op.needs_inc and not op.is_dma:
                    c += 1
                    op.inc_val = c
        for k in self.dma_counts:
            sems[("dma", k)] = es.enter_context(nc.semaphore("semd_%s" % (str(k),)))
        self.nsem = len(sems)
        handles = {"pe": nc.tensor, "act": nc.scalar, "dve": nc.vector, "pool": nc.gpsimd, "sp": nc.sync}
        block = es.enter_context(nc.Block())
        sect = {"pe": block.tensor, "act": block.scalar, "dve": block.vector, "pool": block.gpsimd,
                "sp": block.sync}

        def make(e):
            def body(_eng):
                h = handles[e]
                known = {}
                for op in self.ops[e]:
                    waits = {}
                    for d in op.deps:
                        if d.is_dma:
                            sk = ("dma", d.key)
                            v = d.dcount * 16
                        else:
                            sk = d.eng
                            v = d.inc_val
                        if known.get(sk, 0) >= v:
                            continue
                        if waits.get(sk, 0) < v:
                            waits[sk] = v
                    for sk, v in waits.items():
                        h.wait_ge(sems[sk], v)
                        known[sk] = v
                    inst = op.fn(h)
                    if op.is_dma:
                        inst.then_inc(sems[("dma", op.key)], 16)
                    elif op.needs_inc:
                        inst.then_inc(sems[e], 1)
                if e == "sp":
                    for k, op in self.last_dma.items():
                        sk = ("dma", k)
                        v = op.dcount * 16
                        if known.get(sk, 0) < v:
                            h.wait_ge(sems[sk], v)
            return body

        for e in self.ENGS:
            sect[e](make(e))


class Ring:
    def __init__(self, items):
        self.items = items
        self.i = 0

    def next(self):
        it = self.items[self.i % len(self.items)]
        self.i += 1
        return it


class WStream:
    def __init__(self, S, slots, depth):
        self.S = S
        self.slots = slots
        self.depth = depth
        self.plan = []
        self.i = 0
        self.issued = 0
        self.recording = True

    def _issue(self, j):
        src, nk, nco = self.plan[j]
        t, b = self.slots[j % len(self.slots)]
        dst = t[:, 0:nk, 0:nco]
        self.S.add("sp", lambda h, dst=dst, src=src: h.dma_start(out=dst, in_=src),
                   reads=[WSC_BUF], writes=[b], key="w%d" % (j % len(self.slots)))

    def get(self, src, nk, nco):
        if self.recording:
            self.plan.append((src, nk, nco))
            return self.slots[0]
        j = self.i
        self.i += 1
        while self.issued < min(len(self.plan), j + self.depth + 1):
            self._issue(self.issued)
            self.issued += 1
        return self.slots[j % len(self.slots)]


WSC_BUF = Buf("wscratch")


def _consts(T):
    half = 128
    inv = (10000.0 ** (-np.arange(half, dtype=np.float32) / np.float32(half))).astype(np.float32)
    g = 1.0 - 2.0 ** (-5.0 - np.arange(4, dtype=np.float64))
    logg = np.log(g)

    def tables(pos, period):
        ang = (pos.astype(np.float32)[None, :] * inv[:, None]).astype(np.float32)
        c = np.cos(ang.astype(np.float64))
        s = np.sin(ang.astype(np.float64))
        l = (np.arange(len(pos)) % period).astype(np.float64)
        rq = np.zeros((128, 8, len(pos)), np.float64)
        for h in range(4):
            dec = np.exp(logg[h] * (l + 1.0))
            rq[:, 2 * h, :] = c * dec[None, :]
            rq[:, 2 * h + 1, :] = s * dec[None, :]
        rk = np.stack([c, s], axis=1)
        return rq.astype(ml_dtypes.bfloat16), rk.astype(ml_dtypes.bfloat16)

    rq_p, rk_p = tables(np.arange(T), 128)
    rq_s, rk_s = tables(np.concatenate([PAST + np.arange(32)] * 2), 32)
    m = np.arange(128)[:, None].astype(np.float64)
    l = np.arange(128)[None, :].astype(np.float64)
    dmask = np.zeros((128, 4, 128), np.float32)
    kw = np.zeros((128, 8), np.float32)
    for h in range(4):
        dmask[:, h, :] = np.where(l >= m, np.exp(-logg[h] * (m + 1.0)) / 16.0, 0.0)
        kw[:, h] = np.exp(logg[h] * (127.0 - np.arange(128))) / 16.0
        kw[:32, 4 + h] = np.exp(logg[h] * (31.0 - np.arange(32))) / 16.0
    gl_p = [float(np.exp(logg[h] * 128.0)) for h in range(4)]
    gl_s = [float(np.exp(logg[h] * 32.0)) for h in range(4)]
    cm = np.zeros((128, 2, 2, 256), np.float32)
    kk = np.arange(128)[:, None]
    qq = np.arange(256)[None, :]
    cm[:, 0, :, :] = ((kk // 64) <= (qq // 64))[:, None, :]
    cm[:, 1, :, :] = ((2 + kk // 64) <= (qq // 64))[:, None, :]
    ident = np.eye(128, dtype=np.float32)
    return dict(rq_p=rq_p, rk_p=rk_p, rq_s=rq_s, rk_s=rk_s, dmask=dmask, kw=kw,
                cmask=cm.astype(ml_dtypes.bfloat16), ident=ident), gl_p, gl_s


def build(NBP=4, T=SEQ, do_sample=True):
    assert T % 512 == 0
    NTILE = T // 512
    consts, GL_P, GL_S = _consts(T)
    nc = bass.Bass("TRN2", target_bir_lowering=False)
    es = ExitStack()

    def din(name, shape, dt=F32):
        return nc.dram_tensor(name, list(shape), dt, kind="ExternalInput").ap()

    def dout(name, shape, dt=F32):
        return nc.dram_tensor(name, list(shape), dt, kind="ExternalOutput").ap()

    def dscr(name, shape, dt=BF16):
        return nc.dram_tensor(name, list(shape), dt, kind="Internal").ap()

    xp = din("xp", [NBP, T, D])
    xs = din("xs", [2, 32, D])
    st_in = din("st_in", [2, 4, 256, 256])
    ck_in = din("ck_in", [2, PAST, D])
    cv_in = din("cv_in", [2, PAST, D])
    cmk_in = din("cmk_in", [2, 2, 256, 512])
    cmv_in = din("cmv_in", [2, 2, 256, 512])
    memp = din("memp", [NBP, 256, D])
    w_in_a = din("w_in_a", [D, 5120])
    w_out_a = din("w_out_a", [1536, D])
    w_kv = din("w_kv", [D, 2048])
    w_in_b = din("w_in_b", [D, 3072])
    w_out_b = din("w_out_b", [1536, D])
    w_mem = din("w_mem", [2, D, D])
    gains_in = din("gains", [6, D])
    subln_in = din("subln", [256])
    lam_in = din("lam_in", [512])
    c_rq_p = din("c_rq_p", [128, 8, T], BF16)
    c_rk_p = din("c_rk_p", [128, 2, T], BF16)
    c_rq_s = din("c_rq_s", [128, 8, 64], BF16)
    c_rk_s = din("c_rk_s", [128, 2, 64], BF16)
    c_dmask = din("c_dmask", [128, 4, 128])
    c_kw = din("c_kw", [128, 8])
    c_cmask = din("c_cmask", [128, 2, 2, 256], BF16)
    c_ident = din("c_ident", [128, 128])
    y_p = dout("y_p", [NBP, T, D])
    y_s = dout("y_s", [2, 32, D])
    st_p = dout("st_p", [NBP, 4, 256, 256])
    k_p = dout("k_p", [NBP, T, D])
    v_p = dout("v_p", [NBP, T, D])
    mk_p = dout("mk_p", [2, NBP, 256, 512])
    mv_p = dout("mv_p", [2, NBP, 256, 512])
    st_s = dout("st_s", [2, 4, 256, 256])
    k_s = dout("k_s", [2, 32, D])
    v_s = dout("v_s", [2, 32, D])
    ws_in_a = dscr("ws_in_a", [128, 8, 5120])
    ws_out_a = dscr("ws_out_a", [128, 12, 1024])
    ws_kv = dscr("ws_kv", [128, 8, 2048])
    ws_in_b = dscr("ws_in_b", [128, 8, 3072])
    ws_out_b = dscr("ws_out_b", [128, 12, 1024])
    ws_mem = dscr("ws_mem", [2, 128, 8, 1024])
    kt_sc = [dscr("kt_sc%d" % i, [8, 128, T]) for i in range(2)]
    v_sc = [dscr("v_sc%d" % i, [T // 128, 128, 4, 258]) for i in range(2)]
    kts_sc = dscr("kts_sc", [2, 8, 128, NKT_S * 128])
    vs_sc = dscr("vs_sc", [2, NKT_S, 128, 4, 258])

    S = Sched()

    def sb(name, shape, dt):
        return es.enter_context(nc.sbuf_tensor(name, list(shape), dt))

    ident_f = sb("ident_f", [128, 128], F32)
    ident_b = sb("ident_b", [128, 128], BF16)
    dmask = sb("dmask", [128, 4, 128], F32)
    kwt = sb("kwt", [128, 8], F32)
    cmask = sb("cmask", [128, 2, 2, 256], BF16)
    gains = sb("gains_t", [128, 6, 8], F32)
    subln = sb("subln_t", [128, 2], F32)
    nf_b = sb("nf_b", [128, D], F32)
    lamw = sb("lamw", [128, 8], F32)
    mhalf = sb("mhalf", [128, 8], F32)
    B_const = Buf("const")

    x_t = [sb("x%d" % i, [128, 4, D], F32) for i in range(2)]
    B_x = [Buf("x%d" % i) for i in range(2)]
    r1 = sb("r1", [128, 12 * 512], BF16)
    xn = r1[:, 0:4096].rearrange("p (s f) -> p s f", s=4)
    omT = r1[:, :].rearrange("p (c t) -> p c t", c=12)
    B_r1 = Buf("r1")
    lamt = r1[:, 0:1024].bitcast(F32)
    r2 = sb("r2", [128, 4 * 1536], BF16)
    hT = r2[:, 0:4096].rearrange("p (c t) -> p c t", c=8)
    om = r2[:, :].rearrange("p (s f) -> p s f", s=4)
    B_r2 = Buf("r2")
    qT = sb("qT", [128, 8, 512], BF16)
    B_qT = Buf("qT")
    sg = sb("sg", [128, 4, D], BF16)
    B_sg = Buf("sg")
    qmT = sb("qmT", [128, 4, 512], BF16)
    B_qmT = Buf("qmT")
    sgm = sb("sgm", [128, 4, 512], BF16)
    B_sgm = Buf("sgm")
    rq_t = sb("rq_t", [128, 8, 512], BF16)
    rk_t = sb("rk_t", [128, 2, 512], BF16)
    B_rope = Buf("rope")
    S_f = sb("S_f", [128, 4, 2, 256], F32)
    B_Sf = [Buf("Sf%d" % h) for h in range(4)]
    _sbt = sb("S_b", [128, 4, 2, 256], BF16)
    S_b = [_sbt, _sbt]
    _bsb = [Buf("Sb_%d" % h) for h in range(4)]
    B_Sb = [_bsb, _bsb]
    memKT = sb("memKT", [128, 2, 2, 4, 256], BF16)
    memV = sb("memV", [128, 2, 2, 2, 4, 130], BF16)
    B_mem = [Buf("mem%d" % i) for i in range(2)]
    wsl = [(sb("wsl%d" % i, [128, 8, 512], BF16), Buf("wsl%d" % i)) for i in range(3)]
    mPT = [(sb("mPT%d" % i, [128, 2, 128], BF16), Buf("mPT%d" % i)) for i in range(4)]
    small = sb("small", [128, 64], F32)
    B_small = {}

    def sm(name, c0, n):
        if name not in B_small:
            B_small[name] = Buf("sm_" + name)
        return small[:, c0:c0 + n], B_small[name]

    AB = sb("AB", [128, 21512], BF16)
    kT = AB[:, 0:4096].rearrange("p (c t) -> p c t", c=8)
    vtok = AB[:, 4096:8192].rearrange("p (s f) -> p s f", s=4)
    kt_tok = [AB[:, 8192 + i * 1024: 8192 + (i + 1) * 1024] for i in range(4)]
    ropetmp = [AB[:, 12288 + i * 1024: 12288 + (i + 1) * 1024].bitcast(F32) for i in range(4)]
    scT = [AB[:, 16384 + i * 512: 16384 + (i + 1) * 512].rearrange("p (h l) -> p h l", h=4) for i in range(4)]
    B_kT, B_vtok = Buf("kT"), Buf("vtok")
    B_kttok = [Buf("kttok%d" % i) for i in range(4)]
    B_ropetmp = [Buf("ropetmp%d" % i) for i in range(4)]
    B_scT = [Buf("scT%d" % i) for i in range(4)]
    A_bufs = [B_kT, B_vtok] + B_kttok + B_ropetmp + B_scT
    stage = [AB[:, i * 1024:(i + 1) * 1024].bitcast(F32) for i in range(4)]
    KTn = AB[:, 4096:8192].rearrange("p (c t) -> p c t", c=8)
    Vn = AB[:, 8192:8192 + 4128].rearrange("p (s h e) -> p s h e", s=4, h=4)
    PT = [AB[:, 12320 + i * 512: 12320 + (i + 1) * 512].rearrange("p (m q) -> p m q", m=2) for i in range(4)]
    KTc = [AB[:, 14368 + i * 1024: 14368 + (i + 1) * 1024].rearrange("p (m k) -> p m k", m=2) for i in range(3)]
    Vc = [AB[:, 17440 + i * 1032: 17440 + (i + 1) * 1032].rearrange("p (t e) -> p t e", t=4) for i in range(3)]
    otmp = [AB[:, 20536 + i * 488: 20536 + i * 488 + 488] for i in range(2)]
    B_stage = [Buf("stage%d" % i) for i in range(4)]
    B_KTn, B_Vn = Buf("KTn"), Buf("Vn")
    B_PT = [Buf("PT%d" % i) for i in range(4)]
    B_KTc = [Buf("KTc%d" % i) for i in range(3)]
    B_Vc = [Buf("Vc%d" % i) for i in range(3)]
    B_otmp = [Buf("otmp0"), Buf("otmp1")]
    B_bufs = B_stage + [B_KTn, B_Vn] + B_PT + B_KTc + B_Vc + B_otmp
    otm = sb("otm", [128, 2, 256], F32)
    B_otm = [Buf("otm0"), Buf("otm1")]

    psum = [es.enter_context(nc.psum_tensor("ps%d" % i, [128, 512], F32)) for i in range(8)]
    B_ps = [Buf("ps%d" % i) for i in range(8)]
    PS = Ring(list(range(8)))

    def ps_f32(i):
        return psum[i]

    def ps_bf(i):
        return psum[i][:, :].bitcast(BF16)

    ws = WStream(S, wsl, depth=2)
    evq = Ring(["dve", "act"])

    B_ktsc = [[Buf("ktsc%d_%d" % (i, t)) for t in range(NTILE)] for i in range(2)]
    B_vsc = [[Buf("vsc%d_%d" % (i, t)) for t in range(NTILE)] for i in range(2)]
    B_ktss = [[Buf("ktss%d_%d" % (b, c)) for c in range(9)] for b in range(2)]
    B_vss = [[Buf("vss%d_%d" % (b, c)) for c in range(9)] for b in range(2)]
    B_out = Buf("outputs")

    def dma(dst, src, reads, writes, key):
        S.add("sp", lambda h: h.dma_start(out=dst, in_=src), reads=reads, writes=writes, key=key)

    def dma_nc(dst, src, reads, writes, key):
        def f(h):
            with nc.allow_non_contiguous_dma(reason="small constant layout"):
                return h.dma_start(out=dst, in_=src)
        S.add("sp", f, reads=reads, writes=writes, key=key)

    def mm(out, lhsT, rhs, start, stop, reads, writes):
        S.add("pe", lambda h: h.matmul(out, lhsT, rhs, start=start, stop=stop), reads=reads, writes=writes)

    def tr(out, in_, ident, reads, writes):
        S.add("pe", lambda h: h.transpose(out, in_, ident), reads=reads, writes=writes)

    def act(out, in_, func, reads, writes, scale=None, bias=None, accum=None):
        kw = {}
        if scale is not None:
            kw["scale"] = scale
        if bias is not None:
            kw["bias"] = bias
        if accum is not None:
            kw["accum_out"] = accum
        S.add("act", lambda h: h.activation(out, in_, func, **kw), reads=reads, writes=writes)

    def copy_ev(eng, out, in_, reads, writes):
        if eng == "act":
            act(out, in_, AF.Copy, reads, writes)
        else:
            S.add(eng, lambda h: h.tensor_copy(out, in_), reads=reads, writes=writes)

    def tt(eng, out, a, b, op, reads, writes):
        S.add(eng, lambda h: h.tensor_tensor(out, a, b, op), reads=reads, writes=writes)

    def ts(eng, out, a, s1, s2, op0, op1, reads, writes):
        if op1 is None:
            S.add(eng, lambda h: h.tensor_scalar(out, a, s1, None, op0), reads=reads, writes=writes)
        else:
            S.add(eng, lambda h: h.tensor_scalar(out, a, s1, s2, op0, op1), reads=reads, writes=writes)

    def stt(out, a, sc, b, op0, op1, reads, writes):
        S.add("dve", lambda h: h.scalar_tensor_tensor(out, a, sc, b, op0, op1), reads=reads, writes=writes)

    def rstd_from_ss(ss_ap, B_ss, out_ap, B_o, n, mult, epsv, Pn):
        tmp, B_tmp = sm("rs_tmp", 56, 4)
        ts("pool", tmp[:Pn, 0:n], ss_ap, mult, epsv, ALU.mult, ALU.add, [B_ss], [B_tmp])
        tt("pool", out_ap, tmp[:Pn, 0:n], mhalf[:Pn, 0:n], ALU.pow, [B_tmp, B_const], [B_o])

    prep_ld = [(AB[:, i * 4096:(i + 1) * 4096].bitcast(F32), Buf("wld%d" % i)) for i in range(3)]
    prep_cv = [(AB[:, 12288 + i * 2056:12288 + (i + 1) * 2056], Buf("wcv%d" % i)) for i in range(3)]

    def setup():
        dma(ident_f[:, :], c_ident[:, :], [], [B_const], "c0")
        dma(dmask[:, :, :], c_dmask[:, :, :], [], [B_const], "c0")
        dma(kwt[:, :], c_kw[:, :], [], [B_const], "c0")
        dma(cmask[:, :, :, :], c_cmask[:, :, :, :], [], [B_const], "c0")
        dma_nc(gains[:, :, :], gains_in.rearrange("g (k p) -> p g k", p=128), [], [B_const], "c0")
        dma_nc(subln[:, :], subln_in.rearrange("(k p) -> p k", p=128), [], [B_const], "c0")
        dma(nf_b[:, :], gains_in[5:6, :].to_broadcast([128, D]), [], [B_const], "c0")
        dma(lamt[:, :], lam_in.rearrange("(o n) -> o n", o=1).to_broadcast([128, 512]), [], [B_r1], "c0")
        S.add("pool", lambda h: h.memset(mhalf[:, :], -0.5), [], [B_const])
        S.add("dve", lambda h: h.tensor_copy(ident_b[:, :], ident_f[:, :]), [B_const], [B_const])
        ts("dve", subln[:, :], subln[:, :], 1.0 - LAM_INIT, None, ALU.mult, None, [B_const], [B_const])
        tt("dve", lamt[:, 0:128], lamt[:, 0:128], lamt[:, 128:256], ALU.mult, [B_r1], [B_r1])
        tt("dve", lamt[:, 256:384], lamt[:, 256:384], lamt[:, 384:512], ALU.mult, [B_r1], [B_r1])
        S.add("dve", lambda h: h.tensor_reduce(lamw[:, 0:1], lamt[:, 0:128], mybir.AxisListType.X, ALU.add),
              [B_r1], [B_const])
        S.add("dve", lambda h: h.tensor_reduce(lamw[:, 1:2], lamt[:, 256:384], mybir.AxisListType.X, ALU.add),
              [B_r1], [B_const])
        act(lamw[:, 0:2], lamw[:, 0:2], AF.Exp, [B_const], [B_const])
        tt("dve", lamw[:, 2:3], lamw[:, 1:2], lamw[:, 0:1], ALU.subtract, [B_const], [B_const])
        ts("dve", lamw[:, 2:3], lamw[:, 2:3], -LAM_INIT, None, ALU.add, None, [B_const], [B_const])

        xl, xc = [], []
        for i in range(2):
            for j in range(2):
                bb = Buf("xpl%d_%d" % (i, j))
                bb.overlaps.append(B_x[i])
                B_x[i].overlaps.append(bb)
                xl.append((x_t[i][:, 2 * j:2 * j + 2, :].rearrange("p s d -> p (s d)"), bb))
        for k, (rt, rb) in enumerate(((r1, B_r1), (r2, B_r2))):
            for j in range(2):
                bb = Buf("rcv%d_%d" % (k, j))
                bb.overlaps.append(rb)
                rb.overlaps.append(bb)
                xc.append((rt[:, j * 2048:(j + 1) * 2048], bb))
        NSL = 3 + 4
        ldr = Ring(prep_ld + xl)
        cvr = Ring(prep_cv + xc)
        engs = Ring(["dve", "act"])
        ldk = Ring(["pl%d" % i for i in range(NSL)])
        stk = Ring(["ps%d" % i for i in range(NSL)])

        def prep(wsrc, R, C, dst, gain_fn):
            for kc in range(R // 128):
                for c0 in range(0, C, 2048):
                    cw = min(2048, C - c0)
                    (lt, lb), (ct, cb) = ldr.next(), cvr.next()
                    dma(lt[:, 0:cw], wsrc[kc * 128:(kc + 1) * 128, c0:c0 + cw], [], [lb], ldk.next())
                    gsc = gain_fn(kc)
                    e = engs.next()
                    if gsc is None:
                        copy_ev(e, ct[:, 0:cw], lt[:, 0:cw], [lb], [cb])
                    elif e == "act":
                        act(ct[:, 0:cw], lt[:, 0:cw], AF.Copy, [lb, B_const], [cb], scale=gsc)
                    else:
                        ts(e, ct[:, 0:cw], lt[:, 0:cw], gsc, None, ALU.mult, None, [lb, B_const], [cb])
                    S.add("sp", lambda h, d_=dst[:, kc, c0:c0 + cw], s_=ct[:, 0:cw]: h.dma_start(out=d_, in_=s_),
                          reads=[cb], pw=[WSC_BUF], key=stk.next())

        prep(w_in_a, D, 5120, ws_in_a, lambda kc: gains[:, 0, kc:kc + 1])
        prep(w_out_a, 1536, D, ws_out_a, lambda kc: None)
        prep(w_kv, D, 2048, ws_kv, lambda kc: gains[:, 1, kc:kc + 1])
        prep(w_in_b, D, 3072, ws_in_b, lambda kc: gains[:, 2, kc:kc + 1])
        prep(w_out_b, 1536, D, ws_out_b, lambda kc: subln[:, (kc % 2):(kc % 2) + 1] if kc < 8 else None)
        for l in range(2):
            prep(w_mem[l], D, D, ws_mem[l], lambda kc, l=l: gains[:, 3 + l, kc:kc + 1])

    def norm_stats(x_ap, B_xb, nsub, Pn):
        ss, B_ss = sm("ss", 0, 4)
        rs, B_rs = sm("rs", 4, 4)
        junk = otm[:, :, :].rearrange("p j e -> p (j e)").bitcast(BF16)
        for s in range(nsub):
            act(junk[:Pn, 0:D], x_ap[:Pn, s, :], AF.Square, [B_xb], B_otm + [B_ss], accum=ss[:Pn, s:s + 1])
        rstd_from_ss(ss[:Pn, 0:nsub], B_ss, rs[:Pn, 0:nsub], B_rs, nsub, 1.0 / D, EPS, Pn)

    def norm_apply(x_ap, B_xb, nsub, Pn):
        rs, B_rs = sm("rs", 4, 4)
        for s in range(nsub):
            ts("dve", xn[:Pn, s, :], x_ap[:Pn, s, :], rs[:Pn, s:s + 1], None, ALU.mult, None,
               [B_xb, B_rs], [B_r1])
        for s in range(nsub):
            b = PS.next()
            pv = ps_bf(b).rearrange("p (c t) -> p c t", c=8)
            for kc in range(8):
                tr(pv[:, kc, 0:Pn], xn[:Pn, s, kc * 128:(kc + 1) * 128], ident_b[:Pn, :Pn],
                   [B_r1, B_const], [B_ps[b]])
            copy_ev(evq.next(), hT[:, :, s * Pn:(s + 1) * Pn], pv[:, :, 0:Pn], [B_ps[b]], [B_r2])

    def norm_to_hT(x_ap, B_xb, nsub, Pn):
        norm_stats(x_ap, B_xb, nsub, Pn)
        norm_apply(x_ap, B_xb, nsub, Pn)

    def proj_fm(wt, wb, nk, col0, NT, srcT, B_src):
        b = PS.next()
        for kc in range(nk):
            mm(psum[b][:, 0:NT], wt[:, kc, col0:col0 + 128], srcT[:, kc, 0:NT], kc == 0, kc == nk - 1,
               [wb, B_src], [B_ps[b]])
        return b

    def proj_tm(wt, wb, nk, s, Pn, srcT, B_src, ncol=512):
        b = PS.next()
        for kc in range(nk):
            mm(psum[b][:Pn, 0:ncol], srcT[:, kc, s * Pn:(s + 1) * Pn], wt[:, kc, 0:ncol], kc == 0, kc == nk - 1,
               [wb, B_src], [B_ps[b]])
        return b

    tanh_tmp = [sb("tanh%d" % i, [128, 512], BF16) for i in range(2)]
    B_tanh = [Buf("tanh0"), Buf("tanh1")]
    tq = Ring([0, 1])

    def gate_evac(b, Pn, out_ap, B_o):
        i = tq.next()
        act(tanh_tmp[i][:Pn, :], psum[b][:Pn, :], AF.Tanh, [B_ps[b]], [B_tanh[i]], scale=0.5)
        stt(out_ap, tanh_tmp[i][:Pn, :], 1.0, psum[b][:Pn, :], ALU.add, ALU.mult, [B_tanh[i], B_ps[b]], [B_o])

    def rope_pair(bA, bB, hd, NT, tab, tcos, tsin, dstT, B_dst):
        A = psum[bA][:, 0:NT]
        Bp = psum[bB][:, 0:NT]
        c = tab[:, tcos, 0:NT]
        s_ = tab[:, tsin, 0:NT]
        t = [ropetmp[i][:, 0:NT] for i in range(4)]
        tt("dve", t[0], A, c, ALU.mult, [B_ps[bA], B_rope], [B_ropetmp[0]])
        tt("dve", t[1], Bp, s_, ALU.mult, [B_ps[bB], B_rope], [B_ropetmp[1]])
        tt("dve", t[2], Bp, c, ALU.mult, [B_ps[bB], B_rope], [B_ropetmp[2]])
        tt("dve", t[3], A, s_, ALU.mult, [B_ps[bA], B_rope], [B_ropetmp[3]])
        tt("pool", dstT[:, 2 * hd, 0:NT], t[0], t[1], ALU.subtract, [B_ropetmp[0], B_ropetmp[1]], [B_dst])
        tt("pool", dstT[:, 2 * hd + 1, 0:NT], t[2], t[3], ALU.add, [B_ropetmp[2], B_ropetmp[3]], [B_dst])

    def in_proj_A(nsub, Pn):
        NT = nsub * Pn
        for g in range(10):
            wt, wb = ws.get(ws_in_a[:, :, g * 512:(g + 1) * 512], 8, 512)
            if g < 4:
                for hh in range(2):
                    hd = (g % 2) * 2 + hh
                    bA = proj_fm(wt, wb, 8, hh * 256, NT, hT, B_r2)
                    bB = proj_fm(wt, wb, 8, hh * 256 + 128, NT, hT, B_r2)
                    if g < 2:
                        rope_pair(bA, bB, hd, NT, rq_t, 2 * hd, 2 * hd + 1, qT, B_qT)
                    else:
                        rope_pair(bA, bB, hd, NT, rk_t, 0, 1, kT, B_kT)
            elif g < 6:
                for s in range(nsub):
                    b = proj_tm(wt, wb, 8, s, Pn, hT, B_r2)
                    copy_ev("act", vtok[:Pn, s, (g - 4) * 512:(g - 3) * 512], psum[b][:Pn, :], [B_ps[b]], [B_vtok])
            elif g < 8:
                for s in range(nsub):
                    b = proj_tm(wt, wb, 8, s, Pn, hT, B_r2)
                    gate_evac(b, Pn, sg[:Pn, s, (g - 6) * 512:(g - 5) * 512], B_sg)
            elif g == 8:
                for hd in range(4):
                    b = proj_fm(wt, wb, 8, hd * 128, NT, hT, B_r2)
                    copy_ev(evq.next(), qmT[:, hd, 0:NT], psum[b][:, 0:NT], [B_ps[b]], [B_qmT])
            else:
                for s in range(nsub):
                    b = proj_tm(wt, wb, 8, s, Pn, hT, B_r2)
                    gate_evac(b, Pn, sgm[:Pn, s, :], B_sgm)

    sbi = [0]

    def retention(nsub, Pn, GL, kwcol, sample):
        for s in range(nsub):
            cs = slice(s * Pn, (s + 1) * Pn)
            bs = PS.next()
            pv = psum[bs][:, :].rearrange("p (h l) -> p h l", h=4)
            for h in range(4):
                for dc in range(2):
                    mm(pv[:Pn, h, 0:Pn], kT[:, 2 * h + dc, cs], qT[:, 2 * h + dc, cs], dc == 0, dc == 1,
                       [B_kT, B_qT], [B_ps[bs]])
            tt("dve", scT[s][:Pn, :, 0:Pn], pv[:Pn, :, 0:Pn], dmask[:Pn, :, 0:Pn], ALU.mult,
               [B_ps[bs], B_const], [B_scT[s]])
            bk = PS.next()
            pk = ps_bf(bk)
            for blk in range(8):
                tr(pk[:Pn, blk * 128:(blk + 1) * 128], kT[:, blk, cs], ident_b[:, :], [B_kT, B_const], [B_ps[bk]])
            for h in range(4):
                act(kt_tok[s][:Pn, h * 256:(h + 1) * 256], pk[:Pn, h * 256:(h + 1) * 256], AF.Copy,
                    [B_ps[bk], B_const], [B_kttok[s]], scale=kwt[:Pn, kwcol + h:kwcol + h + 1])
        for s in range(nsub):
            cs = slice(s * Pn, (s + 1) * Pn)
            if sample:
                dma(S_f[:, :, :, :], st_in[s].rearrange("h (c p) e -> p h c e", p=128), [], B_Sf, "stld")
                for h in range(4):
                    copy_ev("pool", S_b[sbi[0] % 2][:, h, :, :], S_f[:, h, :, :], [B_Sf[h]], [B_Sb[sbi[0] % 2][h]])
            cur, nxt = sbi[0] % 2, (sbi[0] + 1) % 2
            sbi[0] += 1
            bus = []
            for h in range(4):
                bu = PS.next()
                bus.append(bu)
                uv = psum[bu][:, :].rearrange("p (c e) -> p c e", c=2)
                for dc in range(2):
                    mm(uv[:, dc, :], kt_tok[s][:Pn, (2 * h + dc) * 128:(2 * h + dc + 1) * 128],
                       vtok[:Pn, s, h * 256:(h + 1) * 256], True, True, [B_kttok[s], B_vtok], [B_ps[bu]])
            bo = [PS.next(), PS.next()]
            for h in range(4):
                ob = bo[h // 2]
                oview = psum[ob][:Pn, (h % 2) * 256:(h % 2 + 1) * 256]
                mm(oview, scT[s][:Pn, h, 0:Pn], vtok[:Pn, s, h * 256:(h + 1) * 256], True, False,
                   [B_scT[s], B_vtok], [B_ps[ob]])
                for dc in range(2):
                    mm(oview, qT[:, 2 * h + dc, cs], S_b[cur][:, h, dc, :], False, dc == 1,
                       [B_qT, B_Sb[cur][h]], [B_ps[ob]])
            for h in range(4):
                uv = psum[bus[h]][:, :].rearrange("p (c e) -> p c e", c=2)
                stt(S_b[nxt][:, h, :, :], S_f[:, h, :, :], GL[h], uv[:, :, :], ALU.mult, ALU.add,
                    [B_Sf[h], B_ps[bus[h]]], [B_Sb[nxt][h]])
            for h in range(4):
                uv = psum[bus[h]][:, :].rearrange("p (c e) -> p c e", c=2)
                stt(S_f[:, h, :, :], S_f[:, h, :, :], GL[h], uv[:, :, :], ALU.mult, ALU.add,
                    [B_Sf[h], B_ps[bus[h]]], [B_Sf[h]])
            if sample:
                dma(st_s[s].rearrange("h (c p) e -> p h c e", p=128), S_f[:, :, :, :], B_Sf, [], "stst")
            ss4, B_ss4 = sm("ss4", 8, 4)
            rs4, B_rs4 = sm("rs4", 12, 4)
            for h in range(4):
                ob = bo[h // 2]
                oview = psum[ob][:Pn, (h % 2) * 256:(h % 2 + 1) * 256]
                act(otm[:Pn, h % 2, :], oview, AF.Square, [B_ps[ob]], [B_otm[h % 2], B_ss4], accum=ss4[:Pn, h:h + 1])
            rstd_from_ss(ss4[:Pn, 0:4], B_ss4, rs4[:Pn, 0:4], B_rs4, 4, 4.0 / 256, 4.0 * EPS, Pn)
            for h in range(4):
                ob = bo[h // 2]
                oview = psum[ob][:Pn, (h % 2) * 256:(h % 2 + 1) * 256]
                stt(om[:Pn, s, h * 256:(h + 1) * 256], oview, rs4[:Pn, h:h + 1], sg[:Pn, s, h * 256:(h + 1) * 256],
                    ALU.mult, ALU.mult, [B_ps[ob], B_rs4, B_sg], [B_r2])

    def mem_attn(layer, nsub, Pn, slot_of_sub):
        for s in range(nsub):
            sl = slot_of_sub(s)
            for hp in range(2):
                bm = PS.next()
                mview = psum[bm][:, :].rearrange("p (j e) -> p j e", j=2)
                pts = []
                for j in range(2):
                    h = hp * 2 + j
                    bsx = PS.next()
                    sv = psum[bsx][:, :].rearrange("p (m t) -> p m t", m=4)
                    for mc in range(2):
                        mm(sv[:, mc, 0:Pn], memKT[:, sl, layer, h, mc * 128:(mc + 1) * 128],
                           qmT[:, h, s * Pn:(s + 1) * Pn], True, True, [B_mem[sl], B_qmT], [B_ps[bsx]])
                    pt, pb = mPT[(2 * s + hp * 2 + j) % 4]
                    act(pt[:, :, 0:Pn], sv[:, 0:2, 0:Pn], AF.Exp, [B_ps[bsx]], [pb], scale=SC_ATT)
                    pts.append((pt, pb))
                for j in range(2):
                    h = hp * 2 + j
                    pt, pb = pts[j]
                    for mc in range(2):
                        mm(mview[:Pn, j, 0:129], pt[:, mc, 0:Pn], memV[:, sl, layer, mc, h, 0:129],
                           mc == 0, mc == 1, [pb, B_mem[sl]], [B_ps[bm]])
                rr, B_rr = sm("mrr", 16, 2)
                for j in range(2):
                    S.add("dve", lambda hh, j=j, mview=mview: hh.reciprocal(rr[:Pn, j:j + 1], mview[:Pn, j, 128:129]),
                          [B_ps[bm]], [B_rr])
                for j in range(2):
                    h = hp * 2 + j
                    stt(om[:Pn, s, 1024 + h * 128:1024 + (h + 1) * 128], mview[:Pn, j, 0:128], rr[:Pn, j:j + 1],
                        sgm[:Pn, s, h * 128:(h + 1) * 128], ALU.mult, ALU.mult, [B_ps[bm], B_rr, B_sgm], [B_r2])

    def out_proj(wsrc, x_ap, B_xb, nsub, Pn):
        for s in range(nsub):
            b0, b1 = PS.next(), PS.next()
            p0 = ps_bf(b0).rearrange("p (c t) -> p c t", c=8)
            p1 = ps_bf(b1).rearrange("p (c t) -> p c t", c=8)
            for fc in range(12):
                pv, bb = (p0, b0) if fc < 8 else (p1, b1)
                tr(pv[:, fc % 8, 0:Pn], om[:Pn, s, fc * 128:(fc + 1) * 128], ident_b[:Pn, :Pn],
                   [B_r2, B_const], [B_ps[bb]])
            copy_ev(evq.next(), omT[:, 0:8, s * Pn:(s + 1) * Pn], p0[:, :, 0:Pn], [B_ps[b0]], [B_r1])
            copy_ev(evq.next(), omT[:, 8:12, s * Pn:(s + 1) * Pn], p1[:, 0:4, 0:Pn], [B_ps[b1]], [B_r1])
        for cg in range(2):
            banks = [PS.next() for _ in range(nsub)]
            for half in range(2):
                wt, wb = ws.get(wsrc[:, half * 6:(half + 1) * 6, cg * 512:(cg + 1) * 512], 6, 512)
                for s in range(nsub):
                    b = banks[s]
                    for f6 in range(6):
                        fc = half * 6 + f6
                        mm(psum[b][:Pn, 0:512], omT[:, fc, s * Pn:(s + 1) * Pn], wt[:, f6, 0:512], fc == 0, fc == 11,
                           [wb, B_r1], [B_ps[b]])
            for s in range(nsub):
                b = banks[s]
                tt("dve", x_ap[:Pn, s, cg * 512:(cg + 1) * 512], psum[b][:Pn, :], x_ap[:Pn, s, cg * 512:(cg + 1) * 512],
                   ALU.add, [B_ps[b], B_xb], [B_xb])

    stq = Ring([0, 1, 2, 3])

    def kv_proj(nsub, Pn, k_dst, v_dst, kt_dst_fn, v_dst_fn, B_ktd, B_vd):
        S.add("pool", lambda h: h.memset(Vn[:, :, :, 256:258], 1.0), [], [B_Vn])
        for g in range(4):
            wt, wb = ws.get(ws_kv[:, :, g * 512:(g + 1) * 512], 8, 512)
            for s in range(nsub):
                b = proj_tm(wt, wb, 8, s, Pn, hT, B_r2)
                i = stq.next()
                copy_ev("act", stage[i][:Pn, :], psum[b][:Pn, :], [B_ps[b]], [B_stage[i]])
                if g < 2:
                    dma(k_dst(s)[:, g * 512:(g + 1) * 512], stage[i][:Pn, :], [B_stage[i]], [], "st%d" % i)
                    bt = PS.next()
                    tv = psum[bt][:, :].rearrange("p (c t) -> p c t", c=4)
                    for blk in range(4):
                        tr(tv[:, blk, 0:Pn], stage[i][:Pn, blk * 128:(blk + 1) * 128], ident_f[:Pn, :Pn],
                           [B_stage[i], B_const], [B_ps[bt]])
                    copy_ev("dve", KTn[:, g * 4:(g + 1) * 4, s * Pn:(s + 1) * Pn], tv[:, :, 0:Pn], [B_ps[bt]], [B_KTn])
                else:
                    dma(v_dst(s)[:, (g - 2) * 512:(g - 1) * 512], stage[i][:Pn, :], [B_stage[i]], [], "st%d" % i)
                    copy_ev("pool", Vn[:Pn, s, (g - 2) * 2:(g - 1) * 2, 0:256],
                            stage[i][:Pn, :].rearrange("p (h e) -> p h e", h=2), [B_stage[i]], [B_Vn])
        kt_dst_fn()
        v_dst_fn()

    def in_proj_B(nsub, Pn):
        NT = nsub * Pn
        for g in range(6):
            wt, wb = ws.get(ws_in_b[:, :, g * 512:(g + 1) * 512], 8, 512)
            if g < 2:
                for blk in range(4):
                    b = proj_fm(wt, wb, 8, blk * 128, NT, hT, B_r2)
                    copy_ev(evq.next(), qT[:, g * 4 + blk, 0:NT], psum[b][:, 0:NT], [B_ps[b]], [B_qT])
            elif g < 4:
                for s in range(nsub):
                    b = proj_tm(wt, wb, 8, s, Pn, hT, B_r2)
                    gate_evac(b, Pn, sg[:Pn, s, (g - 2) * 512:(g - 1) * 512], B_sg)
            elif g == 4:
                for hd in range(4):
                    b = proj_fm(wt, wb, 8, hd * 128, NT, hT, B_r2)
                    copy_ev(evq.next(), qmT[:, hd, 0:NT], psum[b][:, 0:NT], [B_ps[b]], [B_qmT])
            else:
                for s in range(nsub):
                    b = proj_tm(wt, wb, 8, s, Pn, hT, B_r2)
                    gate_evac(b, Pn, sgm[:Pn, s, :], B_sgm)

    kvq = Ring([0, 1, 2])
    ptq = Ring([0, 1, 2, 3])

    def attn_tile(blocks):
        LA = 2
        sring = [0, 1, 2, 3]
        steps = []
        cl = []
        for bi, blk in enumerate(blocks):
            nst = sum(c["nkt"] for c in blk["chunks"])
            k = 0
            for c in blk["chunks"]:
                gi = len(cl)
                cl.append(c)
                koff = 0
                for t in range(c["nkt"]):
                    steps.append(dict(b=bi, g=gi, t=t, kp=c["kps"][t], koff=koff, mask=c["masks"][t],
                                      first=(k == 0), last=(k == nst - 1)))
                    koff += c["kps"][t]
                    k += 1
                c["last_step"] = len(steps) - 1
        total = len(steps)
        slot_of = {}
        slot_last = {0: -1, 1: -1, 2: -1}
        state = dict(pv_done=0, nstream=0)

        def load_chunk(gi, force):
            if gi >= len(cl) or gi in slot_of:
                return
            c = cl[gi]
            if "direct" in c:
                slot_of[gi] = None
                return
            ci = state["nstream"] % 3
            if slot_last[ci] >= state["pv_done"] and not force:
                return
            assert slot_last[ci] < state["pv_done"]
            state["nstream"] += 1
            slot_of[gi] = ci
            slot_last[ci] = c["last_step"]
            nk = sum(c["kps"])
            dma(KTc[ci][:, :, 0:nk], c["kt_src"], c["deps"], [B_KTc[ci]], "ktc%d" % ci)
            dma(Vc[ci][:, 0:c["nkt"], :], c["v_src"], c["deps"], [B_Vc[ci]], "vc%d" % ci)

        def emit_scores(st):
            load_chunk(st["g"], True)
            blk = blocks[st["b"]]
            c = cl[st["g"]]
            ci = slot_of[st["g"]]
            kp = st["kp"]
            NQ = blk["nqs"] * blk["Pn"]
            h = blk["h"]
            bs = sring[attn_tile.sr % len(sring)]
            attn_tile.sr += 1
            sv = psum[bs][:, :].rearrange("p (m q) -> p m q", m=2)
            for m in range(2):
                if ci is None:
                    lk = c["direct"][0](m, st["koff"], kp)
                    rb = c["direct"][2]
                else:
                    lk = KTc[ci][:, m, st["koff"]:st["koff"] + kp]
                    rb = [B_KTc[ci]]
                mm(sv[:kp, m, 0:NQ], lk, qT[:, 2 * h + m, blk["qc0"]:blk["qc0"] + NQ],
                   True, True, rb + [B_qT], [B_ps[bs]])
            pi = ptq.next()
            act(PT[pi][:kp, :, 0:NQ], sv[:kp, :, 0:NQ], AF.Exp, [B_ps[bs]], [B_PT[pi]], scale=SC_ATT)
            if st["mask"] is not None:
                tt("pool", PT[pi][:kp, :, 0:NQ], PT[pi][:kp, :, 0:NQ], cmask[:kp, st["mask"], :, 0:NQ], ALU.mult,
                   [B_PT[pi], B_const], [B_PT[pi]])
            st["pi"] = pi

        for i in range(min(LA, total)):
            emit_scores(steps[i])
        for i in range(total):
            if i + LA < total:
                emit_scores(steps[i + LA])
            nxt = steps[min(i + LA, total - 1)]["g"]
            load_chunk(nxt + 1, False)
            load_chunk(nxt + 2, False)
            st = steps[i]
            blk = blocks[st["b"]]
            c = cl[st["g"]]
            ci = slot_of[st["g"]]
            kp, pi, Pn, nqs = st["kp"], st["pi"], blk["Pn"], blk["nqs"]
            obank = [[4 + 2 * j, 5 + 2 * j] for j in range(nqs)]
            for j in range(nqs):
                for m in range(2):
                    ob = obank[j][m]
                    if ci is None:
                        rv = c["direct"][1](st["t"], kp)
                        rb = c["direct"][2]
                    else:
                        rv = Vc[ci][:kp, st["t"], 0:257]
                        rb = [B_Vc[ci]]
                    mm(psum[ob][:Pn, 0:257], PT[pi][:kp, m, j * Pn:(j + 1) * Pn], rv,
                       st["first"], st["last"], [B_PT[pi]] + rb, [B_ps[ob]])
            state["pv_done"] = i + 1
            if st["last"]:
                attn_epilogue(blk, obank)

    def attn_epilogue(blk, obank):
        Pn, nqs, h, sub0 = blk["Pn"], blk["nqs"], blk["h"], blk["sub0"]
        rrs = []
        for j in range(nqs):
            o1, o2 = obank[j]
            rr, B_rr = sm("arr%d" % j, 20 + 4 * j, 4)
            rrs.append((rr, B_rr))
            S.add("dve", lambda hh, rr=rr, o1=o1: hh.reciprocal(rr[:Pn, 0:1], psum[o1][:Pn, 256:257]), [B_ps[o1]], [B_rr])
            S.add("dve", lambda hh, rr=rr, o2=o2: hh.reciprocal(rr[:Pn, 1:2], psum[o2][:Pn, 256:257]), [B_ps[o2]], [B_rr])
            tt("dve", rr[:Pn, 2:3], rr[:Pn, 1:2], lamw[:Pn, 2:3], ALU.mult, [B_rr, B_const], [B_rr])
        for j in range(nqs):
            o1, o2 = obank[j]
            rr, B_rr = rrs[j]
            act(otm[:Pn, j, :], psum[o1][:Pn, 0:256], AF.Copy, [B_ps[o1], B_rr], [B_otm[j]], scale=rr[:Pn, 0:1])
        for j in range(nqs):
            o1, o2 = obank[j]
            rr, B_rr = rrs[j]
            stt(otm[:Pn, j, :], psum[o2][:Pn, 0:256], rr[:Pn, 2:3], otm[:Pn, j, :], ALU.mult, ALU.add,
                [B_ps[o2], B_rr, B_otm[j]], [B_otm[j]])
        for j in range(nqs):
            s = sub0 + j
            ssq, B_ssq = sm("assq%d" % j, 28 + 2 * j, 1)
            rsq, B_rsq = sm("arsq%d" % j, 32 + 2 * j, 1)
            junk = otmp[j][:Pn, 0:256]
            act(junk, otm[:Pn, j, :], AF.Square, [B_otm[j]], [B_otmp[j], B_ssq], accum=ssq[:Pn, 0:1])
            rstd_from_ss(ssq[:Pn, 0:1], B_ssq, rsq[:Pn, 0:1], B_rsq, 1, 4.0 / 256, 4.0 * EPS, Pn)
            stt(om[:Pn, s, h * 256:(h + 1) * 256], otm[:Pn, j, :], rsq[:Pn, 0:1], sg[:Pn, s, h * 256:(h + 1) * 256],
                ALU.mult, ALU.mult, [B_otm[j], B_rsq, B_sg], [B_r2])

    attn_tile.sr = 0

    mst = [AB[:, 12304 + i * 2048:12304 + (i + 1) * 2048].bitcast(F32) for i in range(2)]
    B_mst = [Buf("mst0"), Buf("mst1")]
    msq = Ring([0, 1])

    def mem_kv_prompt(bi):
        S.add("pool", lambda h: h.memset(memV[:, 0, :, :, :, 128:130], 2.0), [], [B_mem[0]])
        for mc in range(2):
            dma(x_t[0][:, mc, :], memp[bi, mc * 128:(mc + 1) * 128, :], [], [B_x[0]], "xld")
        norm_to_hT(x_t[0], B_x[0], 2, 128)
        for l in range(2):
            for g in range(2):
                wt, wb = ws.get(ws_mem[l][:, :, g * 512:(g + 1) * 512], 8, 512)
                for mc in range(2):
                    b = proj_tm(wt, wb, 8, mc, 128, hT, B_r2)
                    i = msq.next()
                    copy_ev("act", mst[i][:, 0:512], psum[b][:, :], [B_ps[b]], [B_mst[i]])
                    dst = (mk_p if g == 0 else mv_p)[l, bi, mc * 128:(mc + 1) * 128, :]
                    dma(dst, mst[i][:, 0:512], [B_mst[i]], [], "mst%d" % i)
                    if g == 1:
                        copy_ev("pool", memV[:, 0, l, mc, :, 0:128],
                                mst[i][:, 0:512].rearrange("p (h e) -> p h e", h=4), [B_mst[i]], [B_mem[0]])
                if g == 0:
                    for hd in range(4):
                        b = proj_fm(wt, wb, 8, hd * 128, 256, hT, B_r2)
                        copy_ev(evq.next(), memKT[:, 0, l, hd, :], psum[b][:, 0:256], [B_ps[b]], [B_mem[0]])

    def mem_kv_sample():
        for sl in range(2):
            S.add("pool", lambda h, sl=sl: h.memset(memV[:, sl, :, :, :, 128:130], 2.0), [], [B_mem[sl]])
            for l in range(2):
                for mc in range(2):
                    i = msq.next()
                    dma(mst[i][:, 0:512], cmk_in[l, sl, mc * 128:(mc + 1) * 128, :], [], [B_mst[i]], "mst%d" % i)
                    bt = PS.next()
                    tv = psum[bt][:, :].rearrange("p (c t) -> p c t", c=4)
                    for hd in range(4):
                        tr(tv[:, hd, :], mst[i][:, hd * 128:(hd + 1) * 128], ident_f[:, :], [B_mst[i], B_const], [B_ps[bt]])
                    copy_ev("dve", memKT[:, sl, l, :, mc * 128:(mc + 1) * 128], tv[:, :, :], [B_ps[bt]], [B_mem[sl]])
                    dma(mst[i][:, 512:1024], cmv_in[l, sl, mc * 128:(mc + 1) * 128, :], [], [B_mst[i]], "mst%d" % i)
                    copy_ev("pool", memV[:, sl, l, mc, :, 0:128],
                            mst[i][:, 512:1024].rearrange("p (h e) -> p h e", h=4), [B_mst[i]], [B_mem[sl]])

    cvK_f = sb("cvK_f", [128, 1024], F32)
    cvV_f = cvK_f
    cvK_b = sb("cvK_b", [128, 8, 128], BF16)
    cvV_b = sb("cvV_b", [128, 4, 258], BF16)
    B_cvKf, B_cvKb, B_cvVb = Buf("cvKf"), Buf("cvKb"), Buf("cvVb")
    B_cvVf = B_cvKf
    conv_todo = []

    def conv_init():
        del conv_todo[:]
        for b in range(2):
            for kt in range(PAST // 128):
                conv_todo.append((b, kt))
        S.add("pool", lambda h: h.memset(cvV_b[:, :, 256:258], 1.0), [], [B_cvVb])

    sgm_f = sgm[:, :, :].rearrange("p s t -> p (s t)").bitcast(F32)
    qmT_f = qmT[:, :, :].rearrange("p s t -> p (s t)").bitcast(F32)
    CVST = {"c": (cvK_f, B_cvKf), "s": (sgm_f, B_sgm), "q": (qmT_f, B_qmT)}

    def cv_next():
        return conv_todo.pop(0) if conv_todo else None

    def cv_loadK(item, st):
        if item is None:
            return
        b, kt = item
        t_, b_ = CVST[st]
        dma(t_[:, :], ck_in[b, kt * 128:(kt + 1) * 128, :], [], [b_], "cvl")

    def cv_loadV(item, st):
        if item is None:
            return
        b, kt = item
        t_, b_ = CVST[st]
        dma(t_[:, :], cv_in[b, kt * 128:(kt + 1) * 128, :], [], [b_], "cvl")

    def cv_transK(item, st):
        if item is None:
            return
        b, kt = item
        t_, b_ = CVST[st]
        for half in range(2):
            bt = PS.next()
            tv = psum[bt][:, :].rearrange("p (c t) -> p c t", c=4)
            for blk in range(4):
                tr(tv[:, blk, :], t_[:, (half * 4 + blk) * 128:(half * 4 + blk + 1) * 128], ident_f[:, :],
                   [b_, B_const], [B_ps[bt]])
            copy_ev(evq.next(), cvK_b[:, half * 4:(half + 1) * 4, :], tv[:, :, :], [B_ps[bt]], [B_cvKb])
        S.add("sp", lambda h, d_=kts_sc[b, :, :, kt * 128:(kt + 1) * 128].rearrange("m p k -> p m k"):
              h.dma_start(out=d_, in_=cvK_b[:, :, :]), reads=[B_cvKb], pw=[B_ktss[b][kt // 4]], key="cvs")

    def cv_castV(item, st):
        if item is None:
            return
        b, kt = item
        t_, b_ = CVST[st]
        copy_ev("pool", cvV_b[:, :, 0:256], t_[:, :].rearrange("p (h e) -> p h e", h=4), [b_], [B_cvVb])
        S.add("sp", lambda h, d_=vs_sc[b, kt, :, :, :]: h.dma_start(out=d_, in_=cvV_b[:, :, :]),
              reads=[B_cvVb], pw=[B_vss[b][kt // 4]], key="cvs")

    def conv_some(n):
        for _ in range(n):
            if not conv_todo:
                return
            b, kt = conv_todo.pop(0)
            dma(cvK_f[:, :], ck_in[b, kt * 128:(kt + 1) * 128, :], [], [B_cvKf], "cvl")
            for half in range(2):
                bt = PS.next()
                tv = psum[bt][:, :].rearrange("p (c t) -> p c t", c=4)
                for blk in range(4):
                    tr(tv[:, blk, :], cvK_f[:, (half * 4 + blk) * 128:(half * 4 + blk + 1) * 128], ident_f[:, :],
                       [B_cvKf, B_const], [B_ps[bt]])
                copy_ev(evq.next(), cvK_b[:, half * 4:(half + 1) * 4, :], tv[:, :, :], [B_ps[bt]], [B_cvKb])
            dma(cvV_f[:, :], cv_in[b, kt * 128:(kt + 1) * 128, :], [], [B_cvVf], "cvl")
            copy_ev("pool", cvV_b[:, :, 0:256], cvV_f[:, :].rearrange("p (h e) -> p h e", h=4), [B_cvVf], [B_cvVb])
            S.add("sp", lambda h, d_=kts_sc[b, :, :, kt * 128:(kt + 1) * 128].rearrange("m p k -> p m k"):
                  h.dma_start(out=d_, in_=cvK_b[:, :, :]), reads=[B_cvKb], pw=[B_ktss[b][kt // 4]], key="cvs")
            S.add("sp", lambda h, d_=vs_sc[b, kt, :, :, :]: h.dma_start(out=d_, in_=cvV_b[:, :, :]),
                  reads=[B_cvVb], pw=[B_vss[b][kt // 4]], key="cvs")

    def final_store(x_ap, B_xb, nsub, Pn, dst):
        ss, B_ss = sm("fss", 40, 4)
        rs, B_rs = sm("frs", 44, 4)
        junk = otm[:, :, :].rearrange("p j e -> p (j e)").bitcast(BF16)
        for s in range(nsub):
            act(junk[:Pn, 0:D], x_ap[:Pn, s, :], AF.Square, [B_xb], B_otm + [B_ss], accum=ss[:Pn, s:s + 1])
        rstd_from_ss(ss[:Pn, 0:nsub], B_ss, rs[:Pn, 0:nsub], B_rs, nsub, 1.0 / D, EPS, Pn)
        for s in range(nsub):
            stt(x_ap[:Pn, s, :], x_ap[:Pn, s, :], rs[:Pn, s:s + 1], nf_b[:Pn, :], ALU.mult, ALU.mult,
                [B_xb, B_rs, B_const], [B_xb])
        dma(dst, x_ap[:Pn, 0:nsub, :], [B_xb], [], "yst")

    def program():
        PS.i = 0
        sbi[0] = 0
        attn_tile.sr = 0
        setup()
        conv_init()
        for bi in range(NBP):
            par = bi % 2
            mem_kv_prompt(bi)
            for h in range(4):
                S.add("pool", lambda hh, h=h: hh.memset(S_f[:, h, :, :], 0.0), [], [B_Sf[h]])
                S.add("pool", lambda hh, h=h, i=sbi[0] % 2: hh.memset(S_b[i][:, h, :, :], 0.0), [], [B_Sb[sbi[0] % 2][h]])
            xcur = 0
            for t in range(NTILE):
                tok0 = t * 512
                xt, bx = x_t[xcur], B_x[xcur]
                if t == 0 or not PIPE_NEXT:
                    dma(xt[:, :, :], xp[bi, tok0:tok0 + 512, :].rearrange("(s p) d -> p s d", p=128), [], [bx], "xld")
                    dma(rq_t[:, :, :], c_rq_p[:, :, tok0:tok0 + 512], [], [B_rope], "rope")
                    dma(rk_t[:, :, :], c_rk_p[:, :, tok0:tok0 + 512], [], [B_rope], "rope")
                    norm_to_hT(xt, bx, 4, 128)
                in_proj_A(4, 128)
                retention(4, 128, GL_P, 0, False)
                cvi = [None] * 4
                if do_sample and INLOOP_CONV:
                    cvi = [cv_next() for _ in range(4)]
                    cv_loadK(cvi[0], "c")
                mem_attn(0, 4, 128, lambda s: 0)
                cv_transK(cvi[0], "c")
                cv_loadV(cvi[0], "s")
                cv_loadK(cvi[1], "q")
                out_proj(ws_out_a, xt, bx, 4, 128)
                cv_castV(cvi[0], "s")
                cv_transK(cvi[1], "q")
                cv_loadV(cvi[1], "c")
                if t == NTILE - 1:
                    dma(st_p[bi].rearrange("h (c p) e -> p h c e", p=128), S_f[:, :, :, :], B_Sf, [], "stst")
                has_next = (t + 1 < NTILE) and PIPE_NEXT
                if has_next:
                    xnx, bxn = x_t[1 - xcur], B_x[1 - xcur]
                    dma(xnx[:, :, :], xp[bi, tok0 + 512:tok0 + 1024, :].rearrange("(s p) d -> p s d", p=128),
                        [], [bxn], "xld")
                    dma(rq_t[:, :, :], c_rq_p[:, :, tok0 + 512:tok0 + 1024], [], [B_rope], "rope")
                    dma(rk_t[:, :, :], c_rk_p[:, :, tok0 + 512:tok0 + 1024], [], [B_rope], "rope")
                norm_to_hT(xt, bx, 4, 128)
                cv_castV(cvi[1], "c")

                def kt_store(par=par, tok0=tok0, t=t):
                    dma(kt_sc[par][:, :, tok0:tok0 + 512].rearrange("m p k -> p m k"), KTn[:, :, :],
                        [B_KTn], [B_ktsc[par][t]], "ktn")

                def v_store(par=par, tok0=tok0, t=t):
                    dma(v_sc[par][tok0 // 128:tok0 // 128 + 4, :, :, :].rearrange("s p h e -> p s h e"),
                        Vn[:, :, :, :], [B_Vn], [B_vsc[par][t]], "vn")

                kv_proj(4, 128,
                        lambda s, bi=bi, tok0=tok0: k_p[bi, tok0 + s * 128:tok0 + (s + 1) * 128, :],
                        lambda s, bi=bi, tok0=tok0: v_p[bi, tok0 + s * 128:tok0 + (s + 1) * 128, :],
                        kt_store, v_store, None, None)
                in_proj_B(4, 128)
                mem_attn(1, 4, 128, lambda s: 0)
                cv_loadK(cvi[2], "c")
                cv_loadV(cvi[2], "s")
                cv_loadK(cvi[3], "q")
                blocks = []
                for qt in range(2):
                    q0 = tok0 + qt * 256
                    nkt_total = (q0 + 256) // 128
                    for h in range(4):
                        chunks = []
                        for cc in range((nkt_total + 3) // 4):
                            nkt = min(4, nkt_total - cc * 4)
                            masks = []
                            for kk in range(nkt):
                                kt_abs = cc * 4 + kk
                                masks.append(None if kt_abs * 128 < q0 else (kt_abs * 128 - q0) // 128)
                            ch = dict(nkt=nkt, kps=[128] * nkt, masks=masks)
                            if cc == t:
                                ch["direct"] = (
                                    lambda m, koff, kp, h=h: KTn[:, 2 * h + m, koff:koff + kp],
                                    lambda tt_, kp, h=h: Vn[:kp, tt_, h, 0:257],
                                    [B_KTn, B_Vn])
                            else:
                                ch["kt_src"] = kt_sc[par][2 * h:2 * h + 2, :, cc * 512:cc * 512 + nkt * 128].rearrange("m p k -> p m k")
                                ch["v_src"] = v_sc[par][cc * 4:cc * 4 + nkt, :, h, :].rearrange("t p e -> p t e")
                                ch["deps"] = [B_ktsc[par][cc], B_vsc[par][cc]]
                            chunks.append(ch)
                        blocks.append(dict(qc0=qt * 256, nqs=2, Pn=128, h=h, sub0=qt * 2, chunks=chunks))
                attn_tile(blocks)
                cv_transK(cvi[2], "c")
                cv_transK(cvi[3], "q")
                cv_castV(cvi[2], "s")
                if has_next:
                    norm_stats(xnx, bxn, 4, 128)
                out_proj(ws_out_b, xt, bx, 4, 128)
                cv_loadV(cvi[3], "c")
                if has_next:
                    norm_apply(xnx, bxn, 4, 128)
                final_store(xt, bx, 4, 128,
                            y_p[bi, tok0:tok0 + 512, :].rearrange("(s p) d -> p s d", p=128))
                cv_castV(cvi[3], "c")
                xcur = 1 - xcur
        if do_sample:
            sample_program()

    def sample_program():
        xt, bx = x_t[0], B_x[0]
        mem_kv_sample()
        conv_some(1000)
        dma(xt[:32, 0:2, :], xs.rearrange("s p d -> p s d"), [], [bx], "xld")
        dma(rq_t[:, :, 0:64], c_rq_s[:, :, :], [], [B_rope], "rope")
        dma(rk_t[:, :, 0:64], c_rk_s[:, :, :], [], [B_rope], "rope")
        norm_to_hT(xt, bx, 2, 32)
        in_proj_A(2, 32)
        retention(2, 32, GL_S, 4, True)
        mem_attn(0, 2, 32, lambda s: s)
        out_proj(ws_out_a, xt, bx, 2, 32)
        norm_to_hT(xt, bx, 2, 32)

        def kt_store():
            for b in range(2):
                dma(kts_sc[b, :, :, PAST:PAST + 32].rearrange("m p k -> p m k"), KTn[:, :, b * 32:(b + 1) * 32],
                    [B_KTn], [B_ktss[b][8]], "ktn")

        def v_store():
            for b in range(2):
                dma(vs_sc[b, 32, 0:32, :, :], Vn[:32, b, :, :], [B_Vn], [B_vss[b][8]], "vn")

        kv_proj(2, 32, lambda s: k_s[s, :, :], lambda s: v_s[s, :, :], lambda: None, lambda: None, None, None)
        in_proj_B(2, 32)
        mem_attn(1, 2, 32, lambda s: s)
        blocks = []
        for b in range(2):
            for h in range(4):
                chunks = []
                for cc in range(8):
                    chunks.append(dict(
                        kt_src=kts_sc[b, 2 * h:2 * h + 2, :, cc * 512:cc * 512 + 512].rearrange("m p k -> p m k"),
                        v_src=vs_sc[b, cc * 4:cc * 4 + 4, :, h, :].rearrange("t p e -> p t e"),
                        nkt=4, kps=[128] * 4, masks=[None] * 4, deps=[B_ktss[b][cc], B_vss[b][cc]]))
                chunks.append(dict(nkt=1, kps=[32], masks=[None], direct=(
                    lambda m, koff, kp, h=h, b=b: KTn[:, 2 * h + m, b * 32:b * 32 + 32],
                    lambda tt_, kp, h=h, b=b: Vn[:32, b, h, 0:257],
                    [B_KTn, B_Vn])))
                blocks.append(dict(qc0=b * 32, nqs=1, Pn=32, h=h, sub0=b, chunks=chunks))
        attn_tile(blocks)
        out_proj(ws_out_b, xt, bx, 2, 32)
        final_store(xt, bx, 2, 32, y_s.rearrange("s p d -> p s d"))

    ab_ranges = [(B_kT, 0, 4096), (B_vtok, 4096, 8192), (B_KTn, 4096, 8192), (B_Vn, 8192, 8192 + 4128)]
    ab_ranges += [(B_kttok[i], 8192 + i * 1024, 8192 + (i + 1) * 1024) for i in range(4)]
    ab_ranges += [(B_ropetmp[i], 12288 + i * 1024, 12288 + (i + 1) * 1024) for i in range(4)]
    ab_ranges += [(B_scT[i], 16384 + i * 512, 16384 + (i + 1) * 512) for i in range(4)]
    ab_ranges += [(B_stage[i], i * 1024, (i + 1) * 1024) for i in range(4)]
    ab_ranges += [(B_PT[i], 12320 + i * 512, 12320 + (i + 1) * 512) for i in range(4)]
    ab_ranges += [(B_KTc[i], 14368 + i * 1024, 14368 + (i + 1) * 1024) for i in range(3)]
    ab_ranges += [(B_Vc[i], 17440 + i * 1032, 17440 + (i + 1) * 1032) for i in range(3)]
    ab_ranges += [(B_otmp[i], 20536 + i * 488, 20536 + (i + 1) * 488) for i in range(2)]
    ab_ranges += [(prep_ld[i][1], i * 4096, (i + 1) * 4096) for i in range(3)]
    ab_ranges += [(prep_cv[i][1], 12288 + i * 2056, 12288 + (i + 1) * 2056) for i in range(3)]
    ab_ranges += [(B_mst[i], 12304 + i * 2048, 12304 + (i + 1) * 2048) for i in range(2)]
    for i in range(len(ab_ranges)):
        for j in range(i + 1, len(ab_ranges)):
            bi_, lo_i, hi_i = ab_ranges[i]
            bj_, lo_j, hi_j = ab_ranges[j]
            if lo_i < hi_j and lo_j < hi_i:
                bi_.overlaps.append(bj_)
                bj_.overlaps.append(bi_)

    S.dry = True
    ws.recording = True
    program()
    S.dry = False
    ws.recording = False
    program()
    S.emit(nc, es)
    es.close()
    return nc, consts


_CACHE = {}


def _get_prog(NBP, T, do_sample=True):
    key = (NBP, T, do_sample)
    if key not in _CACHE:
        _CACHE[key] = build(NBP, T, do_sample)
    return _CACHE[key]


def make_in_maps(inp, consts, NBP, T, pb_of_core):
    f = np.ascontiguousarray
    gains = np.stack([inp["norm_a"][0], inp["norm_kv"], inp["norm_b"][0], inp["norm_mem"][0],
                      inp["norm_mem"][1], inp["norm_f"]]).astype(np.float32)
    shared = dict(
        w_in_a=f(inp["w_in_a"][0]), w_out_a=f(inp["w_out_a"][0]), w_kv=f(inp["w_kv"]),
        w_in_b=f(inp["w_in_b"][0]), w_out_b=f(inp["w_out_b"][0]), w_mem=f(inp["w_mem_kv"]),
        gains=f(gains), subln=f(inp["subln_b"][0]), lam_in=f(inp["diff_lambda"][0].reshape(512)),
        c_rq_p=consts["rq_p"], c_rk_p=consts["rk_p"], c_rq_s=consts["rq_s"], c_rk_s=consts["rk_s"],
        c_dmask=consts["dmask"], c_kw=consts["kw"], c_cmask=consts["cmask"], c_ident=consts["ident"])
    maps = []
    for c in range(NCORES):
        pb = pb_of_core(c)
        m = dict(shared)
        m["xp"] = f(inp["x_prompt"][pb, :T])
        m["memp"] = f(inp["mem_prompt"][pb])
        sbs = [2 * c, 2 * c + 1]
        m["xs"] = f(inp["x_sample"][sbs])
        m["st_in"] = f(inp["state_ret"][0, sbs])
        m["ck_in"] = f(inp["cache_k"][sbs].reshape(2, PAST, D))
        m["cv_in"] = f(inp["cache_v"][sbs].reshape(2, PAST, D))
        m["cmk_in"] = f(inp["cache_mem_k"][:, sbs].reshape(2, 2, 256, 512))
        m["cmv_in"] = f(inp["cache_mem_v"][:, sbs].reshape(2, 2, 256, 512))
        maps.append(m)
    return maps


def kernel(**inp):
    inp = {k: np.asarray(v) for k, v in inp.items()}
    NBP, T = 4, SEQ
    nc, consts = _get_prog(NBP, T)
    maps = make_in_maps(inp, consts, NBP, T, lambda c: list(range(4 * c, 4 * c + 4)))
    res = run_bass_kernel_spmd(nc, maps, core_ids=list(range(NCORES))).results
    B = 32
    y_p = np.concatenate([r["y_p"] for r in res], 0)
    y_s = np.concatenate([r["y_s"] for r in res], 0)
    st_p = np.concatenate([r["st_p"] for r in res], 0)[None]
    k_p = np.concatenate([r["k_p"] for r in res], 0).reshape(B, T, 4, 2, 128)
    v_p = np.concatenate([r["v_p"] for r in res], 0).reshape(B, T, 4, 256)
    mk_p = np.concatenate([r["mk_p"] for r in res], 1).reshape(2, B, 256, 4, 128)
    mv_p = np.concatenate([r["mv_p"] for r in res], 1).reshape(2, B, 256, 4, 128)
    st_s = np.concatenate([r["st_s"] for r in res], 0)[None]
    k_s = np.concatenate([r["k_s"] for r in res], 0).reshape(16, 32, 4, 2, 128)
    v_s = np.concatenate([r["v_s"] for r in res], 0).reshape(16, 32, 4, 256)
    return (y_p, y_s, st_p, k_p, v_p, mk_p, mv_p, st_s, k_s, v_s)
```

```python
import math
from contextlib import ExitStack

import numpy as np
import ml_dtypes

import concourse.bass as bass
import concourse.mybir as mybir
from concourse.bass_utils import run_bass_kernel_spmd

F32 = mybir.dt.float32
BF16 = mybir.dt.bfloat16
AF = mybir.ActivationFunctionType
ALU = mybir.AluOpType

D = 1024
SEQ = 2048
NCORES = 8
PAST = 4096
EPS = 1e-6
LAM_INIT = 0.8 - 0.6 * math.exp(-0.3 * 1)
SC_ATT = 128 ** -0.5
NKT_S = 33
INLOOP_CONV = True
MAX_DMA_INFLIGHT = 16
PIPE_NEXT = True
CONV_AT = (0, 4, 0, 0)


class Buf:
    __slots__ = ("name", "writers", "readers", "overlaps")

    def __init__(self, name):
        self.name = name
        self.writers = {}
        self.readers = {}
        self.overlaps = []


def link(a_list, b_list):
    for a in a_list:
        for b in b_list:
            a.overlaps.append(b)
            b.overlaps.append(a)


class Op:
    __slots__ = ("eng", "fn", "deps", "raw", "is_dma", "key", "dcount", "needs_inc", "inc_val", "pos")

    def __init__(self, eng, fn, is_dma, key):
        self.eng = eng
        self.fn = fn
        self.deps = set()
        self.raw = set()
        self.is_dma = is_dma
        self.key = key
        self.dcount = 0
        self.needs_inc = False
        self.inc_val = 0
        self.pos = 0


class Sched:
    ENGS = ("pe", "act", "dve", "pool", "sp")

    def __init__(self):
        self.ops = {e: [] for e in self.ENGS}
        self.last_dma = {}
        self.dma_hist = []
        self.dma_counts = {}
        self.dry = False
        self.n = 0

    def add(self, eng, fn, reads=(), writes=(), key=None, pw=()):
        if self.dry:
            return None
        is_dma = key is not None
        op = Op(eng, fn, is_dma, key)
        self.n += 1
        op.pos = self.n
        rset = []
        for b in reads:
            rset.append(b)
            rset.extend(b.overlaps)
        wset = []
        for b in writes:
            wset.append(b)
            wset.extend(b.overlaps)
        pset = []
        for b in pw:
            pset.append(b)
            wset.extend(b.overlaps)
        for b in rset:
            for w in b.writers.values():
                op.deps.add(w)
                op.raw.add(w)
        for b in wset:
            for w in b.writers.values():
                op.deps.add(w)
            for r in b.readers.values():
                op.deps.add(r)
        for b in pset:
            for r in b.readers.values():
                op.deps.add(r)
        if is_dma:
            self.dma_hist.append(op)
            if len(self.dma_hist) > MAX_DMA_INFLIGHT:
                old_ = self.dma_hist[-1 - MAX_DMA_INFLIGHT]
                op.deps.add(old_)
            prev = self.last_dma.get(key)
            if prev is not None:
                op.deps.add(prev)
                op.raw.add(prev)
            self.last_dma[key] = op
            self.dma_counts[key] = self.dma_counts.get(key, 0) + 1
            op.dcount = self.dma_counts[key]
        rk = ("dma", key) if is_dma else eng
        for b in reads:
            b.readers[rk] = op
        for b in writes:
            b.writers = {rk: op}
            b.readers = {}
        for b in pw:
            b.writers[rk] = op
        op.deps.discard(op)
        self.ops[eng].append(op)
        return op

    def emit(self, nc, es):
        for e in self.ENGS:
            for op in self.ops[e]:
                need = set()
                for d in op.deps:
                    if d.is_dma:
                        need.add(d)
                    elif d.eng == op.eng and not op.is_dma:
                        if op.eng != "pe":
                            need.add(d)
                    else:
                        need.add(d)
                op.deps = need
                for d in need:
                    if not d.is_dma:
                        d.needs_inc = True
        sems = {}
        for e in self.ENGS:
            sems[e] = es.enter_context(nc.semaphore("sem_" + e))
            c = 0
            for op in self.ops[e]:
                if op.needs_inc and not op.is_dma:
                    c += 1
                    op.inc_val = c
        for k in self.dma_counts:
            sems[("dma", k)] = es.enter_context(nc.semaphore("semd_%s" % (str(k),)))
        self.nsem = len(sems)
        handles = {"pe": nc.tensor, "act": nc.scalar, "dve": nc.vector, "pool": nc.gpsimd, "sp": nc.sync}
        block = es.enter_context(nc.Block())
        sect = {"pe": block.tensor, "act": block.scalar, "dve": block.vector, "pool": block.gpsimd,
                "sp": block.sync}

        def make(e):
            def body(_eng):
                h = handles[e]
                known = {}
                for op in self.ops[e]:
                    waits = {}
                    for d in op.deps:
                        if d.is_dma:
                            sk = ("dma", d.key)
                            v = d.dcount * 16
                        else:
                            sk = d.eng
                            v = d.inc_val
                        if known.get(sk, 0) >= v:
                            continue
                        if waits.get(sk, 0) < v:
                            waits[sk] = v
                    for sk, v in waits.items():
                        h.wait_ge(sems[sk], v)
                        known[sk] = v
                    inst = op.fn(h)
                    if op.is_dma:
                        inst.then_inc(sems[("dma", op.key)], 16)
                    elif op.needs_inc:
                        inst.then_inc(sems[e], 1)
                if e == "sp":
                    for k, op in self.last_dma.items():
                        sk = ("dma", k)
                        v = op.dcount * 16
                        if known.get(sk, 0) < v:
                            h.wait_ge(sems[sk], v)
            return body

        for e in self.ENGS:
            sect[e](make(e))


class Ring:
    def __init__(self, items):
        self.items = items
        self.i = 0

    def next(self):
        it = self.items[self.i % len(self.items)]
        self.i += 1
        return it


class WStream:
    def __init__(self, S, slots, depth):
        self.S = S
        self.slots = slots
        self.depth = depth
        self.plan = []
        self.i = 0
        self.issued = 0
        self.recording = True

    def _issue(self, j):
        src, nk, nco = self.plan[j]
        t, b = self.slots[j % len(self.slots)]
        dst = t[:, 0:nk, 0:nco]
        self.S.add("sp", lambda h, dst=dst, src=src: h.dma_start(out=dst, in_=src),
                   reads=[WSC_BUF], writes=[b], key="w%d" % (j % len(self.slots)))

    def get(self, src, nk, nco):
        if self.recording:
            self.plan.append((src, nk, nco))
            return self.slots[0]
        j = self.i
        self.i += 1
        while self.issued < min(len(self.plan), j + self.depth + 1):
            self._issue(self.issued)
            self.issued += 1
        return self.slots[j % len(self.slots)]


WSC_BUF = Buf("wscratch")


def _consts(T):
    half = 128
    inv = (10000.0 ** (-np.arange(half, dtype=np.float32) / np.float32(half))).astype(np.float32)
    g = 1.0 - 2.0 ** (-5.0 - np.arange(4, dtype=np.float64))
    logg = np.log(g)

    def tables(pos, period):
        ang = (pos.astype(np.float32)[None, :] * inv[:, None]).astype(np.float32)
        c = np.cos(ang.astype(np.float64))
        s = np.sin(ang.astype(np.float64))
        l = (np.arange(len(pos)) % period).astype(np.float64)
        rq = np.zeros((128, 8, len(pos)), np.float64)
        for h in range(4):
            dec = np.exp(logg[h] * (l + 1.0))
            rq[:, 2 * h, :] = c * dec[None, :]
            rq[:, 2 * h + 1, :] = s * dec[None, :]
        rk = np.stack([c, s], axis=1)
        return rq.astype(ml_dtypes.bfloat16), rk.astype(ml_dtypes.bfloat16)

    rq_p, rk_p = tables(np.arange(T), 128)
    rq_s, rk_s = tables(np.concatenate([PAST + np.arange(32)] * 2), 32)
    m = np.arange(128)[:, None].astype(np.float64)
    l = np.arange(128)[None, :].astype(np.float64)
    dmask = np.zeros((128, 4, 128), np.float32)
    kw = np.zeros((128, 8), np.float32)
    for h in range(4):
        dmask[:, h, :] = np.where(l >= m, np.exp(-logg[h] * (m + 1.0)) / 16.0, 0.0)
        kw[:, h] = np.exp(logg[h] * (127.0 - np.arange(128))) / 16.0
        kw[:32, 4 + h] = np.exp(logg[h] * (31.0 - np.arange(32))) / 16.0
    gl_p = [float(np.exp(logg[h] * 128.0)) for h in range(4)]
    gl_s = [float(np.exp(logg[h] * 32.0)) for h in range(4)]
    cm = np.zeros((128, 2, 2, 256), np.float32)
    kk = np.arange(128)[:, None]
    qq = np.arange(256)[None, :]
    cm[:, 0, :, :] = ((kk // 64) <= (qq // 64))[:, None, :]
    cm[:, 1, :, :] = ((2 + kk // 64) <= (qq // 64))[:, None, :]
    ident = np.eye(128, dtype=np.float32)
    return dict(rq_p=rq_p, rk_p=rk_p, rq_s=rq_s, rk_s=rk_s, dmask=dmask, kw=kw,
                cmask=cm.astype(ml_dtypes.bfloat16), ident=ident), gl_p, gl_s


def build(NBP=4, T=SEQ, do_sample=True):
    assert T % 512 == 0
    NTILE = T // 512
    consts, GL_P, GL_S = _consts(T)
    nc = bass.Bass("TRN2", target_bir_lowering=False)
    es = ExitStack()

    def din(name, shape, dt=F32):
        return nc.dram_tensor(name, list(shape), dt, kind="ExternalInput").ap()

    def dout(name, shape, dt=F32):
        return nc.dram_tensor(name, list(shape), dt, kind="ExternalOutput").ap()

    def dscr(name, shape, dt=BF16):
        return nc.dram_tensor(name, list(shape), dt, kind="Internal").ap()

    xp = din("xp", [NBP, T, D])
    xs = din("xs", [2, 32, D])
    st_in = din("st_in", [2, 4, 256, 256])
    ck_in = din("ck_in", [2, PAST, D])
    cv_in = din("cv_in", [2, PAST, D])
    cmk_in = din("cmk_in", [2, 2, 256, 512])
    cmv_in = din("cmv_in", [2, 2, 256, 512])
    memp = din("memp", [NBP, 256, D])
    w_in_a = din("w_in_a", [D, 5120])
    w_out_a = din("w_out_a", [1536, D])
    w_kv = din("w_kv", [D, 2048])
    w_in_b = din("w_in_b", [D, 3072])
    w_out_b = din("w_out_b", [1536, D])
    w_mem = din("w_mem", [2, D, D])
    gains_in = din("gains", [6, D])
    subln_in = din("subln", [256])
    lam_in = din("lam_in", [512])
    c_rq_p = din("c_rq_p", [128, 8, T], BF16)
    c_rk_p = din("c_rk_p", [128, 2, T], BF16)
    c_rq_s = din("c_rq_s", [128, 8, 64], BF16)
    c_rk_s = din("c_rk_s", [128, 2, 64], BF16)
    c_dmask = din("c_dmask", [128, 4, 128])
    c_kw = din("c_kw", [128, 8])
    c_cmask = din("c_cmask", [128, 2, 2, 256], BF16)
    c_ident = din("c_ident", [128, 128])
    y_p = dout("y_p", [NBP, T, D])
    y_s = dout("y_s", [2, 32, D])
    st_p = dout("st_p", [NBP, 4, 256, 256])
    k_p = dout("k_p", [NBP, T, D])
    v_p = dout("v_p", [NBP, T, D])
    mk_p = dout("mk_p", [2, NBP, 256, 512])
    mv_p = dout("mv_p", [2, NBP, 256, 512])
    st_s = dout("st_s", [2, 4, 256, 256])
    k_s = dout("k_s", [2, 32, D])
    v_s = dout("v_s", [2, 32, D])
    ws_in_a = dscr("ws_in_a", [128, 8, 5120])
    ws_out_a = dscr("ws_out_a", [128, 12, 1024])
    ws_kv = dscr("ws_kv", [128, 8, 2048])
    ws_in_b = dscr("ws_in_b", [128, 8, 3072])
    ws_out_b = dscr("ws_out_b", [128, 12, 1024])
    ws_mem = dscr("ws_mem", [2, 128, 8, 1024])
    kt_sc = [dscr("kt_sc%d" % i, [8, 128, T]) for i in range(2)]
    v_sc = [dscr("v_sc%d" % i, [T // 128, 128, 4, 258]) for i in range(2)]
    kts_sc = dscr("kts_sc", [2, 8, 128, NKT_S * 128])
    vs_sc = dscr("vs_sc", [2, NKT_S, 128, 4, 258])

    S = Sched()

    def sb(name, shape, dt):
        return es.enter_context(nc.sbuf_tensor(name, list(shape), dt))

    ident_f = sb("ident_f", [128, 128], F32)
    ident_b = sb("ident_b", [128, 128], BF16)
    dmask = sb("dmask", [128, 4, 128], F32)
    kwt = sb("kwt", [128, 8], F32)
    cmask = sb("cmask", [128, 2, 2, 256], BF16)
    gains = sb("gains_t", [128, 6, 8], F32)
    subln = sb("subln_t", [128, 2], F32)
    nf_b = sb("nf_b", [128, D], F32)
    lamw = sb("lamw", [128, 8], F32)
    mhalf = sb("mhalf", [128, 8], F32)
    B_const = Buf("const")

    x_t = [sb("x%d" % i, [128, 4, D], F32) for i in range(2)]
    B_x = [Buf("x%d" % i) for i in range(2)]
    r1 = sb("r1", [128, 12 * 512], BF16)
    xn = r1[:, 0:4096].rearrange("p (s f) -> p s f", s=4)
    omT = r1[:, :].rearrange("p (c t) -> p c t", c=12)
    B_r1 = Buf("r1")
    lamt = r1[:, 0:1024].bitcast(F32)
    r2 = sb("r2", [128, 4 * 1536], BF16)
    hT = r2[:, 0:4096].rearrange("p (c t) -> p c t", c=8)
    om = r2[:, :].rearrange("p (s f) -> p s f", s=4)
    B_r2 = Buf("r2")
    qT = sb("qT", [128, 8, 512], BF16)
    B_qT = Buf("qT")
    sg = sb("sg", [128, 4, D], BF16)
    B_sg = Buf("sg")
    qmT = sb("qmT", [128, 4, 512], BF16)
    B_qmT = Buf("qmT")
    sgm = sb("sgm", [128, 4, 512], BF16)
    B_sgm = Buf("sgm")
    rq_t = sb("rq_t", [128, 8, 512], BF16)
    rk_t = sb("rk_t", [128, 2, 512], BF16)
    B_rope = Buf("rope")
    S_f = sb("S_f", [128, 4, 2, 256], F32)
    B_Sf = [Buf("Sf%d" % h) for h in range(4)]
    _sbt = sb("S_b", [128, 4, 2, 256], BF16)
    S_b = [_sbt, _sbt]
    _bsb = [Buf("Sb_%d" % h) for h in range(4)]
    B_Sb = [_bsb, _bsb]
    memKT = sb("memKT", [128, 2, 2, 4, 256], BF16)
    memV = sb("memV", [128, 2, 2, 2, 4, 130], BF16)
    B_mem = [Buf("mem%d" % i) for i in range(2)]
    wsl = [(sb("wsl%d" % i, [128, 8, 512], BF16), Buf("wsl%d" % i)) for i in range(3)]
    mPT = [(sb("mPT%d" % i, [128, 2, 128], BF16), Buf("mPT%d" % i)) for i in range(4)]
    small = sb("small", [128, 64], F32)
    B_small = {}

    def sm(name, c0, n):
        if name not in B_small:
            B_small[name] = Buf("sm_" + name)
        return small[:, c0:c0 + n], B_small[name]

    AB = sb("AB", [128, 21512], BF16)
    kT = AB[:, 0:4096].rearrange("p (c t) -> p c t", c=8)
    vtok = AB[:, 4096:8192].rearrange("p (s f) -> p s f", s=4)
    kt_tok = [AB[:, 8192 + i * 1024: 8192 + (i + 1) * 1024] for i in range(4)]
    ropetmp = [AB[:, 12288 + i * 1024: 12288 + (i + 1) * 1024].bitcast(F32) for i in range(4)]
    scT = [AB[:, 16384 + i * 512: 16384 + (i + 1) * 512].rearrange("p (h l) -> p h l", h=4) for i in range(4)]
    B_kT, B_vtok = Buf("kT"), Buf("vtok")
    B_kttok = [Buf("kttok%d" % i) for i in range(4)]
    B_ropetmp = [Buf("ropetmp%d" % i) for i in range(4)]
    B_scT = [Buf("scT%d" % i) for i in range(4)]
    A_bufs = [B_kT, B_vtok] + B_kttok + B_ropetmp + B_scT
    stage = [AB[:, i * 1024:(i + 1) * 1024].bitcast(F32) for i in range(4)]
    KTn = AB[:, 4096:8192].rearrange("p (c t) -> p c t", c=8)
    Vn = AB[:, 8192:8192 + 4128].rearrange("p (s h e) -> p s h e", s=4, h=4)
    PT = [AB[:, 12320 + i * 512: 12320 + (i + 1) * 512].rearrange("p (m q) -> p m q", m=2) for i in range(4)]
    KTc = [AB[:, 14368 + i * 1024: 14368 + (i + 1) * 1024].rearrange("p (m k) -> p m k", m=2) for i in range(3)]
    Vc = [AB[:, 17440 + i * 1032: 17440 + (i + 1) * 1032].rearrange("p (t e) -> p t e", t=4) for i in range(3)]
    otmp = [AB[:, 20536 + i * 488: 20536 + i * 488 + 488] for i in range(2)]
    B_stage = [Buf("stage%d" % i) for i in range(4)]
    B_KTn, B_Vn = Buf("KTn"), Buf("Vn")
    B_PT = [Buf("PT%d" % i) for i in range(4)]
    B_KTc = [Buf("KTc%d" % i) for i in range(3)]
    B_Vc = [Buf("Vc%d" % i) for i in range(3)]
    B_otmp = [Buf("otmp0"), Buf("otmp1")]
    B_bufs = B_stage + [B_KTn, B_Vn] + B_PT + B_KTc + B_Vc + B_otmp
    otm = sb("otm", [128, 2, 256], F32)
    B_otm = [Buf("otm0"), Buf("otm1")]

    psum = [es.enter_context(nc.psum_tensor("ps%d" % i, [128, 512], F32)) for i in range(8)]
    B_ps = [Buf("ps%d" % i) for i in range(8)]
    PS = Ring(list(range(8)))

    def ps_f32(i):
        return psum[i]

    def ps_bf(i):
        return psum[i][:, :].bitcast(BF16)

    ws = WStream(S, wsl, depth=2)
    evq = Ring(["dve", "act"])

    B_ktsc = [[Buf("ktsc%d_%d" % (i, t)) for t in range(NTILE)] for i in range(2)]
    B_vsc = [[Buf("vsc%d_%d" % (i, t)) for t in range(NTILE)] for i in range(2)]
    B_ktss = [[Buf("ktss%d_%d" % (b, c)) for c in range(9)] for b in range(2)]
    B_vss = [[Buf("vss%d_%d" % (b, c)) for c in range(9)] for b in range(2)]
    B_out = Buf("outputs")

    def dma(dst, src, reads, writes, key):
        S.add("sp", lambda h: h.dma_start(out=dst, in_=src), reads=reads, writes=writes, key=key)

    def dma_nc(dst, src, reads, writes, key):
        def f(h):
            with nc.allow_non_contiguous_dma(reason="small constant layout"):
                return h.dma_start(out=dst, in_=src)
        S.add("sp", f, reads=reads, writes=writes, key=key)

    def mm(out, lhsT, rhs, start, stop, reads, writes):
        S.add("pe", lambda h: h.matmul(out, lhsT, rhs, start=start, stop=stop), reads=reads, writes=writes)

    def tr(out, in_, ident, reads, writes):
        S.add("pe", lambda h: h.transpose(out, in_, ident), reads=reads, writes=writes)

    def act(out, in_, func, reads, writes, scale=None, bias=None, accum=None):
        kw = {}
        if scale is not None:
            kw["scale"] = scale
        if bias is not None:
            kw["bias"] = bias
        if accum is not None:
            kw["accum_out"] = accum
        S.add("act", lambda h: h.activation(out, in_, func, **kw), reads=reads, writes=writes)

    def copy_ev(eng, out, in_, reads, writes):
        if eng == "act":
            act(out, in_, AF.Copy, reads, writes)
        else:
            S.add(eng, lambda h: h.tensor_copy(out, in_), reads=reads, writes=writes)

    def tt(eng, out, a, b, op, reads, writes):
        S.add(eng, lambda h: h.tensor_tensor(out, a, b, op), reads=reads, writes=writes)

    def ts(eng, out, a, s1, s2, op0, op1, reads, writes):
        if op1 is None:
            S.add(eng, lambda h: h.tensor_scalar(out, a, s1, None, op0), reads=reads, writes=writes)
        else:
            S.add(eng, lambda h: h.tensor_scalar(out, a, s1, s2, op0, op1), reads=reads, writes=writes)

    def stt(out, a, sc, b, op0, op1, reads, writes):
        S.add("dve", lambda h: h.scalar_tensor_tensor(out, a, sc, b, op0, op1), reads=reads, writes=writes)

    def rstd_from_ss(ss_ap, B_ss, out_ap, B_o, n, mult, epsv, Pn):
        tmp, B_tmp = sm("rs_tmp", 56, 4)
        ts("pool", tmp[:Pn, 0:n], ss_ap, mult, epsv, ALU.mult, ALU.add, [B_ss], [B_tmp])
        tt("pool", out_ap, tmp[:Pn, 0:n], mhalf[:Pn, 0:n], ALU.pow, [B_tmp, B_const], [B_o])

    prep_ld = [(AB[:, i * 4096:(i + 1) * 4096].bitcast(F32), Buf("wld%d" % i)) for i in range(3)]
    prep_cv = [(AB[:, 12288 + i * 2056:12288 + (i + 1) * 2056], Buf("wcv%d" % i)) for i in range(3)]

    def setup():
        dma(ident_f[:, :], c_ident[:, :], [], [B_const], "c0")
        dma(dmask[:, :, :], c_dmask[:, :, :], [], [B_const], "c0")
        dma(kwt[:, :], c_kw[:, :], [], [B_const], "c0")
        dma(cmask[:, :, :, :], c_cmask[:, :, :, :], [], [B_const], "c0")
        dma_nc(gains[:, :, :], gains_in.rearrange("g (k p) -> p g k", p=128), [], [B_const], "c0")
        dma_nc(subln[:, :], subln_in.rearrange("(k p) -> p k", p=128), [], [B_const], "c0")
        dma(nf_b[:, :], gains_in[5:6, :].to_broadcast([128, D]), [], [B_const], "c0")
        dma(lamt[:, :], lam_in.rearrange("(o n) -> o n", o=1).to_broadcast([128, 512]), [], [B_r1], "c0")
        S.add("pool", lambda h: h.memset(mhalf[:, :], -0.5), [], [B_const])
        S.add("dve", lambda h: h.tensor_copy(ident_b[:, :], ident_f[:, :]), [B_const], [B_const])
        ts("dve", subln[:, :], subln[:, :], 1.0 - LAM_INIT, None, ALU.mult, None, [B_const], [B_const])
        tt("dve", lamt[:, 0:128], lamt[:, 0:128], lamt[:, 128:256], ALU.mult, [B_r1], [B_r1])
        tt("dve", lamt[:, 256:384], lamt[:, 256:384], lamt[:, 384:512], ALU.mult, [B_r1], [B_r1])
        S.add("dve", lambda h: h.tensor_reduce(lamw[:, 0:1], lamt[:, 0:128], mybir.AxisListType.X, ALU.add),
              [B_r1], [B_const])
        S.add("dve", lambda h: h.tensor_reduce(lamw[:, 1:2], lamt[:, 256:384], mybir.AxisListType.X, ALU.add),
              [B_r1], [B_const])
        act(lamw[:, 0:2], lamw[:, 0:2], AF.Exp, [B_const], [B_const])
        tt("dve", lamw[:, 2:3], lamw[:, 1:2], lamw[:, 0:1], ALU.subtract, [B_const], [B_const])
        ts("dve", lamw[:, 2:3], lamw[:, 2:3], -LAM_INIT, None, ALU.add, None, [B_const], [B_const])

        xl, xc = [], []
        for i in range(0):
            for j in range(2):
                bb = Buf("xpl%d_%d" % (i, j))
                bb.overlaps.append(B_x[i])
                B_x[i].overlaps.append(bb)
                xl.append((x_t[i][:, 2 * j:2 * j + 2, :].rearrange("p s d -> p (s d)"), bb))
        for k, (rt, rb) in enumerate(()):
            for j in range(2):
                bb = Buf("rcv%d_%d" % (k, j))
                bb.overlaps.append(rb)
                rb.overlaps.append(bb)
                xc.append((rt[:, j * 2048:(j + 1) * 2048], bb))
        NSL = 3
        ldr = Ring(prep_ld)
        cvr = Ring(prep_cv)
        engs = Ring(["dve", "act"])
        ldk = Ring(["pl%d" % i for i in range(NSL)])
        stk = Ring(["ps%d" % i for i in range(NSL)])

        def prep(wsrc, R, C, dst, gain_fn):
            for kc in range(R // 128):
                for c0 in range(0, C, 2048):
                    cw = min(2048, C - c0)
                    (lt, lb), (ct, cb) = ldr.next(), cvr.next()
                    dma(lt[:, 0:cw], wsrc[kc * 128:(kc + 1) * 128, c0:c0 + cw], [], [lb], ldk.next())
                    gsc = gain_fn(kc)
                    e = engs.next()
                    if gsc is None:
                        copy_ev(e, ct[:, 0:cw], lt[:, 0:cw], [lb], [cb])
                    elif e == "act":
                        act(ct[:, 0:cw], lt[:, 0:cw], AF.Copy, [lb, B_const], [cb], scale=gsc)
                    else:
                        ts(e, ct[:, 0:cw], lt[:, 0:cw], gsc, None, ALU.mult, None, [lb, B_const], [cb])
                    S.add("sp", lambda h, d_=dst[:, kc, c0:c0 + cw], s_=ct[:, 0:cw]: h.dma_start(out=d_, in_=s_),
                          reads=[cb], pw=[WSC_BUF], key=stk.next())

        prep(w_in_a, D, 5120, ws_in_a, lambda kc: gains[:, 0, kc:kc + 1])
        prep(w_out_a, 1536, D, ws_out_a, lambda kc: None)
        prep(w_kv, D, 2048, ws_kv, lambda kc: gains[:, 1, kc:kc + 1])
        prep(w_in_b, D, 3072, ws_in_b, lambda kc: gains[:, 2, kc:kc + 1])
        prep(w_out_b, 1536, D, ws_out_b, lambda kc: subln[:, (kc % 2):(kc % 2) + 1] if kc < 8 else None)
        for l in range(2):
            prep(w_mem[l], D, D, ws_mem[l], lambda kc, l=l: gains[:, 3 + l, kc:kc + 1])

    def norm_stats(x_ap, B_xb, nsub, Pn):
        ss, B_ss = sm("ss", 0, 4)
        rs, B_rs = sm("rs", 4, 4)
        junk = otm[:, :, :].rearrange("p j e -> p (j e)").bitcast(BF16)
        for s in range(nsub):
            act(junk[:Pn, 0:D], x_ap[:Pn, s, :], AF.Square, [B_xb], B_otm + [B_ss], accum=ss[:Pn, s:s + 1])
        rstd_from_ss(ss[:Pn, 0:nsub], B_ss, rs[:Pn, 0:nsub], B_rs, nsub, 1.0 / D, EPS, Pn)

    def norm_apply(x_ap, B_xb, nsub, Pn):
        rs, B_rs = sm("rs", 4, 4)
        for s in range(nsub):
            ts("dve", xn[:Pn, s, :], x_ap[:Pn, s, :], rs[:Pn, s:s + 1], None, ALU.mult, None,
               [B_xb, B_rs], [B_r1])
        for s in range(nsub):
            b = PS.next()
            pv = ps_bf(b).rearrange("p (c t) -> p c t", c=8)
            for kc in range(8):
                tr(pv[:, kc, 0:Pn], xn[:Pn, s, kc * 128:(kc + 1) * 128], ident_b[:Pn, :Pn],
                   [B_r1, B_const], [B_ps[b]])
            copy_ev(evq.next(), hT[:, :, s * Pn:(s + 1) * Pn], pv[:, :, 0:Pn], [B_ps[b]], [B_r2])

    def norm_to_hT(x_ap, B_xb, nsub, Pn):
        norm_stats(x_ap, B_xb, nsub, Pn)
        norm_apply(x_ap, B_xb, nsub, Pn)

    def proj_fm(wt, wb, nk, col0, NT, srcT, B_src):
        b = PS.next()
        for kc in range(nk):
            mm(psum[b][:, 0:NT], wt[:, kc, col0:col0 + 128], srcT[:, kc, 0:NT], kc == 0, kc == nk - 1,
               [wb, B_src], [B_ps[b]])
        return b

    def proj_tm(wt, wb, nk, s, Pn, srcT, B_src, ncol=512):
        b = PS.next()
        for kc in range(nk):
            mm(psum[b][:Pn, 0:ncol], srcT[:, kc, s * Pn:(s + 1) * Pn], wt[:, kc, 0:ncol], kc == 0, kc == nk - 1,
               [wb, B_src], [B_ps[b]])
        return b

    tanh_tmp = [sb("tanh%d" % i, [128, 512], BF16) for i in range(2)]
    B_tanh = [Buf("tanh0"), Buf("tanh1")]
    tq = Ring([0, 1])

    def gate_evac(b, Pn, out_ap, B_o):
        i = tq.next()
        act(tanh_tmp[i][:Pn, :], psum[b][:Pn, :], AF.Tanh, [B_ps[b]], [B_tanh[i]], scale=0.5)
        stt(out_ap, tanh_tmp[i][:Pn, :], 1.0, psum[b][:Pn, :], ALU.add, ALU.mult, [B_tanh[i], B_ps[b]], [B_o])

    def rope_pair(bA, bB, hd, NT, tab, tcos, tsin, dstT, B_dst):
        A = psum[bA][:, 0:NT]
        Bp = psum[bB][:, 0:NT]
        c = tab[:, tcos, 0:NT]
        s_ = tab[:, tsin, 0:NT]
        t = [ropetmp[i][:, 0:NT] for i in range(4)]
        tt("dve", t[0], A, c, ALU.mult, [B_ps[bA], B_rope], [B_ropetmp[0]])
        tt("dve", t[1], Bp, s_, ALU.mult, [B_ps[bB], B_rope], [B_ropetmp[1]])
        tt("dve", t[2], Bp, c, ALU.mult, [B_ps[bB], B_rope], [B_ropetmp[2]])
        tt("dve", t[3], A, s_, ALU.mult, [B_ps[bA], B_rope], [B_ropetmp[3]])
        tt("pool", dstT[:, 2 * hd, 0:NT], t[0], t[1], ALU.subtract, [B_ropetmp[0], B_ropetmp[1]], [B_dst])
        tt("pool", dstT[:, 2 * hd + 1, 0:NT], t[2], t[3], ALU.add, [B_ropetmp[2], B_ropetmp[3]], [B_dst])

    def in_proj_A(nsub, Pn):
        NT = nsub * Pn
        for g in range(10):
            wt, wb = ws.get(ws_in_a[:, :, g * 512:(g + 1) * 512], 8, 512)
            if g < 4:
                for hh in range(2):
                    hd = (g % 2) * 2 + hh
                    bA = proj_fm(wt, wb, 8, hh * 256, NT, hT, B_r2)
                    bB = proj_fm(wt, wb, 8, hh * 256 + 128, NT, hT, B_r2)
                    if g < 2:
                        rope_pair(bA, bB, hd, NT, rq_t, 2 * hd, 2 * hd + 1, qT, B_qT)
                    else:
                        rope_pair(bA, bB, hd, NT, rk_t, 0, 1, kT, B_kT)
            elif g < 6:
                for s in range(nsub):
                    b = proj_tm(wt, wb, 8, s, Pn, hT, B_r2)
                    copy_ev("act", vtok[:Pn, s, (g - 4) * 512:(g - 3) * 512], psum[b][:Pn, :], [B_ps[b]], [B_vtok])
            elif g < 8:
                for s in range(nsub):
                    b = proj_tm(wt, wb, 8, s, Pn, hT, B_r2)
                    gate_evac(b, Pn, sg[:Pn, s, (g - 6) * 512:(g - 5) * 512], B_sg)
            elif g == 8:
                for hd in range(4):
                    b = proj_fm(wt, wb, 8, hd * 128, NT, hT, B_r2)
                    copy_ev(evq.next(), qmT[:, hd, 0:NT], psum[b][:, 0:NT], [B_ps[b]], [B_qmT])
            else:
                for s in range(nsub):
                    b = proj_tm(wt, wb, 8, s, Pn, hT, B_r2)
                    gate_evac(b, Pn, sgm[:Pn, s, :], B_sgm)

    sbi = [0]

    def retention(nsub, Pn, GL, kwcol, sample):
        for s in range(nsub):
            cs = slice(s * Pn, (s + 1) * Pn)
            bs = PS.next()
            pv = psum[bs][:, :].rearrange("p (h l) -> p h l", h=4)
            for h in range(4):
                for dc in range(2):
                    mm(pv[:Pn, h, 0:Pn], kT[:, 2 * h + dc, cs], qT[:, 2 * h + dc, cs], dc == 0, dc == 1,
                       [B_kT, B_qT], [B_ps[bs]])
            tt("dve", scT[s][:Pn, :, 0:Pn], pv[:Pn, :, 0:Pn], dmask[:Pn, :, 0:Pn], ALU.mult,
               [B_ps[bs], B_const], [B_scT[s]])
            bk = PS.next()
            pk = ps_bf(bk)
            for blk in range(8):
                tr(pk[:Pn, blk * 128:(blk + 1) * 128], kT[:, blk, cs], ident_b[:, :], [B_kT, B_const], [B_ps[bk]])
            for h in range(4):
                act(kt_tok[s][:Pn, h * 256:(h + 1) * 256], pk[:Pn, h * 256:(h + 1) * 256], AF.Copy,
                    [B_ps[bk], B_const], [B_kttok[s]], scale=kwt[:Pn, kwcol + h:kwcol + h + 1])
        for s in range(nsub):
            cs = slice(s * Pn, (s + 1) * Pn)
            if sample:
                dma(S_f[:, :, :, :], st_in[s].rearrange("h (c p) e -> p h c e", p=128), [], B_Sf, "stld")
                for h in range(4):
                    copy_ev("pool", S_b[sbi[0] % 2][:, h, :, :], S_f[:, h, :, :], [B_Sf[h]], [B_Sb[sbi[0] % 2][h]])
            cur, nxt = sbi[0] % 2, (sbi[0] + 1) % 2
            sbi[0] += 1
            bus = []
            for h in range(4):
                bu = PS.next()
                bus.append(bu)
                uv = psum[bu][:, :].rearrange("p (c e) -> p c e", c=2)
                for dc in range(2):
                    mm(uv[:, dc, :], kt_tok[s][:Pn, (2 * h + dc) * 128:(2 * h + dc + 1) * 128],
                       vtok[:Pn, s, h * 256:(h + 1) * 256], True, True, [B_kttok[s], B_vtok], [B_ps[bu]])
            bo = [PS.next(), PS.next()]
            for h in range(4):
                ob = bo[h // 2]
                oview = psum[ob][:Pn, (h % 2) * 256:(h % 2 + 1) * 256]
                mm(oview, scT[s][:Pn, h, 0:Pn], vtok[:Pn, s, h * 256:(h + 1) * 256], True, False,
                   [B_scT[s], B_vtok], [B_ps[ob]])
                for dc in range(2):
                    mm(oview, qT[:, 2 * h + dc, cs], S_b[cur][:, h, dc, :], False, dc == 1,
                       [B_qT, B_Sb[cur][h]], [B_ps[ob]])
            for h in range(4):
                uv = psum[bus[h]][:, :].rearrange("p (c e) -> p c e", c=2)
                stt(S_b[nxt][:, h, :, :], S_f[:, h, :, :], GL[h], uv[:, :, :], ALU.mult, ALU.add,
                    [B_Sf[h], B_ps[bus[h]]], [B_Sb[nxt][h]])
            for h in range(4):
                uv = psum[bus[h]][:, :].rearrange("p (c e) -> p c e", c=2)
                stt(S_f[:, h, :, :], S_f[:, h, :, :], GL[h], uv[:, :, :], ALU.mult, ALU.add,
                    [B_Sf[h], B_ps[bus[h]]], [B_Sf[h]])
            if sample:
                dma(st_s[s].rearrange("h (c p) e -> p h c e", p=128), S_f[:, :, :, :], B_Sf, [], "stst")
            ss4, B_ss4 = sm("ss4", 8, 4)
            rs4, B_rs4 = sm("rs4", 12, 4)
            for h in range(4):
                ob = bo[h // 2]
                oview = psum[ob][:Pn, (h % 2) * 256:(h % 2 + 1) * 256]
                act(otm[:Pn, h % 2, :], oview, AF.Square, [B_ps[ob]], [B_otm[h % 2], B_ss4], accum=ss4[:Pn, h:h + 1])
            rstd_from_ss(ss4[:Pn, 0:4], B_ss4, rs4[:Pn, 0:4], B_rs4, 4, 4.0 / 256, 4.0 * EPS, Pn)
            for h in range(4):
                ob = bo[h // 2]
                oview = psum[ob][:Pn, (h % 2) * 256:(h % 2 + 1) * 256]
                stt(om[:Pn, s, h * 256:(h + 1) * 256], oview, rs4[:Pn, h:h + 1], sg[:Pn, s, h * 256:(h + 1) * 256],
                    ALU.mult, ALU.mult, [B_ps[ob], B_rs4, B_sg], [B_r2])

    def mem_attn(layer, nsub, Pn, slot_of_sub):
        for s in range(nsub):
            sl = slot_of_sub(s)
            for hp in range(2):
                bm = PS.next()
                mview = psum[bm][:, :].rearrange("p (j e) -> p j e", j=2)
                pts = []
                for j in range(2):
                    h = hp * 2 + j
                    bsx = PS.next()
                    sv = psum[bsx][:, :].rearrange("p (m t) -> p m t", m=4)
                    for mc in range(2):
                        mm(sv[:, mc, 0:Pn], memKT[:, sl, layer, h, mc * 128:(mc + 1) * 128],
                           qmT[:, h, s * Pn:(s + 1) * Pn], True, True, [B_mem[sl], B_qmT], [B_ps[bsx]])
                    pt, pb = mPT[(2 * s + hp * 2 + j) % 4]
                    act(pt[:, :, 0:Pn], sv[:, 0:2, 0:Pn], AF.Exp, [B_ps[bsx]], [pb], scale=SC_ATT)
                    pts.append((pt, pb))
                for j in range(2):
                    h = hp * 2 + j
                    pt, pb = pts[j]
                    for mc in range(2):
                        mm(mview[:Pn, j, 0:129], pt[:, mc, 0:Pn], memV[:, sl, layer, mc, h, 0:129],
                           mc == 0, mc == 1, [pb, B_mem[sl]], [B_ps[bm]])
                rr, B_rr = sm("mrr", 16, 2)
                for j in range(2):
                    S.add("dve", lambda hh, j=j, mview=mview: hh.reciprocal(rr[:Pn, j:j + 1], mview[:Pn, j, 128:129]),
                          [B_ps[bm]], [B_rr])
                for j in range(2):
                    h = hp * 2 + j
                    stt(om[:Pn, s, 1024 + h * 128:1024 + (h + 1) * 128], mview[:Pn, j, 0:128], rr[:Pn, j:j + 1],
                        sgm[:Pn, s, h * 128:(h + 1) * 128], ALU.mult, ALU.mult, [B_ps[bm], B_rr, B_sgm], [B_r2])

    def out_proj(wsrc, x_ap, B_xb, nsub, Pn):
        for s in range(nsub):
            b0, b1 = PS.next(), PS.next()
            p0 = ps_bf(b0).rearrange("p (c t) -> p c t", c=8)
            p1 = ps_bf(b1).rearrange("p (c t) -> p c t", c=8)
            for fc in range(12):
                pv, bb = (p0, b0) if fc < 8 else (p1, b1)
                tr(pv[:, fc % 8, 0:Pn], om[:Pn, s, fc * 128:(fc + 1) * 128], ident_b[:Pn, :Pn],
                   [B_r2, B_const], [B_ps[bb]])
            copy_ev(evq.next(), omT[:, 0:8, s * Pn:(s + 1) * Pn], p0[:, :, 0:Pn], [B_ps[b0]], [B_r1])
            copy_ev(evq.next(), omT[:, 8:12, s * Pn:(s + 1) * Pn], p1[:, 0:4, 0:Pn], [B_ps[b1]], [B_r1])
        for cg in range(2):
            banks = [PS.next() for _ in range(nsub)]
            for half in range(2):
                wt, wb = ws.get(wsrc[:, half * 6:(half + 1) * 6, cg * 512:(cg + 1) * 512], 6, 512)
                for s in range(nsub):
                    b = banks[s]
                    for f6 in range(6):
                        fc = half * 6 + f6
                        mm(psum[b][:Pn, 0:512], omT[:, fc, s * Pn:(s + 1) * Pn], wt[:, f6, 0:512], fc == 0, fc == 11,
                           [wb, B_r1], [B_ps[b]])
            for s in range(nsub):
                b = banks[s]
                tt("dve", x_ap[:Pn, s, cg * 512:(cg + 1) * 512], psum[b][:Pn, :], x_ap[:Pn, s, cg * 512:(cg + 1) * 512],
                   ALU.add, [B_ps[b], B_xb], [B_xb])

    stq = Ring([0, 1, 2, 3])

    def kv_proj(nsub, Pn, k_dst, v_dst, kt_dst_fn, v_dst_fn, B_ktd, B_vd):
        S.add("pool", lambda h: h.memset(Vn[:, :, :, 256:258], 1.0), [], [B_Vn])
        for g in range(4):
            wt, wb = ws.get(ws_kv[:, :, g * 512:(g + 1) * 512], 8, 512)
            for s in range(nsub):
                b = proj_tm(wt, wb, 8, s, Pn, hT, B_r2)
                i = stq.next()
                copy_ev("act", stage[i][:Pn, :], psum[b][:Pn, :], [B_ps[b]], [B_stage[i]])
                if g < 2:
                    dma(k_dst(s)[:, g * 512:(g + 1) * 512], stage[i][:Pn, :], [B_stage[i]], [], "st%d" % i)
                    bt = PS.next()
                    tv = psum[bt][:, :].rearrange("p (c t) -> p c t", c=4)
                    for blk in range(4):
                        tr(tv[:, blk, 0:Pn], stage[i][:Pn, blk * 128:(blk + 1) * 128], ident_f[:Pn, :Pn],
                           [B_stage[i], B_const], [B_ps[bt]])
                    copy_ev("dve", KTn[:, g * 4:(g + 1) * 4, s * Pn:(s + 1) * Pn], tv[:, :, 0:Pn], [B_ps[bt]], [B_KTn])
                else:
                    dma(v_dst(s)[:, (g - 2) * 512:(g - 1) * 512], stage[i][:Pn, :], [B_stage[i]], [], "st%d" % i)
                    copy_ev("pool", Vn[:Pn, s, (g - 2) * 2:(g - 1) * 2, 0:256],
                            stage[i][:Pn, :].rearrange("p (h e) -> p h e", h=2), [B_stage[i]], [B_Vn])
        kt_dst_fn()
        v_dst_fn()

    def in_proj_B(nsub, Pn):
        NT = nsub * Pn
        for g in range(6):
            wt, wb = ws.get(ws_in_b[:, :, g * 512:(g + 1) * 512], 8, 512)
            if g < 2:
                for blk in range(4):
                    b = proj_fm(wt, wb, 8, blk * 128, NT, hT, B_r2)
                    copy_ev(evq.next(), qT[:, g * 4 + blk, 0:NT], psum[b][:, 0:NT], [B_ps[b]], [B_qT])
            elif g < 4:
                for s in range(nsub):
                    b = proj_tm(wt, wb, 8, s, Pn, hT, B_r2)
                    gate_evac(b, Pn, sg[:Pn, s, (g - 2) * 512:(g - 1) * 512], B_sg)
            elif g == 4:
                for hd in range(4):
                    b = proj_fm(wt, wb, 8, hd * 128, NT, hT, B_r2)
                    copy_ev(evq.next(), qmT[:, hd, 0:NT], psum[b][:, 0:NT], [B_ps[b]], [B_qmT])
            else:
                for s in range(nsub):
                    b = proj_tm(wt, wb, 8, s, Pn, hT, B_r2)
                    gate_evac(b, Pn, sgm[:Pn, s, :], B_sgm)

    kvq = Ring([0, 1, 2])
    ptq = Ring([0, 1, 2, 3])

    def attn_tile(blocks):
        LA = 2
        sring = [0, 1, 2, 3]
        steps = []
        cl = []
        for bi, blk in enumerate(blocks):
            nst = sum(c["nkt"] for c in blk["chunks"])
            k = 0
            for c in blk["chunks"]:
                gi = len(cl)
                cl.append(c)
                koff = 0
                for t in range(c["nkt"]):
                    steps.append(dict(b=bi, g=gi, t=t, kp=c["kps"][t], koff=koff, mask=c["masks"][t],
                                      first=(k == 0), last=(k == nst - 1)))
                    koff += c["kps"][t]
                    k += 1
                c["last_step"] = len(steps) - 1
        total = len(steps)
        slot_of = {}
        slot_last = {0: -1, 1: -1, 2: -1}
        state = dict(pv_done=0, nstream=0)

        def load_chunk(gi, force):
            if gi >= len(cl) or gi in slot_of:
                return
            c = cl[gi]
            if "direct" in c:
                slot_of[gi] = None
                return
            ci = state["nstream"] % 3
            if slot_last[ci] >= state["pv_done"] and not force:
                return
            assert slot_last[ci] < state["pv_done"]
            state["nstream"] += 1
            slot_of[gi] = ci
            slot_last[ci] = c["last_step"]
            nk = sum(c["kps"])
            dma(KTc[ci][:, :, 0:nk], c["kt_src"], c["deps"], [B_KTc[ci]], "ktc%d" % ci)
            dma(Vc[ci][:, 0:c["nkt"], :], c["v_src"], c["deps"], [B_Vc[ci]], "vc%d" % ci)

        def emit_scores(st):
            load_chunk(st["g"], True)
            blk = blocks[st["b"]]
            c = cl[st["g"]]
            ci = slot_of[st["g"]]
            kp = st["kp"]
            NQ = blk["nqs"] * blk["Pn"]
            h = blk["h"]
            bs = sring[attn_tile.sr % len(sring)]
            attn_tile.sr += 1
            sv = psum[bs][:, :].rearrange("p (m q) -> p m q", m=2)
            for m in range(2):
                if ci is None:
                    lk = c["direct"][0](m, st["koff"], kp)
                    rb = c["direct"][2]
                else:
                    lk = KTc[ci][:, m, st["koff"]:st["koff"] + kp]
                    rb = [B_KTc[ci]]
                mm(sv[:kp, m, 0:NQ], lk, qT[:, 2 * h + m, blk["qc0"]:blk["qc0"] + NQ],
                   True, True, rb + [B_qT], [B_ps[bs]])
            pi = ptq.next()
            act(PT[pi][:kp, :, 0:NQ], sv[:kp, :, 0:NQ], AF.Exp, [B_ps[bs]], [B_PT[pi]], scale=SC_ATT)
            if st["mask"] is not None:
                tt("pool", PT[pi][:kp, :, 0:NQ], PT[pi][:kp, :, 0:NQ], cmask[:kp, st["mask"], :, 0:NQ], ALU.mult,
                   [B_PT[pi], B_const], [B_PT[pi]])
            st["pi"] = pi

        for i in range(min(LA, total)):
            emit_scores(steps[i])
        for i in range(total):
            if i + LA < total:
                emit_scores(steps[i + LA])
            nxt = steps[min(i + LA, total - 1)]["g"]
            load_chunk(nxt + 1, False)
            load_chunk(nxt + 2, False)
            st = steps[i]
            blk = blocks[st["b"]]
            c = cl[st["g"]]
            ci = slot_of[st["g"]]
            kp, pi, Pn, nqs = st["kp"], st["pi"], blk["Pn"], blk["nqs"]
            obank = [[4 + 2 * j, 5 + 2 * j] for j in range(nqs)]
            for j in range(nqs):
                for m in range(2):
                    ob = obank[j][m]
                    if ci is None:
                        rv = c["direct"][1](st["t"], kp)
                        rb = c["direct"][2]
                    else:
                        rv = Vc[ci][:kp, st["t"], 0:257]
                        rb = [B_Vc[ci]]
                    mm(psum[ob][:Pn, 0:257], PT[pi][:kp, m, j * Pn:(j + 1) * Pn], rv,
                       st["first"], st["last"], [B_PT[pi]] + rb, [B_ps[ob]])
            state["pv_done"] = i + 1
            if st["last"]:
                attn_epilogue(blk, obank)

    def attn_epilogue(blk, obank):
        Pn, nqs, h, sub0 = blk["Pn"], blk["nqs"], blk["h"], blk["sub0"]
        rrs = []
        for j in range(nqs):
            o1, o2 = obank[j]
            rr, B_rr = sm("arr%d" % j, 20 + 4 * j, 4)
            rrs.append((rr, B_rr))
            S.add("dve", lambda hh, rr=rr, o1=o1: hh.reciprocal(rr[:Pn, 0:1], psum[o1][:Pn, 256:257]), [B_ps[o1]], [B_rr])
            S.add("dve", lambda hh, rr=rr, o2=o2: hh.reciprocal(rr[:Pn, 1:2], psum[o2][:Pn, 256:257]), [B_ps[o2]], [B_rr])
            tt("dve", rr[:Pn, 2:3], rr[:Pn, 1:2], lamw[:Pn, 2:3], ALU.mult, [B_rr, B_const], [B_rr])
        for j in range(nqs):
            o1, o2 = obank[j]
            rr, B_rr = rrs[j]
            act(otm[:Pn, j, :], psum[o1][:Pn, 0:256], AF.Copy, [B_ps[o1], B_rr], [B_otm[j]], scale=rr[:Pn, 0:1])
        for j in range(nqs):
            o1, o2 = obank[j]
            rr, B_rr = rrs[j]
            stt(otm[:Pn, j, :], psum[o2][:Pn, 0:256], rr[:Pn, 2:3], otm[:Pn, j, :], ALU.mult, ALU.add,
                [B_ps[o2], B_rr, B_otm[j]], [B_otm[j]])
        for j in range(nqs):
            s = sub0 + j
            ssq, B_ssq = sm("assq%d" % j, 28 + 2 * j, 1)
            rsq, B_rsq = sm("arsq%d" % j, 32 + 2 * j, 1)
            junk = otmp[j][:Pn, 0:256]
            act(junk, otm[:Pn, j, :], AF.Square, [B_otm[j]], [B_otmp[j], B_ssq], accum=ssq[:Pn, 0:1])
            rstd_from_ss(ssq[:Pn, 0:1], B_ssq, rsq[:Pn, 0:1], B_rsq, 1, 4.0 / 256, 4.0 * EPS, Pn)
            stt(om[:Pn, s, h * 256:(h + 1) * 256], otm[:Pn, j, :], rsq[:Pn, 0:1], sg[:Pn, s, h * 256:(h + 1) * 256],
                ALU.mult, ALU.mult, [B_otm[j], B_rsq, B_sg], [B_r2])

    attn_tile.sr = 0

    mst = [AB[:, 12304 + i * 2048:12304 + (i + 1) * 2048].bitcast(F32) for i in range(2)]
    B_mst = [Buf("mst0"), Buf("mst1")]
    msq = Ring([0, 1])

    def mem_kv_prompt(bi):
        S.add("pool", lambda h: h.memset(memV[:, 0, :, :, :, 128:130], 2.0), [], [B_mem[0]])
        for mc in range(2):
            dma(x_t[0][:, mc, :], memp[bi, mc * 128:(mc + 1) * 128, :], [], [B_x[0]], "xld")
        norm_to_hT(x_t[0], B_x[0], 2, 128)
        for l in range(2):
            for g in range(2):
                wt, wb = ws.get(ws_mem[l][:, :, g * 512:(g + 1) * 512], 8, 512)
                for mc in range(2):
                    b = proj_tm(wt, wb, 8, mc, 128, hT, B_r2)
                    i = msq.next()
                    copy_ev("act", mst[i][:, 0:512], psum[b][:, :], [B_ps[b]], [B_mst[i]])
                    dst = (mk_p if g == 0 else mv_p)[l, bi, mc * 128:(mc + 1) * 128, :]
                    dma(dst, mst[i][:, 0:512], [B_mst[i]], [], "mst%d" % i)
                    if g == 1:
                        copy_ev("pool", memV[:, 0, l, mc, :, 0:128],
                                mst[i][:, 0:512].rearrange("p (h e) -> p h e", h=4), [B_mst[i]], [B_mem[0]])
                if g == 0:
                    for hd in range(4):
                        b = proj_fm(wt, wb, 8, hd * 128, 256, hT, B_r2)
                        copy_ev(evq.next(), memKT[:, 0, l, hd, :], psum[b][:, 0:256], [B_ps[b]], [B_mem[0]])

    def mem_kv_sample():
        for sl in range(2):
            S.add("pool", lambda h, sl=sl: h.memset(memV[:, sl, :, :, :, 128:130], 2.0), [], [B_mem[sl]])
            for l in range(2):
                for mc in range(2):
                    i = msq.next()
                    dma(mst[i][:, 0:512], cmk_in[l, sl, mc * 128:(mc + 1) * 128, :], [], [B_mst[i]], "mst%d" % i)
                    bt = PS.next()
                    tv = psum[bt][:, :].rearrange("p (c t) -> p c t", c=4)
                    for hd in range(4):
                        tr(tv[:, hd, :], mst[i][:, hd * 128:(hd + 1) * 128], ident_f[:, :], [B_mst[i], B_const], [B_ps[bt]])
                    copy_ev("dve", memKT[:, sl, l, :, mc * 128:(mc + 1) * 128], tv[:, :, :], [B_ps[bt]], [B_mem[sl]])
                    dma(mst[i][:, 512:1024], cmv_in[l, sl, mc * 128:(mc + 1) * 128, :], [], [B_mst[i]], "mst%d" % i)
                    copy_ev("pool", memV[:, sl, l, mc, :, 0:128],
                            mst[i][:, 512:1024].rearrange("p (h e) -> p h e", h=4), [B_mst[i]], [B_mem[sl]])

    cvK_f = sb("cvK_f", [128, 1024], F32)
    cvV_f = cvK_f
    cvK_b = sb("cvK_b", [128, 8, 128], BF16)
    cvV_b = sb("cvV_b", [128, 4, 258], BF16)
    B_cvKf, B_cvKb, B_cvVb = Buf("cvKf"), Buf("cvKb"), Buf("cvVb")
    B_cvVf = B_cvKf
    conv_todo = []

    def conv_init():
        del conv_todo[:]
        for b in range(2):
            for kt in range(PAST // 128):
                conv_todo.append((b, kt))
        S.add("pool", lambda h: h.memset(cvV_b[:, :, 256:258], 1.0), [], [B_cvVb])

    sgm_f = sgm[:, :, :].rearrange("p s t -> p (s t)").bitcast(F32)
    qmT_f = qmT[:, :, :].rearrange("p s t -> p (s t)").bitcast(F32)
    CVST = {"c": (cvK_f, B_cvKf), "s": (sgm_f, B_sgm), "q": (qmT_f, B_qmT)}

    def cv_next():
        return conv_todo.pop(0) if conv_todo else None

    def cv_loadK(item, st):
        if item is None:
            return
        b, kt = item
        t_, b_ = CVST[st]
        dma(t_[:, :], ck_in[b, kt * 128:(kt + 1) * 128, :], [], [b_], "cvl")

    def cv_loadV(item, st):
        if item is None:
            return
        b, kt = item
        t_, b_ = CVST[st]
        dma(t_[:, :], cv_in[b, kt * 128:(kt + 1) * 128, :], [], [b_], "cvl")

    def cv_transK(item, st):
        if item is None:
            return
        b, kt = item
        t_, b_ = CVST[st]
        for half in range(2):
            bt = PS.next()
            tv = psum[bt][:, :].rearrange("p (c t) -> p c t", c=4)
            for blk in range(4):
                tr(tv[:, blk, :], t_[:, (half * 4 + blk) * 128:(half * 4 + blk + 1) * 128], ident_f[:, :],
                   [b_, B_const], [B_ps[bt]])
            copy_ev(evq.next(), cvK_b[:, half * 4:(half + 1) * 4, :], tv[:, :, :], [B_ps[bt]], [B_cvKb])
        S.add("sp", lambda h, d_=kts_sc[b, :, :, kt * 128:(kt + 1) * 128].rearrange("m p k -> p m k"):
              h.dma_start(out=d_, in_=cvK_b[:, :, :]), reads=[B_cvKb], pw=[B_ktss[b][kt // 4]], key="cvs")

    def cv_castV(item, st):
        if item is None:
            return
        b, kt = item
        t_, b_ = CVST[st]
        copy_ev("pool", cvV_b[:, :, 0:256], t_[:, :].rearrange("p (h e) -> p h e", h=4), [b_], [B_cvVb])
        S.add("sp", lambda h, d_=vs_sc[b, kt, :, :, :]: h.dma_start(out=d_, in_=cvV_b[:, :, :]),
              reads=[B_cvVb], pw=[B_vss[b][kt // 4]], key="cvs")

    def conv_some(n):
        for _ in range(n):
            if not conv_todo:
                return
            b, kt = conv_todo.pop(0)
            dma(cvK_f[:, :], ck_in[b, kt * 128:(kt + 1) * 128, :], [], [B_cvKf], "cvl")
            for half in range(2):
                bt = PS.next()
                tv = psum[bt][:, :].rearrange("p (c t) -> p c t", c=4)
                for blk in range(4):
                    tr(tv[:, blk, :], cvK_f[:, (half * 4 + blk) * 128:(half * 4 + blk + 1) * 128], ident_f[:, :],
                       [B_cvKf, B_const], [B_ps[bt]])
                copy_ev(evq.next(), cvK_b[:, half * 4:(half + 1) * 4, :], tv[:, :, :], [B_ps[bt]], [B_cvKb])
            dma(cvV_f[:, :], cv_in[b, kt * 128:(kt + 1) * 128, :], [], [B_cvVf], "cvl")
            copy_ev("pool", cvV_b[:, :, 0:256], cvV_f[:, :].rearrange("p (h e) -> p h e", h=4), [B_cvVf], [B_cvVb])
            S.add("sp", lambda h, d_=kts_sc[b, :, :, kt * 128:(kt + 1) * 128].rearrange("m p k -> p m k"):
                  h.dma_start(out=d_, in_=cvK_b[:, :, :]), reads=[B_cvKb], pw=[B_ktss[b][kt // 4]], key="cvs")
            S.add("sp", lambda h, d_=vs_sc[b, kt, :, :, :]: h.dma_start(out=d_, in_=cvV_b[:, :, :]),
                  reads=[B_cvVb], pw=[B_vss[b][kt // 4]], key="cvs")

    def final_store(x_ap, B_xb, nsub, Pn, dst):
        ss, B_ss = sm("fss", 40, 4)
        rs, B_rs = sm("frs", 44, 4)
        junk = otm[:, :, :].rearrange("p j e -> p (j e)").bitcast(BF16)
        for s in range(nsub):
            act(junk[:Pn, 0:D], x_ap[:Pn, s, :], AF.Square, [B_xb], B_otm + [B_ss], accum=ss[:Pn, s:s + 1])
        rstd_from_ss(ss[:Pn, 0:nsub], B_ss, rs[:Pn, 0:nsub], B_rs, nsub, 1.0 / D, EPS, Pn)
        for s in range(nsub):
            stt(x_ap[:Pn, s, :], x_ap[:Pn, s, :], rs[:Pn, s:s + 1], nf_b[:Pn, :], ALU.mult, ALU.mult,
                [B_xb, B_rs, B_const], [B_xb])
        dma(dst, x_ap[:Pn, 0:nsub, :], [B_xb], [], "yst")

    def program():
        PS.i = 0
        sbi[0] = 0
        attn_tile.sr = 0
        setup()
        conv_init()
        for bi in range(NBP):
            par = bi % 2
            mem_kv_prompt(bi)
            for h in range(4):
                S.add("pool", lambda hh, h=h: hh.memset(S_f[:, h, :, :], 0.0), [], [B_Sf[h]])
                S.add("pool", lambda hh, h=h, i=sbi[0] % 2: hh.memset(S_b[i][:, h, :, :], 0.0), [], [B_Sb[sbi[0] % 2][h]])
            xcur = 0
            for t in range(NTILE):
                tok0 = t * 512
                xt, bx = x_t[xcur], B_x[xcur]
                if t == 0 or not PIPE_NEXT:
                    dma(xt[:, :, :], xp[bi, tok0:tok0 + 512, :].rearrange("(s p) d -> p s d", p=128), [], [bx], "xld")
                    dma(rq_t[:, :, :], c_rq_p[:, :, tok0:tok0 + 512], [], [B_rope], "rope")
                    dma(rk_t[:, :, :], c_rk_p[:, :, tok0:tok0 + 512], [], [B_rope], "rope")
                    norm_to_hT(xt, bx, 4, 128)
                in_proj_A(4, 128)
                retention(4, 128, GL_P, 0, False)
                cvi = [None] * 4
                if do_sample and INLOOP_CONV:
                    cvi = [cv_next() for _ in range(4)]
                    cv_loadK(cvi[0], "c")
                mem_attn(0, 4, 128, lambda s: 0)
                cv_transK(cvi[0], "c")
                cv_loadV(cvi[0], "s")
                cv_loadK(cvi[1], "q")
                out_proj(ws_out_a, xt, bx, 4, 128)
                cv_castV(cvi[0], "s")
                cv_transK(cvi[1], "q")
                cv_loadV(cvi[1], "c")
                if t == NTILE - 1:
                    dma(st_p[bi].rearrange("h (c p) e -> p h c e", p=128), S_f[:, :, :, :], B_Sf, [], "stst")
                has_next = (t + 1 < NTILE) and PIPE_NEXT
                if has_next:
                    xnx, bxn = x_t[1 - xcur], B_x[1 - xcur]
                    dma(xnx[:, :, :], xp[bi, tok0 + 512:tok0 + 1024, :].rearrange("(s p) d -> p s d", p=128),
                        [], [bxn], "xld")
                    dma(rq_t[:, :, :], c_rq_p[:, :, tok0 + 512:tok0 + 1024], [], [B_rope], "rope")
                    dma(rk_t[:, :, :], c_rk_p[:, :, tok0 + 512:tok0 + 1024], [], [B_rope], "rope")
                norm_to_hT(xt, bx, 4, 128)
                cv_castV(cvi[1], "c")

                def kt_store(par=par, tok0=tok0, t=t):
                    dma(kt_sc[par][:, :, tok0:tok0 + 512].rearrange("m p k -> p m k"), KTn[:, :, :],
                        [B_KTn], [B_ktsc[par][t]], "ktn")

                def v_store(par=par, tok0=tok0, t=t):
                    dma(v_sc[par][tok0 // 128:tok0 // 128 + 4, :, :, :].rearrange("s p h e -> p s h e"),
                        Vn[:, :, :, :], [B_Vn], [B_vsc[par][t]], "vn")

                kv_proj(4, 128,
                        lambda s, bi=bi, tok0=tok0: k_p[bi, tok0 + s * 128:tok0 + (s + 1) * 128, :],
                        lambda s, bi=bi, tok0=tok0: v_p[bi, tok0 + s * 128:tok0 + (s + 1) * 128, :],
                        kt_store, v_store, None, None)
                in_proj_B(4, 128)
                mem_attn(1, 4, 128, lambda s: 0)
                cv_loadK(cvi[2], "c")
                cv_loadV(cvi[2], "s")
                cv_loadK(cvi[3], "q")
                blocks = []
                for qt in range(2):
                    q0 = tok0 + qt * 256
                    nkt_total = (q0 + 256) // 128
                    for h in range(4):
                        chunks = []
                        for cc in range((nkt_total + 3) // 4):
                            nkt = min(4, nkt_total - cc * 4)
                            masks = []
                            for kk in range(nkt):
                                kt_abs = cc * 4 + kk
                                masks.append(None if kt_abs * 128 < q0 else (kt_abs * 128 - q0) // 128)
                            ch = dict(nkt=nkt, kps=[128] * nkt, masks=masks)
                            if cc == t:
                                ch["direct"] = (
                                    lambda m, koff, kp, h=h: KTn[:, 2 * h + m, koff:koff + kp],
                                    lambda tt_, kp, h=h: Vn[:kp, tt_, h, 0:257],
                                    [B_KTn, B_Vn])
                            else:
                                ch["kt_src"] = kt_sc[par][2 * h:2 * h + 2, :, cc * 512:cc * 512 + nkt * 128].rearrange("m p k -> p m k")
                                ch["v_src"] = v_sc[par][cc * 4:cc * 4 + nkt, :, h, :].rearrange("t p e -> p t e")
                                ch["deps"] = [B_ktsc[par][cc], B_vsc[par][cc]]
                            chunks.append(ch)
                        blocks.append(dict(qc0=qt * 256, nqs=2, Pn=128, h=h, sub0=qt * 2, chunks=chunks))
                attn_tile(blocks)
                cv_transK(cvi[2], "c")
                cv_transK(cvi[3], "q")
                cv_castV(cvi[2], "s")
                if has_next:
                    norm_stats(xnx, bxn, 4, 128)
                out_proj(ws_out_b, xt, bx, 4, 128)
                cv_loadV(cvi[3], "c")
                if has_next:
                    norm_apply(xnx, bxn, 4, 128)
                final_store(xt, bx, 4, 128,
                            y_p[bi, tok0:tok0 + 512, :].rearrange("(s p) d -> p s d", p=128))
                cv_castV(cvi[3], "c")
                xcur = 1 - xcur
        if do_sample:
            sample_program()

    def sample_program():
        xt, bx = x_t[0], B_x[0]
        mem_kv_sample()
        conv_some(1000)
        dma(xt[:32, 0:2, :], xs.rearrange("s p d -> p s d"), [], [bx], "xld")
        dma(rq_t[:, :, 0:64], c_rq_s[:, :, :], [], [B_rope], "rope")
        dma(rk_t[:, :, 0:64], c_rk_s[:, :, :], [], [B_rope], "rope")
        norm_to_hT(xt, bx, 2, 32)
        in_proj_A(2, 32)
        retention(2, 32, GL_S, 4, True)
        mem_attn(0, 2, 32, lambda s: s)
        out_proj(ws_out_a, xt, bx, 2, 32)
        norm_to_hT(xt, bx, 2, 32)

        def kt_store():
            for b in range(2):
                dma(kts_sc[b, :, :, PAST:PAST + 32].rearrange("m p k -> p m k"), KTn[:, :, b * 32:(b + 1) * 32],
                    [B_KTn], [B_ktss[b][8]], "ktn")

        def v_store():
            for b in range(2):
                dma(vs_sc[b, 32, 0:32, :, :], Vn[:32, b, :, :], [B_Vn], [B_vss[b][8]], "vn")

        kv_proj(2, 32, lambda s: k_s[s, :, :], lambda s: v_s[s, :, :], lambda: None, lambda: None, None, None)
        in_proj_B(2, 32)
        mem_attn(1, 2, 32, lambda s: s)
        blocks = []
        for b in range(2):
            for h in range(4):
                chunks = []
                for cc in range(8):
                    chunks.append(dict(
                        kt_src=kts_sc[b, 2 * h:2 * h + 2, :, cc * 512:cc * 512 + 512].rearrange("m p k -> p m k"),
                        v_src=vs_sc[b, cc * 4:cc * 4 + 4, :, h, :].rearrange("t p e -> p t e"),
                        nkt=4, kps=[128] * 4, masks=[None] * 4, deps=[B_ktss[b][cc], B_vss[b][cc]]))
                chunks.append(dict(nkt=1, kps=[32], masks=[None], direct=(
                    lambda m, koff, kp, h=h, b=b: KTn[:, 2 * h + m, b * 32:b * 32 + 32],
                    lambda tt_, kp, h=h, b=b: Vn[:32, b, h, 0:257],
                    [B_KTn, B_Vn])))
                blocks.append(dict(qc0=b * 32, nqs=1, Pn=32, h=h, sub0=b, chunks=chunks))
        attn_tile(blocks)
        out_proj(ws_out_b, xt, bx, 2, 32)
        final_store(xt, bx, 2, 32, y_s.rearrange("s p d -> p s d"))

    ab_ranges = [(B_kT, 0, 4096), (B_vtok, 4096, 8192), (B_KTn, 4096, 8192), (B_Vn, 8192, 8192 + 4128)]
    ab_ranges += [(B_kttok[i], 8192 + i * 1024, 8192 + (i + 1) * 1024) for i in range(4)]
    ab_ranges += [(B_ropetmp[i], 12288 + i * 1024, 12288 + (i + 1) * 1024) for i in range(4)]
    ab_ranges += [(B_scT[i], 16384 + i * 512, 16384 + (i + 1) * 512) for i in range(4)]
    ab_ranges += [(B_stage[i], i * 1024, (i + 1) * 1024) for i in range(4)]
    ab_ranges += [(B_PT[i], 12320 + i * 512, 12320 + (i + 1) * 512) for i in range(4)]
    ab_ranges += [(B_KTc[i], 14368 + i * 1024, 14368 + (i + 1) * 1024) for i in range(3)]
    ab_ranges += [(B_Vc[i], 17440 + i * 1032, 17440 + (i + 1) * 1032) for i in range(3)]
    ab_ranges += [(B_otmp[i], 20536 + i * 488, 20536 + (i + 1) * 488) for i in range(2)]
    ab_ranges += [(prep_ld[i][1], i * 4096, (i + 1) * 4096) for i in range(3)]
    ab_ranges += [(prep_cv[i][1], 12288 + i * 2056, 12288 + (i + 1) * 2056) for i in range(3)]
    ab_ranges += [(B_mst[i], 12304 + i * 2048, 12304 + (i + 1) * 2048) for i in range(2)]
    for i in range(len(ab_ranges)):
        for j in range(i + 1, len(ab_ranges)):
            bi_, lo_i, hi_i = ab_ranges[i]
            bj_, lo_j, hi_j = ab_ranges[j]
            if lo_i < hi_j and lo_j < hi_i:
                bi_.overlaps.append(bj_)
                bj_.overlaps.append(bi_)

    S.dry = True
    ws.recording = True
    program()
    S.dry = False
    ws.recording = False
    program()
    S.emit(nc, es)
    es.close()
    return nc, consts


_CACHE = {}


def _get_prog(NBP, T, do_sample=True):
    key = (NBP, T, do_sample)
    if key not in _CACHE:
        _CACHE[key] = build(NBP, T, do_sample)
    return _CACHE[key]


def make_in_maps(inp, consts, NBP, T, pb_of_core):
    f = np.ascontiguousarray
    gains = np.stack([inp["norm_a"][0], inp["norm_kv"], inp["norm_b"][0], inp["norm_mem"][0],
                      inp["norm_mem"][1], inp["norm_f"]]).astype(np.float32)
    shared = dict(
        w_in_a=f(inp["w_in_a"][0]), w_out_a=f(inp["w_out_a"][0]), w_kv=f(inp["w_kv"]),
        w_in_b=f(inp["w_in_b"][0]), w_out_b=f(inp["w_out_b"][0]), w_mem=f(inp["w_mem_kv"]),
        gains=f(gains), subln=f(inp["subln_b"][0]), lam_in=f(inp["diff_lambda"][0].reshape(512)),
        c_rq_p=consts["rq_p"], c_rk_p=consts["rk_p"], c_rq_s=consts["rq_s"], c_rk_s=consts["rk_s"],
        c_dmask=consts["dmask"], c_kw=consts["kw"], c_cmask=consts["cmask"], c_ident=consts["ident"])
    maps = []
    for c in range(NCORES):
        pb = pb_of_core(c)
        m = dict(shared)
        m["xp"] = f(inp["x_prompt"][pb, :T])
        m["memp"] = f(inp["mem_prompt"][pb])
        sbs = [2 * c, 2 * c + 1]
        m["xs"] = f(inp["x_sample"][sbs])
        m["st_in"] = f(inp["state_ret"][0, sbs])
        m["ck_in"] = f(inp["cache_k"][sbs].reshape(2, PAST, D))
        m["cv_in"] = f(inp["cache_v"][sbs].reshape(2, PAST, D))
        m["cmk_in"] = f(inp["cache_mem_k"][:, sbs].reshape(2, 2, 256, 512))
        m["cmv_in"] = f(inp["cache_mem_v"][:, sbs].reshape(2, 2, 256, 512))
        maps.append(m)
    return maps


def kernel(**inp):
    inp = {k: np.asarray(v) for k, v in inp.items()}
    NBP, T = 4, SEQ
    nc, consts = _get_prog(NBP, T)
    maps = make_in_maps(inp, consts, NBP, T, lambda c: list(range(4 * c, 4 * c + 4)))
    res = run_bass_kernel_spmd(nc, maps, core_ids=list(range(NCORES))).results
    B = 32
    y_p = np.concatenate([r["y_p"] for r in res], 0)
    y_s = np.concatenate([r["y_s"] for r in res], 0)
    st_p = np.concatenate([r["st_p"] for r in res], 0)[None]
    k_p = np.concatenate([r["k_p"] for r in res], 0).reshape(B, T, 4, 2, 128)
    v_p = np.concatenate([r["v_p"] for r in res], 0).reshape(B, T, 4, 256)
    mk_p = np.concatenate([r["mk_p"] for r in res], 1).reshape(2, B, 256, 4, 128)
    mv_p = np.concatenate([r["mv_p"] for r in res], 1).reshape(2, B, 256, 4, 128)
    st_s = np.concatenate([r["st_s"] for r in res], 0)[None]
    k_s = np.concatenate([r["k_s"] for r in res], 0).reshape(16, 32, 4, 2, 128)
    v_s = np.concatenate([r["v_s"] for r in res], 0).reshape(16, 32, 4, 256)
    return (y_p, y_s, st_p, k_p, v_p, mk_p, mv_p, st_s, k_s, v_s)
```

```python
import math
from contextlib import ExitStack

import numpy as np
import ml_dtypes

import concourse.bass as bass
import concourse.mybir as mybir
from concourse.bass_utils import run_bass_kernel_spmd

F32 = mybir.dt.float32
BF16 = mybir.dt.bfloat16
AF = mybir.ActivationFunctionType
ALU = mybir.AluOpType

D = 1024
SEQ = 2048
NCORES = 8
PAST = 4096
EPS = 1e-6
LAM_INIT = 0.8 - 0.6 * math.exp(-0.3 * 1)
SC_ATT = 128 ** -0.5
NKT_S = 33
INLOOP_CONV = True
MAX_DMA_INFLIGHT = 16
PIPE_NEXT = True
CONV_AT = (0, 4, 0, 0)


class Buf:
    __slots__ = ("name", "writers", "readers", "overlaps")

    def __init__(self, name):
        self.name = name
        self.writers = {}
        self.readers = {}
        self.overlaps = []


def link(a_list, b_list):
    for a in a_list:
        for b in b_list:
            a.overlaps.append(b)
            b.overlaps.append(a)


class Op:
    __slots__ = ("eng", "fn", "deps", "raw", "is_dma", "key", "dcount", "needs_inc", "inc_val", "pos")

    def __init__(self, eng, fn, is_dma, key):
        self.eng = eng
        self.fn = fn
        self.deps = set()
        self.raw = set()
        self.is_dma = is_dma
        self.key = key
        self.dcount = 0
        self.needs_inc = False
        self.inc_val = 0
        self.pos = 0


class Sched:
    ENGS = ("pe", "act", "dve", "pool", "sp")

    def __init__(self):
        self.ops = {e: [] for e in self.ENGS}
        self.last_dma = {}
        self.dma_hist = []
        self.dma_counts = {}
        self.dry = False
        self.n = 0

    def add(self, eng, fn, reads=(), writes=(), key=None, pw=()):
        if self.dry:
            return None
        is_dma = key is not None
        op = Op(eng, fn, is_dma, key)
        self.n += 1
        op.pos = self.n
        rset = []
        for b in reads:
            rset.append(b)
            rset.extend(b.overlaps)
        wset = []
        for b in writes:
            wset.append(b)
            wset.extend(b.overlaps)
        pset = []
        for b in pw:
            pset.append(b)
            wset.extend(b.overlaps)
        for b in rset:
            for w in b.writers.values():
                op.deps.add(w)
                op.raw.add(w)
        for b in wset:
            for w in b.writers.values():
                op.deps.add(w)
            for r in b.readers.values():
                op.deps.add(r)
        for b in pset:
            for r in b.readers.values():
                op.deps.add(r)
        if is_dma:
            self.dma_hist.append(op)
            if len(self.dma_hist) > MAX_DMA_INFLIGHT:
                old_ = self.dma_hist[-1 - MAX_DMA_INFLIGHT]
                op.deps.add(old_)
            prev = self.last_dma.get(key)
            if prev is not None:
                op.deps.add(prev)
                op.raw.add(prev)
            self.last_dma[key] = op
            self.dma_counts[key] = self.dma_counts.get(key, 0) + 1
            op.dcount = self.dma_counts[key]
        rk = ("dma", key) if is_dma else eng
        for b in reads:
            b.readers[rk] = op
        for b in writes:
            b.writers = {rk: op}
            b.readers = {}
        for b in pw:
            b.writers[rk] = op
        op.deps.discard(op)
        self.ops[eng].append(op)
        return op

    def emit(self, nc, es):
        for e in self.ENGS:
            for op in self.ops[e]:
                need = set()
                for d in op.deps:
                    if d.is_dma:
                        need.add(d)
                    elif d.eng == op.eng and not op.is_dma:
                        if op.eng != "pe":
                            need.add(d)
                    else:
                        need.add(d)
                op.deps = need
                for d in need:
                    if not d.is_dma:
                        d.needs_inc = True
        sems = {}
        for e in self.ENGS:
            sems[e] = es.enter_context(nc.semaphore("sem_" + e))
            c = 0
            for op in self.ops[e]:
                if op.needs_inc and not op.is_dma:
                    c += 1
                    op.inc_val = c
        for k in self.dma_counts:
            sems[("dma", k)] = es.enter_context(nc.semaphore("semd_%s" % (str(k),)))
        self.nsem = len(sems)
        handles = {"pe": nc.tensor, "act": nc.scalar, "dve": nc.vector, "pool": nc.gpsimd, "sp": nc.sync}
        block = es.enter_context(nc.Block())
        sect = {"pe": block.tensor, "act": block.scalar, "dve": block.vector, "pool": block.gpsimd,
                "sp": block.sync}

        def make(e):
            def body(_eng):
                h = handles[e]
                known = {}
                for op in self.ops[e]:
                    waits = {}
                    for d in op.deps:
                        if d.is_dma:
                            sk = ("dma", d.key)
                            v = d.dcount * 16
                        else:
                            sk = d.eng
                            v = d.inc_val
                        if known.get(sk, 0) >= v:
                            continue
                        if waits.get(sk, 0) < v:
                            waits[sk] = v
                    for sk, v in waits.items():
                        h.wait_ge(sems[sk], v)
                        known[sk] = v
                    inst = op.fn(h)
                    if op.is_dma:
                        inst.then_inc(sems[("dma", op.key)], 16)
                    elif op.needs_inc:
                        inst.then_inc(sems[e], 1)
                if e == "sp":
                    for k, op in self.last_dma.items():
                        sk = ("dma", k)
                        v = op.dcount * 16
                        if known.get(sk, 0) < v:
                            h.wait_ge(sems[sk], v)
            return body

        for e in self.ENGS:
            sect[e](make(e))


class Ring:
    def __init__(self, items):
        self.items = items
        self.i = 0

    def next(self):
        it = self.items[self.i % len(self.items)]
        self.i += 1
        return it


class WStream:
    def __init__(self, S, slots, depth):
        self.S = S
        self.slots = slots
        self.depth = depth
        self.plan = []
        self.i = 0
        self.issued = 0
        self.recording = True

    def _issue(self, j):
        src, nk, nco = self.plan[j]
        t, b = self.slots[j % len(self.slots)]
        dst = t[:, 0:nk, 0:nco]
        self.S.add("sp", lambda h, dst=dst, src=src: h.dma_start(out=dst, in_=src),
                   reads=[WSC_BUF], writes=[b], key="w%d" % (j % len(self.slots)))

    def get(self, src, nk, nco):
        if self.recording:
            self.plan.append((src, nk, nco))
            return self.slots[0]
        j = self.i
        self.i += 1
        while self.issued < min(len(self.plan), j + self.depth + 1):
            self._issue(self.issued)
            self.issued += 1
        return self.slots[j % len(self.slots)]


WSC_BUF = Buf("wscratch")


def _consts(T):
    half = 128
    inv = (10000.0 ** (-np.arange(half, dtype=np.float32) / np.float32(half))).astype(np.float32)
    g = 1.0 - 2.0 ** (-5.0 - np.arange(4, dtype=np.float64))
    logg = np.log(g)

    def tables(pos, period):
        ang = (pos.astype(np.float32)[None, :] * inv[:, None]).astype(np.float32)
        c = np.cos(ang.astype(np.float64))
        s = np.sin(ang.astype(np.float64))
        l = (np.arange(len(pos)) % period).astype(np.float64)
        rq = np.zeros((128, 8, len(pos)), np.float64)
        for h in range(4):
            dec = np.exp(logg[h] * (l + 1.0))
            rq[:, 2 * h, :] = c * dec[None, :]
            rq[:, 2 * h + 1, :] = s * dec[None, :]
        rk = np.stack([c, s], axis=1)
        return rq.astype(ml_dtypes.bfloat16), rk.astype(ml_dtypes.bfloat16)

    rq_p, rk_p = tables(np.arange(T), 128)
    rq_s, rk_s = tables(np.concatenate([PAST + np.arange(32)] * 2), 32)
    m = np.arange(128)[:, None].astype(np.float64)
    l = np.arange(128)[None, :].astype(np.float64)
    dmask = np.zeros((128, 4, 128), np.float32)
    kw = np.zeros((128, 8), np.float32)
    for h in range(4):
        dmask[:, h, :] = np.where(l >= m, np.exp(-logg[h] * (m + 1.0)) / 16.0, 0.0)
        kw[:, h] = np.exp(logg[h] * (127.0 - np.arange(128))) / 16.0
        kw[:32, 4 + h] = np.exp(logg[h] * (31.0 - np.arange(32))) / 16.0
    gl_p = [float(np.exp(logg[h] * 128.0)) for h in range(4)]
    gl_s = [float(np.exp(logg[h] * 32.0)) for h in range(4)]
    cm = np.zeros((128, 2, 2, 256), np.float32)
    kk = np.arange(128)[:, None]
    qq = np.arange(256)[None, :]
    cm[:, 0, :, :] = ((kk // 64) <= (qq // 64))[:, None, :]
    cm[:, 1, :, :] = ((2 + kk // 64) <= (qq // 64))[:, None, :]
    ident = np.eye(128, dtype=np.float32)
    return dict(rq_p=rq_p, rk_p=rk_p, rq_s=rq_s, rk_s=rk_s, dmask=dmask, kw=kw,
                cmask=cm.astype(ml_dtypes.bfloat16), ident=ident), gl_p, gl_s


def build(NBP=4, T=SEQ, do_sample=True):
    assert T % 512 == 0
    NTILE = T // 512
    consts, GL_P, GL_S = _consts(T)
    nc = bass.Bass("TRN2", target_bir_lowering=False)
    es = ExitStack()

    def din(name, shape, dt=F32):
        return nc.dram_tensor(name, list(shape), dt, kind="ExternalInput").ap()

    def dout(name, shape, dt=F32):
        return nc.dram_tensor(name, list(shape), dt, kind="ExternalOutput").ap()

    def dscr(name, shape, dt=BF16):
        return nc.dram_tensor(name, list(shape), dt, kind="Internal").ap()

    xp = din("xp", [NBP, T, D])
    xs = din("xs", [2, 32, D])
    st_in = din("st_in", [2, 4, 256, 256])
    ck_in = din("ck_in", [2, PAST, D])
    cv_in = din("cv_in", [2, PAST, D])
    cmk_in = din("cmk_in", [2, 2, 256, 512])
    cmv_in = din("cmv_in", [2, 2, 256, 512])
    memp = din("memp", [NBP, 256, D])
    w_in_a = din("w_in_a", [D, 5120])
    w_out_a = din("w_out_a", [1536, D])
    w_kv = din("w_kv", [D, 2048])
    w_in_b = din("w_in_b", [D, 3072])
    w_out_b = din("w_out_b", [1536, D])
    w_mem = din("w_mem", [2, D, D])
    gains_in = din("gains", [6, D])
    subln_in = din("subln", [256])
    lam_in = din("lam_in", [512])
    c_rq_p = din("c_rq_p", [128, 8, T], BF16)
    c_rk_p = din("c_rk_p", [128, 2, T], BF16)
    c_rq_s = din("c_rq_s", [128, 8, 64], BF16)
    c_rk_s = din("c_rk_s", [128, 2, 64], BF16)
    c_dmask = din("c_dmask", [128, 4, 128])
    c_kw = din("c_kw", [128, 8])
    c_cmask = din("c_cmask", [128, 2, 2, 256], BF16)
    c_ident = din("c_ident", [128, 128])
    y_p = dout("y_p", [NBP, T, D])
    y_s = dout("y_s", [2, 32, D])
    st_p = dout("st_p", [NBP, 4, 256, 256])
    k_p = dout("k_p", [NBP, T, D])
    v_p = dout("v_p", [NBP, T, D])
    mk_p = dout("mk_p", [2, NBP, 256, 512])
    mv_p = dout("mv_p", [2, NBP, 256, 512])
    st_s = dout("st_s", [2, 4, 256, 256])
    k_s = dout("k_s", [2, 32, D])
    v_s = dout("v_s", [2, 32, D])
    ws_in_a = dscr("ws_in_a", [128, 8, 5120])
    ws_out_a = dscr("ws_out_a", [128, 12, 1024])
    ws_kv = dscr("ws_kv", [128, 8, 2048])
    ws_in_b = dscr("ws_in_b", [128, 8, 3072])
    ws_out_b = dscr("ws_out_b", [128, 12, 1024])
    ws_mem = dscr("ws_mem", [2, 128, 8, 1024])
    kt_sc = [dscr("kt_sc%d" % i, [8, 128, T]) for i in range(2)]
    v_sc = [dscr("v_sc%d" % i, [T // 128, 128, 4, 258]) for i in range(2)]
    kts_sc = dscr("kts_sc", [2, 8, 128, NKT_S * 128])
    vs_sc = dscr("vs_sc", [2, NKT_S, 128, 4, 258])

    S = Sched()

    def sb(name, shape, dt):
        return es.enter_context(nc.sbuf_tensor(name, list(shape), dt))

    ident_f = sb("ident_f", [128, 128], F32)
    ident_b = sb("ident_b", [128, 128], BF16)
    dmask = sb("dmask", [128, 4, 128], F32)
    kwt = sb("kwt", [128, 8], F32)
    cmask = sb("cmask", [128, 2, 2, 256], BF16)
    gains = sb("gains_t", [128, 6, 8], F32)
    subln = sb("subln_t", [128, 2], F32)
    nf_b = sb("nf_b", [128, D], F32)
    lamw = sb("lamw", [128, 8], F32)
    mhalf = sb("mhalf", [128, 8], F32)
    B_const = Buf("const")

    x_t = [sb("x%d" % i, [128, 4, D], F32) for i in range(2)]
    B_x = [Buf("x%d" % i) for i in range(2)]
    r1 = sb("r1", [128, 12 * 512], BF16)
    xn = r1[:, 0:4096].rearrange("p (s f) -> p s f", s=4)
    omT = r1[:, :].rearrange("p (c t) -> p c t", c=12)
    B_r1 = Buf("r1")
    lamt = r1[:, 0:1024].bitcast(F32)
    r2 = sb("r2", [128, 4 * 1536], BF16)
    hT = r2[:, 0:4096].rearrange("p (c t) -> p c t", c=8)
    om = r2[:, :].rearrange("p (s f) -> p s f", s=4)
    B_r2 = Buf("r2")
    qT = sb("qT", [128, 8, 512], BF16)
    B_qT = Buf("qT")
    sg = sb("sg", [128, 4, D], BF16)
    B_sg = Buf("sg")
    qmT = sb("qmT", [128, 4, 512], BF16)
    B_qmT = Buf("qmT")
    sgm = sb("sgm", [128, 4, 512], BF16)
    B_sgm = Buf("sgm")
    rq_t = sb("rq_t", [128, 8, 512], BF16)
    rk_t = sb("rk_t", [128, 2, 512], BF16)
    B_rope = Buf("rope")
    S_f = sb("S_f", [128, 4, 2, 256], F32)
    B_Sf = [Buf("Sf%d" % h) for h in range(4)]
    _sbt = sb("S_b", [128, 4, 2, 256], BF16)
    S_b = [_sbt, _sbt]
    _bsb = [Buf("Sb_%d" % h) for h in range(4)]
    B_Sb = [_bsb, _bsb]
    memKT = sb("memKT", [128, 2, 2, 4, 256], BF16)
    memV = sb("memV", [128, 2, 2, 2, 4, 130], BF16)
    B_mem = [Buf("mem%d" % i) for i in range(2)]
    wsl = [(sb("wsl%d" % i, [128, 8, 512], BF16), Buf("wsl%d" % i)) for i in range(3)]
    mPT = [(sb("mPT%d" % i, [128, 2, 128], BF16), Buf("mPT%d" % i)) for i in range(4)]
    small = sb("small", [128, 64], F32)
    B_small = {}

    def sm(name, c0, n):
        if name not in B_small:
            B_small[name] = Buf("sm_" + name)
        return small[:, c0:c0 + n], B_small[name]

    AB = sb("AB", [128, 21512], BF16)
    kT = AB[:, 0:4096].rearrange("p (c t) -> p c t", c=8)
    vtok = AB[:, 4096:8192].rearrange("p (s f) -> p s f", s=4)
    kt_tok = [AB[:, 8192 + i * 1024: 8192 + (i + 1) * 1024] for i in range(4)]
    ropetmp = [AB[:, 12288 + i * 1024: 12288 + (i + 1) * 1024].bitcast(F32) for i in range(4)]
    scT = [AB[:, 16384 + i * 512: 16384 + (i + 1) * 512].rearrange("p (h l) -> p h l", h=4) for i in range(4)]
    B_kT, B_vtok = Buf("kT"), Buf("vtok")
    B_kttok = [Buf("kttok%d" % i) for i in range(4)]
    B_ropetmp = [Buf("ropetmp%d" % i) for i in range(4)]
    B_scT = [Buf("scT%d" % i) for i in range(4)]
    A_bufs = [B_kT, B_vtok] + B_kttok + B_ropetmp + B_scT
    stage = [AB[:, i * 1024:(i + 1) * 1024].bitcast(F32) for i in range(4)]
    KTn = AB[:, 4096:8192].rearrange("p (c t) -> p c t", c=8)
    Vn = AB[:, 8192:8192 + 4128].rearrange("p (s h e) -> p s h e", s=4, h=4)
    PT = [AB[:, 12320 + i * 512: 12320 + (i + 1) * 512].rearrange("p (m q) -> p m q", m=2) for i in range(4)]
    KTc = [AB[:, 14368 + i * 1024: 14368 + (i + 1) * 1024].rearrange("p (m k) -> p m k", m=2) for i in range(3)]
    Vc = [AB[:, 17440 + i * 1032: 17440 + (i + 1) * 1032].rearrange("p (t e) -> p t e", t=4) for i in range(3)]
    otmp = [AB[:, 20536 + i * 488: 20536 + i * 488 + 488] for i in range(2)]
    B_stage = [Buf("stage%d" % i) for i in range(4)]
    B_KTn, B_Vn = Buf("KTn"), Buf("Vn")
    B_PT = [Buf("PT%d" % i) for i in range(4)]
    B_KTc = [Buf("KTc%d" % i) for i in range(3)]
    B_Vc = [Buf("Vc%d" % i) for i in range(3)]
    B_otmp = [Buf("otmp0"), Buf("otmp1")]
    B_bufs = B_stage + [B_KTn, B_Vn] + B_PT + B_KTc + B_Vc + B_otmp
    otm = sb("otm", [128, 2, 256], F32)
    B_otm = [Buf("otm0"), Buf("otm1")]

    psum = [es.enter_context(nc.psum_tensor("ps%d" % i, [128, 512], F32)) for i in range(8)]
    B_ps = [Buf("ps%d" % i) for i in range(8)]
    PS = Ring(list(range(8)))

    def ps_f32(i):
        return psum[i]

    def ps_bf(i):
        return psum[i][:, :].bitcast(BF16)

    ws = WStream(S, wsl, depth=2)
    evq = Ring(["dve", "act"])

    B_ktsc = [[Buf("ktsc%d_%d" % (i, t)) for t in range(NTILE)] for i in range(2)]
    B_vsc = [[Buf("vsc%d_%d" % (i, t)) for t in range(NTILE)] for i in range(2)]
    B_ktss = [[Buf("ktss%d_%d" % (b, c)) for c in range(9)] for b in range(2)]
    B_vss = [[Buf("vss%d_%d" % (b, c)) for c in range(9)] for b in range(2)]
    B_out = Buf("outputs")

    def dma(dst, src, reads, writes, key):
        S.add("sp", lambda h: h.dma_start(out=dst, in_=src), reads=reads, writes=writes, key=key)

    def dma_nc(dst, src, reads, writes, key):
        def f(h):
            with nc.allow_non_contiguous_dma(reason="small constant layout"):
                return h.dma_start(out=dst, in_=src)
        S.add("sp", f, reads=reads, writes=writes, key=key)

    def mm(out, lhsT, rhs, start, stop, reads, writes):
        S.add("pe", lambda h: h.matmul(out, lhsT, rhs, start=start, stop=stop), reads=reads, writes=writes)

    def tr(out, in_, ident, reads, writes):
        S.add("pe", lambda h: h.transpose(out, in_, ident), reads=reads, writes=writes)

    def act(out, in_, func, reads, writes, scale=None, bias=None, accum=None):
        kw = {}
        if scale is not None:
            kw["scale"] = scale
        if bias is not None:
            kw["bias"] = bias
        if accum is not None:
            kw["accum_out"] = accum
        S.add("act", lambda h: h.activation(out, in_, func, **kw), reads=reads, writes=writes)

    def copy_ev(eng, out, in_, reads, writes):
        if eng == "act":
            act(out, in_, AF.Copy, reads, writes)
        else:
            S.add(eng, lambda h: h.tensor_copy(out, in_), reads=reads, writes=writes)

    def tt(eng, out, a, b, op, reads, writes):
        S.add(eng, lambda h: h.tensor_tensor(out, a, b, op), reads=reads, writes=writes)

    def ts(eng, out, a, s1, s2, op0, op1, reads, writes):
        if op1 is None:
            S.add(eng, lambda h: h.tensor_scalar(out, a, s1, None, op0), reads=reads, writes=writes)
        else:
            S.add(eng, lambda h: h.tensor_scalar(out, a, s1, s2, op0, op1), reads=reads, writes=writes)

    def stt(out, a, sc, b, op0, op1, reads, writes):
        S.add("dve", lambda h: h.scalar_tensor_tensor(out, a, sc, b, op0, op1), reads=reads, writes=writes)

    def rstd_from_ss(ss_ap, B_ss, out_ap, B_o, n, mult, epsv, Pn):
        tmp, B_tmp = sm("rs_tmp", 56, 4)
        ts("pool", tmp[:Pn, 0:n], ss_ap, mult, epsv, ALU.mult, ALU.add, [B_ss], [B_tmp])
        tt("pool", out_ap, tmp[:Pn, 0:n], mhalf[:Pn, 0:n], ALU.pow, [B_tmp, B_const], [B_o])

    prep_ld = [(AB[:, i * 4096:(i + 1) * 4096].bitcast(F32), Buf("wld%d" % i)) for i in range(3)]
    prep_cv = [(AB[:, 12288 + i * 2056:12288 + (i + 1) * 2056], Buf("wcv%d" % i)) for i in range(3)]

    def setup():
        dma(ident_f[:, :], c_ident[:, :], [], [B_const], "c0")
        dma(dmask[:, :, :], c_dmask[:, :, :], [], [B_const], "c0")
        dma(kwt[:, :], c_kw[:, :], [], [B_const], "c0")
        dma(cmask[:, :, :, :], c_cmask[:, :, :, :], [], [B_const], "c0")
        dma_nc(gains[:, :, :], gains_in.rearrange("g (k p) -> p g k", p=128), [], [B_const], "c0")
        dma_nc(subln[:, :], subln_in.rearrange("(k p) -> p k", p=128), [], [B_const], "c0")
        dma(nf_b[:, :], gains_in[5:6, :].to_broadcast([128, D]), [], [B_const], "c0")
        dma(lamt[:, :], lam_in.rearrange("(o n) -> o n", o=1).to_broadcast([128, 512]), [], [B_r1], "c0")
        S.add("pool", lambda h: h.memset(mhalf[:, :], -0.5), [], [B_const])
        S.add("dve", lambda h: h.tensor_copy(ident_b[:, :], ident_f[:, :]), [B_const], [B_const])
        ts("dve", subln[:, :], subln[:, :], 1.0 - LAM_INIT, None, ALU.mult, None, [B_const], [B_const])
        tt("dve", lamt[:, 0:128], lamt[:, 0:128], lamt[:, 128:256], ALU.mult, [B_r1], [B_r1])
        tt("dve", lamt[:, 256:384], lamt[:, 256:384], lamt[:, 384:512], ALU.mult, [B_r1], [B_r1])
        S.add("dve", lambda h: h.tensor_reduce(lamw[:, 0:1], lamt[:, 0:128], mybir.AxisListType.X, ALU.add),
              [B_r1], [B_const])
        S.add("dve", lambda h: h.tensor_reduce(lamw[:, 1:2], lamt[:, 256:384], mybir.AxisListType.X, ALU.add),
              [B_r1], [B_const])
        act(lamw[:, 0:2], lamw[:, 0:2], AF.Exp, [B_const], [B_const])
        tt("dve", lamw[:, 2:3], lamw[:, 1:2], lamw[:, 0:1], ALU.subtract, [B_const], [B_const])
        ts("dve", lamw[:, 2:3], lamw[:, 2:3], -LAM_INIT, None, ALU.add, None, [B_const], [B_const])

        xl, xc = [], []
        for i in range(0):
            for j in range(2):
                bb = Buf("xpl%d_%d" % (i, j))
                bb.overlaps.append(B_x[i])
                B_x[i].overlaps.append(bb)
                xl.append((x_t[i][:, 2 * j:2 * j + 2, :].rearrange("p s d -> p (s d)"), bb))
        for k, (rt, rb) in enumerate(()):
            for j in range(2):
                bb = Buf("rcv%d_%d" % (k, j))
                bb.overlaps.append(rb)
                rb.overlaps.append(bb)
                xc.append((rt[:, j * 2048:(j + 1) * 2048], bb))
        NSL = 3
        ldr = Ring(prep_ld)
        cvr = Ring(prep_cv)
        engs = Ring(["dve", "act"])
        ldk = Ring(["pl%d" % i for i in range(NSL)])
        stk = Ring(["ps%d" % i for i in range(NSL)])

        def prep(wsrc, R, C, dst, gain_fn):
            for kc in range(R // 128):
                for c0 in range(0, C, 2048):
                    cw = min(2048, C - c0)
                    (lt, lb), (ct, cb) = ldr.next(), cvr.next()
                    dma(lt[:, 0:cw], wsrc[kc * 128:(kc + 1) * 128, c0:c0 + cw], [], [lb], ldk.next())
                    gsc = gain_fn(kc)
                    e = engs.next()
                    if gsc is None:
                        copy_ev(e, ct[:, 0:cw], lt[:, 0:cw], [lb], [cb])
                    elif e == "act":
                        act(ct[:, 0:cw], lt[:, 0:cw], AF.Copy, [lb, B_const], [cb], scale=gsc)
                    else:
                        ts(e, ct[:, 0:cw], lt[:, 0:cw], gsc, None, ALU.mult, None, [lb, B_const], [cb])
                    S.add("act", lambda h, d_=dst[:, kc, c0:c0 + cw], s_=ct[:, 0:cw]: h.dma_start(out=d_, in_=s_),
                          reads=[cb], pw=[WSC_BUF], key=stk.next())

        prep(w_in_a, D, 5120, ws_in_a, lambda kc: gains[:, 0, kc:kc + 1])
        prep(w_out_a, 1536, D, ws_out_a, lambda kc: None)
        prep(w_kv, D, 2048, ws_kv, lambda kc: gains[:, 1, kc:kc + 1])
        prep(w_in_b, D, 3072, ws_in_b, lambda kc: gains[:, 2, kc:kc + 1])
        prep(w_out_b, 1536, D, ws_out_b, lambda kc: subln[:, (kc % 2):(kc % 2) + 1] if kc < 8 else None)
        for l in range(2):
            prep(w_mem[l], D, D, ws_mem[l], lambda kc, l=l: gains[:, 3 + l, kc:kc + 1])

    def norm_stats(x_ap, B_xb, nsub, Pn):
        ss, B_ss = sm("ss", 0, 4)
        rs, B_rs = sm("rs", 4, 4)
        junk = otm[:, :, :].rearrange("p j e -> p (j e)").bitcast(BF16)
        for s in range(nsub):
            act(junk[:Pn, 0:D], x_ap[:Pn, s, :], AF.Square, [B_xb], B_otm + [B_ss], accum=ss[:Pn, s:s + 1])
        rstd_from_ss(ss[:Pn, 0:nsub], B_ss, rs[:Pn, 0:nsub], B_rs, nsub, 1.0 / D, EPS, Pn)

    def norm_apply(x_ap, B_xb, nsub, Pn):
        rs, B_rs = sm("rs", 4, 4)
        for s in range(nsub):
            ts("dve", xn[:Pn, s, :], x_ap[:Pn, s, :], rs[:Pn, s:s + 1], None, ALU.mult, None,
               [B_xb, B_rs], [B_r1])
        for s in range(nsub):
            b = PS.next()
            pv = ps_bf(b).rearrange("p (c t) -> p c t", c=8)
            for kc in range(8):
                tr(pv[:, kc, 0:Pn], xn[:Pn, s, kc * 128:(kc + 1) * 128], ident_b[:Pn, :Pn],
                   [B_r1, B_const], [B_ps[b]])
            copy_ev(evq.next(), hT[:, :, s * Pn:(s + 1) * Pn], pv[:, :, 0:Pn], [B_ps[b]], [B_r2])

    def norm_to_hT(x_ap, B_xb, nsub, Pn):
        norm_stats(x_ap, B_xb, nsub, Pn)
        norm_apply(x_ap, B_xb, nsub, Pn)

    def proj_fm(wt, wb, nk, col0, NT, srcT, B_src):
        b = PS.next()
        for kc in range(nk):
            mm(psum[b][:, 0:NT], wt[:, kc, col0:col0 + 128], srcT[:, kc, 0:NT], kc == 0, kc == nk - 1,
               [wb, B_src], [B_ps[b]])
        return b

    def proj_tm(wt, wb, nk, s, Pn, srcT, B_src, ncol=512):
        b = PS.next()
        for kc in range(nk):
            mm(psum[b][:Pn, 0:ncol], srcT[:, kc, s * Pn:(s + 1) * Pn], wt[:, kc, 0:ncol], kc == 0, kc == nk - 1,
               [wb, B_src], [B_ps[b]])
        return b

    tanh_tmp = [sb("tanh%d" % i, [128, 512], BF16) for i in range(2)]
    B_tanh = [Buf("tanh0"), Buf("tanh1")]
    tq = Ring([0, 1])

    def gate_evac(b, Pn, out_ap, B_o):
        i = tq.next()
        act(tanh_tmp[i][:Pn, :], psum[b][:Pn, :], AF.Tanh, [B_ps[b]], [B_tanh[i]], scale=0.5)
        stt(out_ap, tanh_tmp[i][:Pn, :], 1.0, psum[b][:Pn, :], ALU.add, ALU.mult, [B_tanh[i], B_ps[b]], [B_o])

    def rope_pair(bA, bB, hd, NT, tab, tcos, tsin, dstT, B_dst):
        A = psum[bA][:, 0:NT]
        Bp = psum[bB][:, 0:NT]
        c = tab[:, tcos, 0:NT]
        s_ = tab[:, tsin, 0:NT]
        t = [ropetmp[i][:, 0:NT] for i in range(4)]
        tt("dve", t[0], A, c, ALU.mult, [B_ps[bA], B_rope], [B_ropetmp[0]])
        tt("dve", t[1], Bp, s_, ALU.mult, [B_ps[bB], B_rope], [B_ropetmp[1]])
        tt("dve", t[2], Bp, c, ALU.mult, [B_ps[bB], B_rope], [B_ropetmp[2]])
        tt("dve", t[3], A, s_, ALU.mult, [B_ps[bA], B_rope], [B_ropetmp[3]])
        tt("pool", dstT[:, 2 * hd, 0:NT], t[0], t[1], ALU.subtract, [B_ropetmp[0], B_ropetmp[1]], [B_dst])
        tt("pool", dstT[:, 2 * hd + 1, 0:NT], t[2], t[3], ALU.add, [B_ropetmp[2], B_ropetmp[3]], [B_dst])

    def in_proj_A(nsub, Pn):
        NT = nsub * Pn
        for g in range(10):
            wt, wb = ws.get(ws_in_a[:, :, g * 512:(g + 1) * 512], 8, 512)
            if g < 4:
                for hh in range(2):
                    hd = (g % 2) * 2 + hh
                    bA = proj_fm(wt, wb, 8, hh * 256, NT, hT, B_r2)
                    bB = proj_fm(wt, wb, 8, hh * 256 + 128, NT, hT, B_r2)
                    if g < 2:
                        rope_pair(bA, bB, hd, NT, rq_t, 2 * hd, 2 * hd + 1, qT, B_qT)
                    else:
                        rope_pair(bA, bB, hd, NT, rk_t, 0, 1, kT, B_kT)
            elif g < 6:
                for s in range(nsub):
                    b = proj_tm(wt, wb, 8, s, Pn, hT, B_r2)
                    copy_ev("act", vtok[:Pn, s, (g - 4) * 512:(g - 3) * 512], psum[b][:Pn, :], [B_ps[b]], [B_vtok])
            elif g < 8:
                for s in range(nsub):
                    b = proj_tm(wt, wb, 8, s, Pn, hT, B_r2)
                    gate_evac(b, Pn, sg[:Pn, s, (g - 6) * 512:(g - 5) * 512], B_sg)
            elif g == 8:
                for hd in range(4):
                    b = proj_fm(wt, wb, 8, hd * 128, NT, hT, B_r2)
                    copy_ev(evq.next(), qmT[:, hd, 0:NT], psum[b][:, 0:NT], [B_ps[b]], [B_qmT])
            else:
                for s in range(nsub):
                    b = proj_tm(wt, wb, 8, s, Pn, hT, B_r2)
                    gate_evac(b, Pn, sgm[:Pn, s, :], B_sgm)

    sbi = [0]

    def retention(nsub, Pn, GL, kwcol, sample):
        for s in range(nsub):
            cs = slice(s * Pn, (s + 1) * Pn)
            bs = PS.next()
            pv = psum[bs][:, :].rearrange("p (h l) -> p h l", h=4)
            for h in range(4):
                for dc in range(2):
                    mm(pv[:Pn, h, 0:Pn], kT[:, 2 * h + dc, cs], qT[:, 2 * h + dc, cs], dc == 0, dc == 1,
                       [B_kT, B_qT], [B_ps[bs]])
            tt("dve", scT[s][:Pn, :, 0:Pn], pv[:Pn, :, 0:Pn], dmask[:Pn, :, 0:Pn], ALU.mult,
               [B_ps[bs], B_const], [B_scT[s]])
            bk = PS.next()
            pk = ps_bf(bk)
            for blk in range(8):
                tr(pk[:Pn, blk * 128:(blk + 1) * 128], kT[:, blk, cs], ident_b[:, :], [B_kT, B_const], [B_ps[bk]])
            for h in range(4):
                act(kt_tok[s][:Pn, h * 256:(h + 1) * 256], pk[:Pn, h * 256:(h + 1) * 256], AF.Copy,
                    [B_ps[bk], B_const], [B_kttok[s]], scale=kwt[:Pn, kwcol + h:kwcol + h + 1])
        for s in range(nsub):
            cs = slice(s * Pn, (s + 1) * Pn)
            if sample:
                dma(S_f[:, :, :, :], st_in[s].rearrange("h (c p) e -> p h c e", p=128), [], B_Sf, "stld")
                for h in range(4):
                    copy_ev("pool", S_b[sbi[0] % 2][:, h, :, :], S_f[:, h, :, :], [B_Sf[h]], [B_Sb[sbi[0] % 2][h]])
            cur, nxt = sbi[0] % 2, (sbi[0] + 1) % 2
            sbi[0] += 1
            bus = []
            for h in range(4):
                bu = PS.next()
                bus.append(bu)
                uv = psum[bu][:, :].rearrange("p (c e) -> p c e", c=2)
                for dc in range(2):
                    mm(uv[:, dc, :], kt_tok[s][:Pn, (2 * h + dc) * 128:(2 * h + dc + 1) * 128],
                       vtok[:Pn, s, h * 256:(h + 1) * 256], True, True, [B_kttok[s], B_vtok], [B_ps[bu]])
            bo = [PS.next(), PS.next()]
            for h in range(4):
                ob = bo[h // 2]
                oview = psum[ob][:Pn, (h % 2) * 256:(h % 2 + 1) * 256]
                mm(oview, scT[s][:Pn, h, 0:Pn], vtok[:Pn, s, h * 256:(h + 1) * 256], True, False,
                   [B_scT[s], B_vtok], [B_ps[ob]])
                for dc in range(2):
                    mm(oview, qT[:, 2 * h + dc, cs], S_b[cur][:, h, dc, :], False, dc == 1,
                       [B_qT, B_Sb[cur][h]], [B_ps[ob]])
            for h in range(4):
                uv = psum[bus[h]][:, :].rearrange("p (c e) -> p c e", c=2)
                stt(S_b[nxt][:, h, :, :], S_f[:, h, :, :], GL[h], uv[:, :, :], ALU.mult, ALU.add,
                    [B_Sf[h], B_ps[bus[h]]], [B_Sb[nxt][h]])
            for h in range(4):
                uv = psum[bus[h]][:, :].rearrange("p (c e) -> p c e", c=2)
                stt(S_f[:, h, :, :], S_f[:, h, :, :], GL[h], uv[:, :, :], ALU.mult, ALU.add,
                    [B_Sf[h], B_ps[bus[h]]], [B_Sf[h]])
            if sample:
                dma(st_s[s].rearrange("h (c p) e -> p h c e", p=128), S_f[:, :, :, :], B_Sf, [], "stst")
            ss4, B_ss4 = sm("ss4", 8, 4)
            rs4, B_rs4 = sm("rs4", 12, 4)
            for h in range(4):
                ob = bo[h // 2]
                oview = psum[ob][:Pn, (h % 2) * 256:(h % 2 + 1) * 256]
                act(otm[:Pn, h % 2, :], oview, AF.Square, [B_ps[ob]], [B_otm[h % 2], B_ss4], accum=ss4[:Pn, h:h + 1])
            rstd_from_ss(ss4[:Pn, 0:4], B_ss4, rs4[:Pn, 0:4], B_rs4, 4, 4.0 / 256, 4.0 * EPS, Pn)
            for h in range(4):
                ob = bo[h // 2]
                oview = psum[ob][:Pn, (h % 2) * 256:(h % 2 + 1) * 256]
                stt(om[:Pn, s, h * 256:(h + 1) * 256], oview, rs4[:Pn, h:h + 1], sg[:Pn, s, h * 256:(h + 1) * 256],
                    ALU.mult, ALU.mult, [B_ps[ob], B_rs4, B_sg], [B_r2])

    def mem_attn(layer, nsub, Pn, slot_of_sub):
        for s in range(nsub):
            sl = slot_of_sub(s)
            for hp in range(2):
                bm = PS.next()
                mview = psum[bm][:, :].rearrange("p (j e) -> p j e", j=2)
                pts = []
                for j in range(2):
                    h = hp * 2 + j
                    bsx = PS.next()
                    sv = psum[bsx][:, :].rearrange("p (m t) -> p m t", m=4)
                    for mc in range(2):
                        mm(sv[:, mc, 0:Pn], memKT[:, sl, layer, h, mc * 128:(mc + 1) * 128],
                           qmT[:, h, s * Pn:(s + 1) * Pn], True, True, [B_mem[sl], B_qmT], [B_ps[bsx]])
                    pt, pb = mPT[(2 * s + hp * 2 + j) % 4]
                    act(pt[:, :, 0:Pn], sv[:, 0:2, 0:Pn], AF.Exp, [B_ps[bsx]], [pb], scale=SC_ATT)
                    pts.append((pt, pb))
                for j in range(2):
                    h = hp * 2 + j
                    pt, pb = pts[j]
                    for mc in range(2):
                        mm(mview[:Pn, j, 0:129], pt[:, mc, 0:Pn], memV[:, sl, layer, mc, h, 0:129],
                           mc == 0, mc == 1, [pb, B_mem[sl]], [B_ps[bm]])
                rr, B_rr = sm("mrr", 16, 2)
                for j in range(2):
                    S.add("dve", lambda hh, j=j, mview=mview: hh.reciprocal(rr[:Pn, j:j + 1], mview[:Pn, j, 128:129]),
                          [B_ps[bm]], [B_rr])
                for j in range(2):
                    h = hp * 2 + j
                    stt(om[:Pn, s, 1024 + h * 128:1024 + (h + 1) * 128], mview[:Pn, j, 0:128], rr[:Pn, j:j + 1],
                        sgm[:Pn, s, h * 128:(h + 1) * 128], ALU.mult, ALU.mult, [B_ps[bm], B_rr, B_sgm], [B_r2])

    def out_proj(wsrc, x_ap, B_xb, nsub, Pn):
        for s in range(nsub):
            b0, b1 = PS.next(), PS.next()
            p0 = ps_bf(b0).rearrange("p (c t) -> p c t", c=8)
            p1 = ps_bf(b1).rearrange("p (c t) -> p c t", c=8)
            for fc in range(12):
                pv, bb = (p0, b0) if fc < 8 else (p1, b1)
                tr(pv[:, fc % 8, 0:Pn], om[:Pn, s, fc * 128:(fc + 1) * 128], ident_b[:Pn, :Pn],
                   [B_r2, B_const], [B_ps[bb]])
            copy_ev(evq.next(), omT[:, 0:8, s * Pn:(s + 1) * Pn], p0[:, :, 0:Pn], [B_ps[b0]], [B_r1])
            copy_ev(evq.next(), omT[:, 8:12, s * Pn:(s + 1) * Pn], p1[:, 0:4, 0:Pn], [B_ps[b1]], [B_r1])
        for cg in range(2):
            banks = [PS.next() for _ in range(nsub)]
            for half in range(2):
                wt, wb = ws.get(wsrc[:, half * 6:(half + 1) * 6, cg * 512:(cg + 1) * 512], 6, 512)
                for s in range(nsub):
                    b = banks[s]
                    for f6 in range(6):
                        fc = half * 6 + f6
                        mm(psum[b][:Pn, 0:512], omT[:, fc, s * Pn:(s + 1) * Pn], wt[:, f6, 0:512], fc == 0, fc == 11,
                           [wb, B_r1], [B_ps[b]])
            for s in range(nsub):
                b = banks[s]
                tt("dve", x_ap[:Pn, s, cg * 512:(cg + 1) * 512], psum[b][:Pn, :], x_ap[:Pn, s, cg * 512:(cg + 1) * 512],
                   ALU.add, [B_ps[b], B_xb], [B_xb])

    stq = Ring([0, 1, 2, 3])

    def kv_proj(nsub, Pn, k_dst, v_dst, kt_dst_fn, v_dst_fn, B_ktd, B_vd):
        S.add("pool", lambda h: h.memset(Vn[:, :, :, 256:258], 1.0), [], [B_Vn])
        for g in range(4):
            wt, wb = ws.get(ws_kv[:, :, g * 512:(g + 1) * 512], 8, 512)
            for s in range(nsub):
                b = proj_tm(wt, wb, 8, s, Pn, hT, B_r2)
                i = stq.next()
                copy_ev("act", stage[i][:Pn, :], psum[b][:Pn, :], [B_ps[b]], [B_stage[i]])
                if g < 2:
                    S.add("act", lambda h, d_=k_dst(s)[:, g * 512:(g + 1) * 512], s_=stage[i][:Pn, :]:
                          h.dma_start(out=d_, in_=s_), reads=[B_stage[i]], key="st%d" % i)
                    bt = PS.next()
                    tv = psum[bt][:, :].rearrange("p (c t) -> p c t", c=4)
                    for blk in range(4):
                        tr(tv[:, blk, 0:Pn], stage[i][:Pn, blk * 128:(blk + 1) * 128], ident_f[:Pn, :Pn],
                           [B_stage[i], B_const], [B_ps[bt]])
                    copy_ev("dve", KTn[:, g * 4:(g + 1) * 4, s * Pn:(s + 1) * Pn], tv[:, :, 0:Pn], [B_ps[bt]], [B_KTn])
                else:
                    S.add("act", lambda h, d_=v_dst(s)[:, (g - 2) * 512:(g - 1) * 512], s_=stage[i][:Pn, :]:
                          h.dma_start(out=d_, in_=s_), reads=[B_stage[i]], key="st%d" % i)
                    copy_ev("pool", Vn[:Pn, s, (g - 2) * 2:(g - 1) * 2, 0:256],
                            stage[i][:Pn, :].rearrange("p (h e) -> p h e", h=2), [B_stage[i]], [B_Vn])
        kt_dst_fn()
        v_dst_fn()

    def in_proj_B(nsub, Pn):
        NT = nsub * Pn
        for g in range(6):
            wt, wb = ws.get(ws_in_b[:, :, g * 512:(g + 1) * 512], 8, 512)
            if g < 2:
                for blk in range(4):
                    b = proj_fm(wt, wb, 8, blk * 128, NT, hT, B_r2)
                    copy_ev(evq.next(), qT[:, g * 4 + blk, 0:NT], psum[b][:, 0:NT], [B_ps[b]], [B_qT])
            elif g < 4:
                for s in range(nsub):
                    b = proj_tm(wt, wb, 8, s, Pn, hT, B_r2)
                    gate_evac(b, Pn, sg[:Pn, s, (g - 2) * 512:(g - 1) * 512], B_sg)
            elif g == 4:
                for hd in range(4):
                    b = proj_fm(wt, wb, 8, hd * 128, NT, hT, B_r2)
                    copy_ev(evq.next(), qmT[:, hd, 0:NT], psum[b][:, 0:NT], [B_ps[b]], [B_qmT])
            else:
                for s in range(nsub):
                    b = proj_tm(wt, wb, 8, s, Pn, hT, B_r2)
                    gate_evac(b, Pn, sgm[:Pn, s, :], B_sgm)

    kvq = Ring([0, 1, 2])
    ptq = Ring([0, 1, 2, 3])

    def attn_tile(blocks):
        LA = 2
        sring = [0, 1, 2, 3]
        steps = []
        cl = []
        for bi, blk in enumerate(blocks):
            nst = sum(c["nkt"] for c in blk["chunks"])
            k = 0
            for c in blk["chunks"]:
                gi = len(cl)
                cl.append(c)
                koff = 0
                for t in range(c["nkt"]):
                    steps.append(dict(b=bi, g=gi, t=t, kp=c["kps"][t], koff=koff, mask=c["masks"][t],
                                      first=(k == 0), last=(k == nst - 1)))
                    koff += c["kps"][t]
                    k += 1
                c["last_step"] = len(steps) - 1
        total = len(steps)
        slot_of = {}
        slot_last = {0: -1, 1: -1, 2: -1}
        state = dict(pv_done=0, nstream=0)

        def load_chunk(gi, force):
            if gi >= len(cl) or gi in slot_of:
                return
            c = cl[gi]
            if "direct" in c:
                slot_of[gi] = None
                return
            ci = state["nstream"] % 3
            if slot_last[ci] >= state["pv_done"] and not force:
                return
            assert slot_last[ci] < state["pv_done"]
            state["nstream"] += 1
            slot_of[gi] = ci
            slot_last[ci] = c["last_step"]
            nk = sum(c["kps"])
            dma(KTc[ci][:, :, 0:nk], c["kt_src"], c["deps"], [B_KTc[ci]], "ktc%d" % ci)
            dma(Vc[ci][:, 0:c["nkt"], :], c["v_src"], c["deps"], [B_Vc[ci]], "vc%d" % ci)

        def emit_scores(st):
            load_chunk(st["g"], True)
            blk = blocks[st["b"]]
            c = cl[st["g"]]
            ci = slot_of[st["g"]]
            kp = st["kp"]
            NQ = blk["nqs"] * blk["Pn"]
            h = blk["h"]
            bs = sring[attn_tile.sr % len(sring)]
            attn_tile.sr += 1
            sv = psum[bs][:, :].rearrange("p (m q) -> p m q", m=2)
            for m in range(2):
                if ci is None:
                    lk = c["direct"][0](m, st["koff"], kp)
                    rb = c["direct"][2]
                else:
                    lk = KTc[ci][:, m, st["koff"]:st["koff"] + kp]
                    rb = [B_KTc[ci]]
                mm(sv[:kp, m, 0:NQ], lk, qT[:, 2 * h + m, blk["qc0"]:blk["qc0"] + NQ],
                   True, True, rb + [B_qT], [B_ps[bs]])
            pi = ptq.next()
            act(PT[pi][:kp, :, 0:NQ], sv[:kp, :, 0:NQ], AF.Exp, [B_ps[bs]], [B_PT[pi]], scale=SC_ATT)
            if st["mask"] is not None:
                tt("pool", PT[pi][:kp, :, 0:NQ], PT[pi][:kp, :, 0:NQ], cmask[:kp, st["mask"], :, 0:NQ], ALU.mult,
                   [B_PT[pi], B_const], [B_PT[pi]])
            st["pi"] = pi

        for i in range(min(LA, total)):
            emit_scores(steps[i])
        for i in range(total):
            if i + LA < total:
                emit_scores(steps[i + LA])
            nxt = steps[min(i + LA, total - 1)]["g"]
            load_chunk(nxt + 1, False)
            load_chunk(nxt + 2, False)
            st = steps[i]
            blk = blocks[st["b"]]
            c = cl[st["g"]]
            ci = slot_of[st["g"]]
            kp, pi, Pn, nqs = st["kp"], st["pi"], blk["Pn"], blk["nqs"]
            obank = [[4 + 2 * j, 5 + 2 * j] for j in range(nqs)]
            for j in range(nqs):
                for m in range(2):
                    ob = obank[j][m]
                    if ci is None:
                        rv = c["direct"][1](st["t"], kp)
                        rb = c["direct"][2]
                    else:
                        rv = Vc[ci][:kp, st["t"], 0:257]
                        rb = [B_Vc[ci]]
                    mm(psum[ob][:Pn, 0:257], PT[pi][:kp, m, j * Pn:(j + 1) * Pn], rv,
                       st["first"], st["last"], [B_PT[pi]] + rb, [B_ps[ob]])
            state["pv_done"] = i + 1
            if st["last"]:
                attn_epilogue(blk, obank)

    def attn_epilogue(blk, obank):
        Pn, nqs, h, sub0 = blk["Pn"], blk["nqs"], blk["h"], blk["sub0"]
        rrs = []
        for j in range(nqs):
            o1, o2 = obank[j]
            rr, B_rr = sm("arr%d" % j, 20 + 4 * j, 4)
            rrs.append((rr, B_rr))
            S.add("dve", lambda hh, rr=rr, o1=o1: hh.reciprocal(rr[:Pn, 0:1], psum[o1][:Pn, 256:257]), [B_ps[o1]], [B_rr])
            S.add("dve", lambda hh, rr=rr, o2=o2: hh.reciprocal(rr[:Pn, 1:2], psum[o2][:Pn, 256:257]), [B_ps[o2]], [B_rr])
            tt("dve", rr[:Pn, 2:3], rr[:Pn, 1:2], lamw[:Pn, 2:3], ALU.mult, [B_rr, B_const], [B_rr])
        for j in range(nqs):
            o1, o2 = obank[j]
            rr, B_rr = rrs[j]
            act(otm[:Pn, j, :], psum[o1][:Pn, 0:256], AF.Copy, [B_ps[o1], B_rr], [B_otm[j]], scale=rr[:Pn, 0:1])
        for j in range(nqs):
            o1, o2 = obank[j]
            rr, B_rr = rrs[j]
            stt(otm[:Pn, j, :], psum[o2][:Pn, 0:256], rr[:Pn, 2:3], otm[:Pn, j, :], ALU.mult, ALU.add,
                [B_ps[o2], B_rr, B_otm[j]], [B_otm[j]])
        for j in range(nqs):
            s = sub0 + j
            ssq, B_ssq = sm("assq%d" % j, 28 + 2 * j, 1)
            rsq, B_rsq = sm("arsq%d" % j, 32 + 2 * j, 1)
            junk = otmp[j][:Pn, 0:256]
            act(junk, otm[:Pn, j, :], AF.Square, [B_otm[j]], [B_otmp[j], B_ssq], accum=ssq[:Pn, 0:1])
            rstd_from_ss(ssq[:Pn, 0:1], B_ssq, rsq[:Pn, 0:1], B_rsq, 1, 4.0 / 256, 4.0 * EPS, Pn)
            stt(om[:Pn, s, h * 256:(h + 1) * 256], otm[:Pn, j, :], rsq[:Pn, 0:1], sg[:Pn, s, h * 256:(h + 1) * 256],
                ALU.mult, ALU.mult, [B_otm[j], B_rsq, B_sg], [B_r2])

    attn_tile.sr = 0

    mst = [AB[:, 12304 + i * 2048:12304 + (i + 1) * 2048].bitcast(F32) for i in range(2)]
    B_mst = [Buf("mst0"), Buf("mst1")]
    msq = Ring([0, 1])

    def mem_kv_prompt(bi):
        S.add("pool", lambda h: h.memset(memV[:, 0, :, :, :, 128:130], 2.0), [], [B_mem[0]])
        for mc in range(2):
            dma(x_t[0][:, mc, :], memp[bi, mc * 128:(mc + 1) * 128, :], [], [B_x[0]], "xld")
        norm_to_hT(x_t[0], B_x[0], 2, 128)
        for l in range(2):
            for g in range(2):
                wt, wb = ws.get(ws_mem[l][:, :, g * 512:(g + 1) * 512], 8, 512)
                for mc in range(2):
                    b = proj_tm(wt, wb, 8, mc, 128, hT, B_r2)
                    i = msq.next()
                    copy_ev("act", mst[i][:, 0:512], psum[b][:, :], [B_ps[b]], [B_mst[i]])
                    dst = (mk_p if g == 0 else mv_p)[l, bi, mc * 128:(mc + 1) * 128, :]
                    dma(dst, mst[i][:, 0:512], [B_mst[i]], [], "mst%d" % i)
                    if g == 1:
                        copy_ev("pool", memV[:, 0, l, mc, :, 0:128],
                                mst[i][:, 0:512].rearrange("p (h e) -> p h e", h=4), [B_mst[i]], [B_mem[0]])
                if g == 0:
                    for hd in range(4):
                        b = proj_fm(wt, wb, 8, hd * 128, 256, hT, B_r2)
                        copy_ev(evq.next(), memKT[:, 0, l, hd, :], psum[b][:, 0:256], [B_ps[b]], [B_mem[0]])

    def mem_kv_sample():
        for sl in range(2):
            S.add("pool", lambda h, sl=sl: h.memset(memV[:, sl, :, :, :, 128:130], 2.0), [], [B_mem[sl]])
            for l in range(2):
                for mc in range(2):
                    i = msq.next()
                    dma(mst[i][:, 0:512], cmk_in[l, sl, mc * 128:(mc + 1) * 128, :], [], [B_mst[i]], "mst%d" % i)
                    bt = PS.next()
                    tv = psum[bt][:, :].rearrange("p (c t) -> p c t", c=4)
                    for hd in range(4):
                        tr(tv[:, hd, :], mst[i][:, hd * 128:(hd + 1) * 128], ident_f[:, :], [B_mst[i], B_const], [B_ps[bt]])
                    copy_ev("dve", memKT[:, sl, l, :, mc * 128:(mc + 1) * 128], tv[:, :, :], [B_ps[bt]], [B_mem[sl]])
                    dma(mst[i][:, 512:1024], cmv_in[l, sl, mc * 128:(mc + 1) * 128, :], [], [B_mst[i]], "mst%d" % i)
                    copy_ev("pool", memV[:, sl, l, mc, :, 0:128],
                            mst[i][:, 512:1024].rearrange("p (h e) -> p h e", h=4), [B_mst[i]], [B_mem[sl]])

    cvK_f = sb("cvK_f", [128, 1024], F32)
    cvV_f = cvK_f
    cvK_b = sb("cvK_b", [128, 8, 128], BF16)
    cvV_b = sb("cvV_b", [128, 4, 258], BF16)
    B_cvKf, B_cvKb, B_cvVb = Buf("cvKf"), Buf("cvKb"), Buf("cvVb")
    B_cvVf = B_cvKf
    conv_todo = []

    def conv_init():
        del conv_todo[:]
        for b in range(2):
            for kt in range(PAST // 128):
                conv_todo.append((b, kt))
        S.add("pool", lambda h: h.memset(cvV_b[:, :, 256:258], 1.0), [], [B_cvVb])

    sgm_f = sgm[:, :, :].rearrange("p s t -> p (s t)").bitcast(F32)
    qmT_f = qmT[:, :, :].rearrange("p s t -> p (s t)").bitcast(F32)
    CVST = {"c": (cvK_f, B_cvKf), "s": (sgm_f, B_sgm), "q": (qmT_f, B_qmT)}

    def cv_next():
        return conv_todo.pop(0) if conv_todo else None

    def cv_loadK(item, st):
        if item is None:
            return
        b, kt = item
        t_, b_ = CVST[st]
        dma(t_[:, :], ck_in[b, kt * 128:(kt + 1) * 128, :], [], [b_], "cvl")

    def cv_loadV(item, st):
        if item is None:
            return
        b, kt = item
        t_, b_ = CVST[st]
        dma(t_[:, :], cv_in[b, kt * 128:(kt + 1) * 128, :], [], [b_], "cvl")

    def cv_transK(item, st):
        if item is None:
            return
        b, kt = item
        t_, b_ = CVST[st]
        for half in range(2):
            bt = PS.next()
            tv = psum[bt][:, :].rearrange("p (c t) -> p c t", c=4)
            for blk in range(4):
                tr(tv[:, blk, :], t_[:, (half * 4 + blk) * 128:(half * 4 + blk + 1) * 128], ident_f[:, :],
                   [b_, B_const], [B_ps[bt]])
            copy_ev(evq.next(), cvK_b[:, half * 4:(half + 1) * 4, :], tv[:, :, :], [B_ps[bt]], [B_cvKb])
        S.add("sp", lambda h, d_=kts_sc[b, :, :, kt * 128:(kt + 1) * 128].rearrange("m p k -> p m k"):
              h.dma_start(out=d_, in_=cvK_b[:, :, :]), reads=[B_cvKb], pw=[B_ktss[b][kt // 4]], key="cvs")

    def cv_castV(item, st):
        if item is None:
            return
        b, kt = item
        t_, b_ = CVST[st]
        copy_ev("pool", cvV_b[:, :, 0:256], t_[:, :].rearrange("p (h e) -> p h e", h=4), [b_], [B_cvVb])
        S.add("sp", lambda h, d_=vs_sc[b, kt, :, :, :]: h.dma_start(out=d_, in_=cvV_b[:, :, :]),
              reads=[B_cvVb], pw=[B_vss[b][kt // 4]], key="cvs")

    def conv_some(n):
        for _ in range(n):
            if not conv_todo:
                return
            b, kt = conv_todo.pop(0)
            dma(cvK_f[:, :], ck_in[b, kt * 128:(kt + 1) * 128, :], [], [B_cvKf], "cvl")
            for half in range(2):
                bt = PS.next()
                tv = psum[bt][:, :].rearrange("p (c t) -> p c t", c=4)
                for blk in range(4):
                    tr(tv[:, blk, :], cvK_f[:, (half * 4 + blk) * 128:(half * 4 + blk + 1) * 128], ident_f[:, :],
                       [B_cvKf, B_const], [B_ps[bt]])
                copy_ev(evq.next(), cvK_b[:, half * 4:(half + 1) * 4, :], tv[:, :, :], [B_ps[bt]], [B_cvKb])
            dma(cvV_f[:, :], cv_in[b, kt * 128:(kt + 1) * 128, :], [], [B_cvVf], "cvl")
            copy_ev("pool", cvV_b[:, :, 0:256], cvV_f[:, :].rearrange("p (h e) -> p h e", h=4), [B_cvVf], [B_cvVb])
            S.add("sp", lambda h, d_=kts_sc[b, :, :, kt * 128:(kt + 1) * 128].rearrange("m p k -> p m k"):
                  h.dma_start(out=d_, in_=cvK_b[:, :, :]), reads=[B_cvKb], pw=[B_ktss[b][kt // 4]], key="cvs")
            S.add("sp", lambda h, d_=vs_sc[b, kt, :, :, :]: h.dma_start(out=d_, in_=cvV_b[:, :, :]),
                  reads=[B_cvVb], pw=[B_vss[b][kt // 4]], key="cvs")

    def final_store(x_ap, B_xb, nsub, Pn, dst):
        ss, B_ss = sm("fss", 40, 4)
        rs, B_rs = sm("frs", 44, 4)
        junk = otm[:, :, :].rearrange("p j e -> p (j e)").bitcast(BF16)
        for s in range(nsub):
            act(junk[:Pn, 0:D], x_ap[:Pn, s, :], AF.Square, [B_xb], B_otm + [B_ss], accum=ss[:Pn, s:s + 1])
        rstd_from_ss(ss[:Pn, 0:nsub], B_ss, rs[:Pn, 0:nsub], B_rs, nsub, 1.0 / D, EPS, Pn)
        for s in range(nsub):
            stt(x_ap[:Pn, s, :], x_ap[:Pn, s, :], rs[:Pn, s:s + 1], nf_b[:Pn, :], ALU.mult, ALU.mult,
                [B_xb, B_rs, B_const], [B_xb])
        dma(dst, x_ap[:Pn, 0:nsub, :], [B_xb], [], "yst")

    def program():
        PS.i = 0
        sbi[0] = 0
        attn_tile.sr = 0
        setup()
        conv_init()
        for bi in range(NBP):
            par = bi % 2
            mem_kv_prompt(bi)
            for h in range(4):
                S.add("pool", lambda hh, h=h: hh.memset(S_f[:, h, :, :], 0.0), [], [B_Sf[h]])
                S.add("pool", lambda hh, h=h, i=sbi[0] % 2: hh.memset(S_b[i][:, h, :, :], 0.0), [], [B_Sb[sbi[0] % 2][h]])
            xcur = 0
            for t in range(NTILE):
                tok0 = t * 512
                xt, bx = x_t[xcur], B_x[xcur]
                if t == 0 or not PIPE_NEXT:
                    dma(xt[:, :, :], xp[bi, tok0:tok0 + 512, :].rearrange("(s p) d -> p s d", p=128), [], [bx], "xld")
                    dma(rq_t[:, :, :], c_rq_p[:, :, tok0:tok0 + 512], [], [B_rope], "rope")
                    dma(rk_t[:, :, :], c_rk_p[:, :, tok0:tok0 + 512], [], [B_rope], "rope")
                    norm_to_hT(xt, bx, 4, 128)
                in_proj_A(4, 128)
                retention(4, 128, GL_P, 0, False)
                cvi = [None] * 4
                if do_sample and INLOOP_CONV:
                    cvi = [cv_next() for _ in range(4)]
                    cv_loadK(cvi[0], "c")
                mem_attn(0, 4, 128, lambda s: 0)
                cv_transK(cvi[0], "c")
                cv_loadV(cvi[0], "s")
                cv_loadK(cvi[1], "q")
                out_proj(ws_out_a, xt, bx, 4, 128)
                cv_castV(cvi[0], "s")
                cv_transK(cvi[1], "q")
                cv_loadV(cvi[1], "c")
                if t == NTILE - 1:
                    dma(st_p[bi].rearrange("h (c p) e -> p h c e", p=128), S_f[:, :, :, :], B_Sf, [], "stst")
                has_next = (t + 1 < NTILE) and PIPE_NEXT
                if has_next:
                    xnx, bxn = x_t[1 - xcur], B_x[1 - xcur]
                    dma(xnx[:, :, :], xp[bi, tok0 + 512:tok0 + 1024, :].rearrange("(s p) d -> p s d", p=128),
                        [], [bxn], "xld")
                    dma(rq_t[:, :, :], c_rq_p[:, :, tok0 + 512:tok0 + 1024], [], [B_rope], "rope")
                    dma(rk_t[:, :, :], c_rk_p[:, :, tok0 + 512:tok0 + 1024], [], [B_rope], "rope")
                norm_to_hT(xt, bx, 4, 128)
                cv_castV(cvi[1], "c")

                def kt_store(par=par, tok0=tok0, t=t):
                    dma(kt_sc[par][:, :, tok0:tok0 + 512].rearrange("m p k -> p m k"), KTn[:, :, :],
                        [B_KTn], [B_ktsc[par][t]], "ktn")

                def v_store(par=par, tok0=tok0, t=t):
                    dma(v_sc[par][tok0 // 128:tok0 // 128 + 4, :, :, :].rearrange("s p h e -> p s h e"),
                        Vn[:, :, :, :], [B_Vn], [B_vsc[par][t]], "vn")

                kv_proj(4, 128,
                        lambda s, bi=bi, tok0=tok0: k_p[bi, tok0 + s * 128:tok0 + (s + 1) * 128, :],
                        lambda s, bi=bi, tok0=tok0: v_p[bi, tok0 + s * 128:tok0 + (s + 1) * 128, :],
                        kt_store, v_store, None, None)
                in_proj_B(4, 128)
                mem_attn(1, 4, 128, lambda s: 0)
                cv_loadK(cvi[2], "c")
                cv_loadV(cvi[2], "s")
                cv_loadK(cvi[3], "q")
                blocks = []
                for qt in range(2):
                    q0 = tok0 + qt * 256
                    nkt_total = (q0 + 256) // 128
                    for h in range(4):
                        chunks = []
                        for cc in range((nkt_total + 3) // 4):
                            nkt = min(4, nkt_total - cc * 4)
                            masks = []
                            for kk in range(nkt):
                                kt_abs = cc * 4 + kk
                                masks.append(None if kt_abs * 128 < q0 else (kt_abs * 128 - q0) // 128)
                            ch = dict(nkt=nkt, kps=[128] * nkt, masks=masks)
                            if cc == t:
                                ch["direct"] = (
                                    lambda m, koff, kp, h=h: KTn[:, 2 * h + m, koff:koff + kp],
                                    lambda tt_, kp, h=h: Vn[:kp, tt_, h, 0:257],
                                    [B_KTn, B_Vn])
                            else:
                                ch["kt_src"] = kt_sc[par][2 * h:2 * h + 2, :, cc * 512:cc * 512 + nkt * 128].rearrange("m p k -> p m k")
                                ch["v_src"] = v_sc[par][cc * 4:cc * 4 + nkt, :, h, :].rearrange("t p e -> p t e")
                                ch["deps"] = [B_ktsc[par][cc], B_vsc[par][cc]]
                            chunks.append(ch)
                        blocks.append(dict(qc0=qt * 256, nqs=2, Pn=128, h=h, sub0=qt * 2, chunks=chunks))
                attn_tile(blocks)
                cv_transK(cvi[2], "c")
                cv_transK(cvi[3], "q")
                cv_castV(cvi[2], "s")
                if has_next:
                    norm_stats(xnx, bxn, 4, 128)
                out_proj(ws_out_b, xt, bx, 4, 128)
                cv_loadV(cvi[3], "c")
                if has_next:
                    norm_apply(xnx, bxn, 4, 128)
                final_store(xt, bx, 4, 128,
                            y_p[bi, tok0:tok0 + 512, :].rearrange("(s p) d -> p s d", p=128))
                cv_castV(cvi[3], "c")
                xcur = 1 - xcur
        if do_sample:
            sample_program()

    def sample_program():
        xt, bx = x_t[0], B_x[0]
        mem_kv_sample()
        conv_some(1000)
        dma(xt[:32, 0:2, :], xs.rearrange("s p d -> p s d"), [], [bx], "xld")
        dma(rq_t[:, :, 0:64], c_rq_s[:, :, :], [], [B_rope], "rope")
        dma(rk_t[:, :, 0:64], c_rk_s[:, :, :], [], [B_rope], "rope")
        norm_to_hT(xt, bx, 2, 32)
        in_proj_A(2, 32)
        retention(2, 32, GL_S, 4, True)
        mem_attn(0, 2, 32, lambda s: s)
        out_proj(ws_out_a, xt, bx, 2, 32)
        norm_to_hT(xt, bx, 2, 32)

        def kt_store():
            for b in range(2):
                dma(kts_sc[b, :, :, PAST:PAST + 32].rearrange("m p k -> p m k"), KTn[:, :, b * 32:(b + 1) * 32],
                    [B_KTn], [B_ktss[b][8]], "ktn")

        def v_store():
            for b in range(2):
                dma(vs_sc[b, 32, 0:32, :, :], Vn[:32, b, :, :], [B_Vn], [B_vss[b][8]], "vn")

        kv_proj(2, 32, lambda s: k_s[s, :, :], lambda s: v_s[s, :, :], lambda: None, lambda: None, None, None)
        in_proj_B(2, 32)
        mem_attn(1, 2, 32, lambda s: s)
        blocks = []
        for b in range(2):
            for h in range(4):
                chunks = []
                for cc in range(8):
                    chunks.append(dict(
                        kt_src=kts_sc[b, 2 * h:2 * h + 2, :, cc * 512:cc * 512 + 512].rearrange("m p k -> p m k"),
                        v_src=vs_sc[b, cc * 4:cc * 4 + 4, :, h, :].rearrange("t p e -> p t e"),
                        nkt=4, kps=[128] * 4, masks=[None] * 4, deps=[B_ktss[b][cc], B_vss[b][cc]]))
                chunks.append(dict(nkt=1, kps=[32], masks=[None], direct=(
                    lambda m, koff, kp, h=h, b=b: KTn[:, 2 * h + m, b * 32:b * 32 + 32],
                    lambda tt_, kp, h=h, b=b: Vn[:32, b, h, 0:257],
                    [B_KTn, B_Vn])))
                blocks.append(dict(qc0=b * 32, nqs=1, Pn=32, h=h, sub0=b, chunks=chunks))
        attn_tile(blocks)
        out_proj(ws_out_b, xt, bx, 2, 32)
        final_store(xt, bx, 2, 32, y_s.rearrange("s p d -> p s d"))

    ab_ranges = [(B_kT, 0, 4096), (B_vtok, 4096, 8192), (B_KTn, 4096, 8192), (B_Vn, 8192, 8192 + 4128)]
    ab_ranges += [(B_kttok[i], 8192 + i * 1024, 8192 + (i + 1) * 1024) for i in range(4)]
    ab_ranges += [(B_ropetmp[i], 12288 + i * 1024, 12288 + (i + 1) * 1024) for i in range(4)]
    ab_ranges += [(B_scT[i], 16384 + i * 512, 16384 + (i + 1) * 512) for i in range(4)]
    ab_ranges += [(B_stage[i], i * 1024, (i + 1) * 1024) for i in range(4)]
    ab_ranges += [(B_PT[i], 12320 + i * 512, 12320 + (i + 1) * 512) for i in range(4)]
    ab_ranges += [(B_KTc[i], 14368 + i * 1024, 14368 + (i + 1) * 1024) for i in range(3)]
    ab_ranges += [(B_Vc[i], 17440 + i * 1032, 17440 + (i + 1) * 1032) for i in range(3)]
    ab_ranges += [(B_otmp[i], 20536 + i * 488, 20536 + (i + 1) * 488) for i in range(2)]
    ab_ranges += [(prep_ld[i][1], i * 4096, (i + 1) * 4096) for i in range(3)]
    ab_ranges += [(prep_cv[i][1], 12288 + i * 2056, 12288 + (i + 1) * 2056) for i in range(3)]
    ab_ranges += [(B_mst[i], 12304 + i * 2048, 12304 + (i + 1) * 2048) for i in range(2)]
    for i in range(len(ab_ranges)):
        for j in range(i + 1, len(ab_ranges)):
            bi_, lo_i, hi_i = ab_ranges[i]
            bj_, lo_j, hi_j = ab_ranges[j]
            if lo_i < hi_j and lo_j < hi_i:
                bi_.overlaps.append(bj_)
                bj_.overlaps.append(bi_)

    S.dry = True
    ws.recording = True
    program()
    S.dry = False
    ws.recording = False
    program()
    S.emit(nc, es)
    es.close()
    return nc, consts


_CACHE = {}


def _get_prog(NBP, T, do_sample=True):
    key = (NBP, T, do_sample)
    if key not in _CACHE:
        _CACHE[key] = build(NBP, T, do_sample)
    return _CACHE[key]


def make_in_maps(inp, consts, NBP, T, pb_of_core):
    f = np.ascontiguousarray
    gains = np.stack([inp["norm_a"][0], inp["norm_kv"], inp["norm_b"][0], inp["norm_mem"][0],
                      inp["norm_mem"][1], inp["norm_f"]]).astype(np.float32)
    shared = dict(
        w_in_a=f(inp["w_in_a"][0]), w_out_a=f(inp["w_out_a"][0]), w_kv=f(inp["w_kv"]),
        w_in_b=f(inp["w_in_b"][0]), w_out_b=f(inp["w_out_b"][0]), w_mem=f(inp["w_mem_kv"]),
        gains=f(gains), subln=f(inp["subln_b"][0]), lam_in=f(inp["diff_lambda"][0].reshape(512)),
        c_rq_p=consts["rq_p"], c_rk_p=consts["rk_p"], c_rq_s=consts["rq_s"], c_rk_s=consts["rk_s"],
        c_dmask=consts["dmask"], c_kw=consts["kw"], c_cmask=consts["cmask"], c_ident=consts["ident"])
    maps = []
    for c in range(NCORES):
        pb = pb_of_core(c)
        m = dict(shared)
        m["xp"] = f(inp["x_prompt"][pb, :T])
        m["memp"] = f(inp["mem_prompt"][pb])
        sbs = [2 * c, 2 * c + 1]
        m["xs"] = f(inp["x_sample"][sbs])
        m["st_in"] = f(inp["state_ret"][0, sbs])
        m["ck_in"] = f(inp["cache_k"][sbs].reshape(2, PAST, D))
        m["cv_in"] = f(inp["cache_v"][sbs].reshape(2, PAST, D))
        m["cmk_in"] = f(inp["cache_mem_k"][:, sbs].reshape(2, 2, 256, 512))
        m["cmv_in"] = f(inp["cache_mem_v"][:, sbs].reshape(2, 2, 256, 512))
        maps.append(m)
    return maps


def kernel(**inp):
    inp = {k: np.asarray(v) for k, v in inp.items()}
    NBP, T = 4, SEQ
    nc, consts = _get_prog(NBP, T)
    maps = make_in_maps(inp, consts, NBP, T, lambda c: list(range(4 * c, 4 * c + 4)))
    res = run_bass_kernel_spmd(nc, maps, core_ids=list(range(NCORES))).results
    B = 32
    y_p = np.concatenate([r["y_p"] for r in res], 0)
    y_s = np.concatenate([r["y_s"] for r in res], 0)
    st_p = np.concatenate([r["st_p"] for r in res], 0)[None]
    k_p = np.concatenate([r["k_p"] for r in res], 0).reshape(B, T, 4, 2, 128)
    v_p = np.concatenate([r["v_p"] for r in res], 0).reshape(B, T, 4, 256)
    mk_p = np.concatenate([r["mk_p"] for r in res], 1).reshape(2, B, 256, 4, 128)
    mv_p = np.concatenate([r["mv_p"] for r in res], 1).reshape(2, B, 256, 4, 128)
    st_s = np.concatenate([r["st_s"] for r in res], 0)[None]
    k_s = np.concatenate([r["k_s"] for r in res], 0).reshape(16, 32, 4, 2, 128)
    v_s = np.concatenate([r["v_s"] for r in res], 0).reshape(16, 32, 4, 256)
    return (y_p, y_s, st_p, k_p, v_p, mk_p, mv_p, st_s, k_s, v_s)
```

```python
import math
from contextlib import ExitStack

import numpy as np
import ml_dtypes

import concourse.bass as bass
import concourse.mybir as mybir
from concourse.bass_utils import run_bass_kernel_spmd

F32 = mybir.dt.float32
BF16 = mybir.dt.bfloat16
AF = mybir.ActivationFunctionType
ALU = mybir.AluOpType

D = 1024
SEQ = 2048
NCORES = 8
PAST = 4096
EPS = 1e-6
LAM_INIT = 0.8 - 0.6 * math.exp(-0.3 * 1)
SC_ATT = 128 ** -0.5
NKT_S = 33
INLOOP_CONV = True
MAX_DMA_INFLIGHT = 16
PIPE_NEXT = True
CONV_AT = (0, 4, 0, 0)


class Buf:
    __slots__ = ("name", "writers", "readers", "overlaps")

    def __init__(self, name):
        self.name = name
        self.writers = {}
        self.readers = {}
        self.overlaps = []


def link(a_list, b_list):
    for a in a_list:
        for b in b_list:
            a.overlaps.append(b)
            b.overlaps.append(a)


class Op:
    __slots__ = ("eng", "fn", "deps", "raw", "is_dma", "key", "dcount", "needs_inc", "inc_val", "pos")

    def __init__(self, eng, fn, is_dma, key):
        self.eng = eng
        self.fn = fn
        self.deps = set()
        self.raw = set()
        self.is_dma = is_dma
        self.key = key
        self.dcount = 0
        self.needs_inc = False
        self.inc_val = 0
        self.pos = 0


class Sched:
    ENGS = ("pe", "act", "dve", "pool", "sp")

    def __init__(self):
        self.ops = {e: [] for e in self.ENGS}
        self.last_dma = {}
        self.dma_hist = []
        self.dma_counts = {}
        self.dry = False
        self.n = 0

    def add(self, eng, fn, reads=(), writes=(), key=None, pw=()):
        if self.dry:
            return None
        is_dma = key is not None
        op = Op(eng, fn, is_dma, key)
        self.n += 1
        op.pos = self.n
        rset = []
        for b in reads:
            rset.append(b)
            rset.extend(b.overlaps)
        wset = []
        for b in writes:
            wset.append(b)
            wset.extend(b.overlaps)
        pset = []
        for b in pw:
            pset.append(b)
            wset.extend(b.overlaps)
        for b in rset:
            for w in b.writers.values():
                op.deps.add(w)
                op.raw.add(w)
        for b in wset:
            for w in b.writers.values():
                op.deps.add(w)
            for r in b.readers.values():
                op.deps.add(r)
        for b in pset:
            for r in b.readers.values():
                op.deps.add(r)
        if is_dma:
            self.dma_hist.append(op)
            if len(self.dma_hist) > MAX_DMA_INFLIGHT:
                old_ = self.dma_hist[-1 - MAX_DMA_INFLIGHT]
                op.deps.add(old_)
            prev = self.last_dma.get(key)
            if prev is not None:
                op.deps.add(prev)
                op.raw.add(prev)
            self.last_dma[key] = op
            self.dma_counts[key] = self.dma_counts.get(key, 0) + 1
            op.dcount = self.dma_counts[key]
        rk = ("dma", key) if is_dma else eng
        for b in reads:
            b.readers[rk] = op
        for b in writes:
            b.writers = {rk: op}
            b.readers = {}
        for b in pw:
            b.writers[rk] = op
        op.deps.discard(op)
        self.ops[eng].append(op)
        return op

    def emit(self, nc, es):
        for e in self.ENGS:
            for op in self.ops[e]:
                need = set()
                for d in op.deps:
                    if d.is_dma:
                        need.add(d)
                    elif d.eng == op.eng and not op.is_dma:
                        if op.eng != "pe":
                            need.add(d)
                    else:
                        need.add(d)
                op.deps = need
                for d in need:
                    if not d.is_dma:
                        d.needs_inc = True
        sems = {}
        for e in self.ENGS:
            sems[e] = es.enter_context(nc.semaphore("sem_" + e))
            c = 0
            for op in self.ops[e]:
                if op.needs_inc and not op.is_dma:
                    c += 1
                    op.inc_val = c
        for k in self.dma_counts:
            sems[("dma", k)] = es.enter_context(nc.semaphore("semd_%s" % (str(k),)))
        self.nsem = len(sems)
        handles = {"pe": nc.tensor, "act": nc.scalar, "dve": nc.vector, "pool": nc.gpsimd, "sp": nc.sync}
        block = es.enter_context(nc.Block())
        sect = {"pe": block.tensor, "act": block.scalar, "dve": block.vector, "pool": block.gpsimd,
                "sp": block.sync}

        def make(e):
            def body(_eng):
                h = handles[e]
                known = {}
                for op in self.ops[e]:
                    waits = {}
                    for d in op.deps:
                        if d.is_dma:
                            sk = ("dma", d.key)
                            v = d.dcount * 16
                        else:
                            sk = d.eng
                            v = d.inc_val
                        if known.get(sk, 0) >= v:
                            continue
                        if waits.get(sk, 0) < v:
                            waits[sk] = v
                    for sk, v in waits.items():
                        h.wait_ge(sems[sk], v)
                        known[sk] = v
                    inst = op.fn(h)
                    if op.is_dma:
                        inst.then_inc(sems[("dma", op.key)], 16)
                    elif op.needs_inc:
                        inst.then_inc(sems[e], 1)
                if e == "sp":
                    for k, op in self.last_dma.items():
                        sk = ("dma", k)
                        v = op.dcount * 16
                        if known.get(sk, 0) < v:
                            h.wait_ge(sems[sk], v)
            return body

        for e in self.ENGS:
            sect[e](make(e))


class Ring:
    def __init__(self, items):
        self.items = items
        self.i = 0

    def next(self):
        it = self.items[self.i % len(self.items)]
        self.i += 1
        return it


class WStream:
    def __init__(self, S, slots, depth):
        self.S = S
        self.slots = slots
        self.depth = depth
        self.plan = []
        self.i = 0
        self.issued = 0
        self.recording = True

    def _issue(self, j):
        src, nk, nco = self.plan[j]
        t, b = self.slots[j % len(self.slots)]
        dst = t[:, 0:nk, 0:nco]
        self.S.add("sp", lambda h, dst=dst, src=src: h.dma_start(out=dst, in_=src),
                   reads=[WSC_BUF], writes=[b], key="w%d" % (j % len(self.slots)))

    def get(self, src, nk, nco):
        if self.recording:
            self.plan.append((src, nk, nco))
            return self.slots[0]
        j = self.i
        self.i += 1
        while self.issued < min(len(self.plan), j + self.depth + 1):
            self._issue(self.issued)
            self.issued += 1
        return self.slots[j % len(self.slots)]


WSC_BUF = Buf("wscratch")


def _consts(T):
    half = 128
    inv = (10000.0 ** (-np.arange(half, dtype=np.float32) / np.float32(half))).astype(np.float32)
    g = 1.0 - 2.0 ** (-5.0 - np.arange(4, dtype=np.float64))
    logg = np.log(g)

    def tables(pos, period):
        ang = (pos.astype(np.float32)[None, :] * inv[:, None]).astype(np.float32)
        c = np.cos(ang.astype(np.float64))
        s = np.sin(ang.astype(np.float64))
        l = (np.arange(len(pos)) % period).astype(np.float64)
        rq = np.zeros((128, 8, len(pos)), np.float64)
        for h in range(4):
            dec = np.exp(logg[h] * (l + 1.0))
            rq[:, 2 * h, :] = c * dec[None, :]
            rq[:, 2 * h + 1, :] = s * dec[None, :]
        rk = np.stack([c, s], axis=1)
        return rq.astype(ml_dtypes.bfloat16), rk.astype(ml_dtypes.bfloat16)

    rq_p, rk_p = tables(np.arange(T), 128)
    rq_s, rk_s = tables(np.concatenate([PAST + np.arange(32)] * 2), 32)
    m = np.arange(128)[:, None].astype(np.float64)
    l = np.arange(128)[None, :].astype(np.float64)
    dmask = np.zeros((128, 4, 128), np.float32)
    kw = np.zeros((128, 8), np.float32)
    for h in range(4):
        dmask[:, h, :] = np.where(l >= m, np.exp(-logg[h] * (m + 1.0)) / 16.0, 0.0)
        kw[:, h] = np.exp(logg[h] * (127.0 - np.arange(128))) / 16.0
        kw[:32, 4 + h] = np.exp(logg[h] * (31.0 - np.arange(32))) / 16.0
    gl_p = [float(np.exp(logg[h] * 128.0)) for h in range(4)]
    gl_s = [float(np.exp(logg[h] * 32.0)) for h in range(4)]
    cm = np.zeros((128, 2, 2, 256), np.float32)
    kk = np.arange(128)[:, None]
    qq = np.arange(256)[None, :]
    cm[:, 0, :, :] = ((kk // 64) <= (qq // 64))[:, None, :]
    cm[:, 1, :, :] = ((2 + kk // 64) <= (qq // 64))[:, None, :]
    ident = np.eye(128, dtype=np.float32)
    return dict(rq_p=rq_p, rk_p=rk_p, rq_s=rq_s, rk_s=rk_s, dmask=dmask, kw=kw,
                cmask=cm.astype(ml_dtypes.bfloat16), ident=ident), gl_p, gl_s


def build(NBP=4, T=SEQ, do_sample=True):
    assert T % 512 == 0
    NTILE = T // 512
    consts, GL_P, GL_S = _consts(T)
    nc = bass.Bass("TRN2", target_bir_lowering=False)
    es = ExitStack()

    def din(name, shape, dt=F32):
        return nc.dram_tensor(name, list(shape), dt, kind="ExternalInput").ap()

    def dout(name, shape, dt=F32):
        return nc.dram_tensor(name, list(shape), dt, kind="ExternalOutput").ap()

    def dscr(name, shape, dt=BF16):
        return nc.dram_tensor(name, list(shape), dt, kind="Internal").ap()

    xp = din("xp", [NBP, T, D])
    xs = din("xs", [2, 32, D])
    st_in = din("st_in", [2, 4, 256, 256])
    ck_in = din("ck_in", [2, PAST, D])
    cv_in = din("cv_in", [2, PAST, D])
    cmk_in = din("cmk_in", [2, 2, 256, 512])
    cmv_in = din("cmv_in", [2, 2, 256, 512])
    memp = din("memp", [NBP, 256, D])
    w_in_a = din("w_in_a", [D, 5120])
    w_out_a = din("w_out_a", [1536, D])
    w_kv = din("w_kv", [D, 2048])
    w_in_b = din("w_in_b", [D, 3072])
    w_out_b = din("w_out_b", [1536, D])
    w_mem = din("w_mem", [2, D, D])
    gains_in = din("gains", [6, D])
    subln_in = din("subln", [256])
    lam_in = din("lam_in", [512])
    c_rq_p = din("c_rq_p", [128, 8, T], BF16)
    c_rk_p = din("c_rk_p", [128, 2, T], BF16)
    c_rq_s = din("c_rq_s", [128, 8, 64], BF16)
    c_rk_s = din("c_rk_s", [128, 2, 64], BF16)
    c_dmask = din("c_dmask", [128, 4, 128])
    c_kw = din("c_kw", [128, 8])
    c_cmask = din("c_cmask", [128, 2, 2, 256], BF16)
    c_ident = din("c_ident", [128, 128])
    y_p = dout("y_p", [NBP, T, D])
    y_s = dout("y_s", [2, 32, D])
    st_p = dout("st_p", [NBP, 4, 256, 256])
    k_p = dout("k_p", [NBP, T, D])
    v_p = dout("v_p", [NBP, T, D])
    mk_p = dout("mk_p", [2, NBP, 256, 512])
    mv_p = dout("mv_p", [2, NBP, 256, 512])
    st_s = dout("st_s", [2, 4, 256, 256])
    k_s = dout("k_s", [2, 32, D])
    v_s = dout("v_s", [2, 32, D])
    ws_in_a = dscr("ws_in_a", [128, 8, 5120])
    ws_out_a = dscr("ws_out_a", [128, 12, 1024])
    ws_kv = dscr("ws_kv", [128, 8, 2048])
    ws_in_b = dscr("ws_in_b", [128, 8, 3072])
    ws_out_b = dscr("ws_out_b", [128, 12, 1024])
    ws_mem = dscr("ws_mem", [2, 128, 8, 1024])
    kt_sc = [dscr("kt_sc%d" % i, [8, 128, T]) for i in range(2)]
    v_sc = [dscr("v_sc%d" % i, [T // 128, 128, 4, 258]) for i in range(2)]
    kts_sc = dscr("kts_sc", [2, 8, 128, NKT_S * 128])
    vs_sc = dscr("vs_sc", [2, NKT_S, 128, 4, 258])

    S = Sched()

    def sb(name, shape, dt):
        return es.enter_context(nc.sbuf_tensor(name, list(shape), dt))

    ident_f = sb("ident_f", [128, 128], F32)
    ident_b = sb("ident_b", [128, 128], BF16)
    dmask = sb("dmask", [128, 4, 128], F32)
    kwt = sb("kwt", [128, 8], F32)
    cmask = sb("cmask", [128, 2, 2, 256], BF16)
    gains = sb("gains_t", [128, 6, 8], F32)
    subln = sb("subln_t", [128, 2], F32)
    nf_b = sb("nf_b", [128, D], F32)
    lamw = sb("lamw", [128, 8], F32)
    mhalf = sb("mhalf", [128, 8], F32)
    B_const = Buf("const")

    x_t = [sb("x%d" % i, [128, 4, D], F32) for i in range(2)]
    B_x = [Buf("x%d" % i) for i in range(2)]
    r1 = sb("r1", [128, 12 * 512], BF16)
    xn = r1[:, 0:4096].rearrange("p (s f) -> p s f", s=4)
    omT = r1[:, :].rearrange("p (c t) -> p c t", c=12)
    B_r1 = Buf("r1")
    lamt = r1[:, 0:1024].bitcast(F32)
    r2 = sb("r2", [128, 4 * 1536], BF16)
    hT = r2[:, 0:4096].rearrange("p (c t) -> p c t", c=8)
    om = r2[:, :].rearrange("p (s f) -> p s f", s=4)
    B_r2 = Buf("r2")
    qT = sb("qT", [128, 8, 512], BF16)
    B_qT = Buf("qT")
    sg = sb("sg", [128, 4, D], BF16)
    B_sg = Buf("sg")
    qmT = sb("qmT", [128, 4, 512], BF16)
    B_qmT = Buf("qmT")
    sgm = sb("sgm", [128, 4, 512], BF16)
    B_sgm = Buf("sgm")
    rq_t = sb("rq_t", [128, 8, 512], BF16)
    rk_t = sb("rk_t", [128, 2, 512], BF16)
    B_rope = Buf("rope")
    S_f = sb("S_f", [128, 4, 2, 256], F32)
    B_Sf = [Buf("Sf%d" % h) for h in range(4)]
    _sbt = sb("S_b", [128, 4, 2, 256], BF16)
    S_b = [_sbt, _sbt]
    _bsb = [Buf("Sb_%d" % h) for h in range(4)]
    B_Sb = [_bsb, _bsb]
    memKT = sb("memKT", [128, 2, 2, 4, 256], BF16)
    memV = sb("memV", [128, 2, 2, 2, 4, 130], BF16)
    B_mem = [Buf("mem%d" % i) for i in range(2)]
    wsl = [(sb("wsl%d" % i, [128, 8, 512], BF16), Buf("wsl%d" % i)) for i in range(3)]
    mPT = [(sb("mPT%d" % i, [128, 2, 128], BF16), Buf("mPT%d" % i)) for i in range(4)]
    small = sb("small", [128, 64], F32)
    B_small = {}

    def sm(name, c0, n):
        if name not in B_small:
            B_small[name] = Buf("sm_" + name)
        return small[:, c0:c0 + n], B_small[name]

    AB = sb("AB", [128, 21512], BF16)
    kT = AB[:, 0:4096].rearrange("p (c t) -> p c t", c=8)
    vtok = AB[:, 4096:8192].rearrange("p (s f) -> p s f", s=4)
    kt_tok = [AB[:, 8192 + i * 1024: 8192 + (i + 1) * 1024] for i in range(4)]
    ropetmp = [AB[:, 12288 + i * 1024: 12288 + (i + 1) * 1024].bitcast(F32) for i in range(4)]
    scT = [AB[:, 16384 + i * 512: 16384 + (i + 1) * 512].rearrange("p (h l) -> p h l", h=4) for i in range(4)]
    B_kT, B_vtok = Buf("kT"), Buf("vtok")
    B_kttok = [Buf("kttok%d" % i) for i in range(4)]
    B_ropetmp = [Buf("ropetmp%d" % i) for i in range(4)]
    B_scT = [Buf("scT%d" % i) for i in range(4)]
    A_bufs = [B_kT, B_vtok] + B_kttok + B_ropetmp + B_scT
    stage = [AB[:, i * 1024:(i + 1) * 1024].bitcast(F32) for i in range(4)]
    KTn = AB[:, 4096:8192].rearrange("p (c t) -> p c t", c=8)
    Vn = AB[:, 8192:8192 + 4128].rearrange("p (s h e) -> p s h e", s=4, h=4)
    PT = [AB[:, 12320 + i * 512: 12320 + (i + 1) * 512].rearrange("p (m q) -> p m q", m=2) for i in range(4)]
    KTc = [AB[:, 14368 + i * 1024: 14368 + (i + 1) * 1024].rearrange("p (m k) -> p m k", m=2) for i in range(3)]
    Vc = [AB[:, 17440 + i * 1032: 17440 + (i + 1) * 1032].rearrange("p (t e) -> p t e", t=4) for i in range(3)]
    otmp = [AB[:, 20536 + i * 488: 20536 + i * 488 + 488] for i in range(2)]
    B_stage = [Buf("stage%d" % i) for i in range(4)]
    B_KTn, B_Vn = Buf("KTn"), Buf("Vn")
    B_PT = [Buf("PT%d" % i) for i in range(4)]
    B_KTc = [Buf("KTc%d" % i) for i in range(3)]
    B_Vc = [Buf("Vc%d" % i) for i in range(3)]
    B_otmp = [Buf("otmp0"), Buf("otmp1")]
    B_bufs = B_stage + [B_KTn, B_Vn] + B_PT + B_KTc + B_Vc + B_otmp
    otm = sb("otm", [128, 2, 256], F32)
    B_otm = [Buf("otm0"), Buf("otm1")]

    psum = [es.enter_context(nc.psum_tensor("ps%d" % i, [128, 512], F32)) for i in range(8)]
    B_ps = [Buf("ps%d" % i) for i in range(8)]
    PS = Ring(list(range(8)))

    def ps_f32(i):
        return psum[i]

    def ps_bf(i):
        return psum[i][:, :].bitcast(BF16)

    ws = WStream(S, wsl, depth=2)
    evq = Ring(["dve", "act"])

    B_ktsc = [[Buf("ktsc%d_%d" % (i, t)) for t in range(NTILE)] for i in range(2)]
    B_vsc = [[Buf("vsc%d_%d" % (i, t)) for t in range(NTILE)] for i in range(2)]
    B_ktss = [[Buf("ktss%d_%d" % (b, c)) for c in range(9)] for b in range(2)]
    B_vss = [[Buf("vss%d_%d" % (b, c)) for c in range(9)] for b in range(2)]
    B_out = Buf("outputs")

    def dma(dst, src, reads, writes, key):
        S.add("sp", lambda h: h.dma_start(out=dst, in_=src), reads=reads, writes=writes, key=key)

    def dma_nc(dst, src, reads, writes, key):
        def f(h):
            with nc.allow_non_contiguous_dma(reason="small constant layout"):
                return h.dma_start(out=dst, in_=src)
        S.add("sp", f, reads=reads, writes=writes, key=key)

    def mm(out, lhsT, rhs, start, stop, reads, writes):
        S.add("pe", lambda h: h.matmul(out, lhsT, rhs, start=start, stop=stop), reads=reads, writes=writes)

    def tr(out, in_, ident, reads, writes):
        S.add("pe", lambda h: h.transpose(out, in_, ident), reads=reads, writes=writes)

    def act(out, in_, func, reads, writes, scale=None, bias=None, accum=None):
        kw = {}
        if scale is not None:
            kw["scale"] = scale
        if bias is not None:
            kw["bias"] = bias
        if accum is not None:
            kw["accum_out"] = accum
        S.add("act", lambda h: h.activation(out, in_, func, **kw), reads=reads, writes=writes)

    def copy_ev(eng, out, in_, reads, writes):
        if eng == "act":
            act(out, in_, AF.Copy, reads, writes)
        else:
            S.add(eng, lambda h: h.tensor_copy(out, in_), reads=reads, writes=writes)

    def tt(eng, out, a, b, op, reads, writes):
        S.add(eng, lambda h: h.tensor_tensor(out, a, b, op), reads=reads, writes=writes)

    def ts(eng, out, a, s1, s2, op0, op1, reads, writes):
        if op1 is None:
            S.add(eng, lambda h: h.tensor_scalar(out, a, s1, None, op0), reads=reads, writes=writes)
        else:
            S.add(eng, lambda h: h.tensor_scalar(out, a, s1, s2, op0, op1), reads=reads, writes=writes)

    def stt(out, a, sc, b, op0, op1, reads, writes):
        S.add("dve", lambda h: h.scalar_tensor_tensor(out, a, sc, b, op0, op1), reads=reads, writes=writes)

    def rstd_from_ss(ss_ap, B_ss, out_ap, B_o, n, mult, epsv, Pn):
        tmp, B_tmp = sm("rs_tmp", 56, 4)
        ts("pool", tmp[:Pn, 0:n], ss_ap, mult, epsv, ALU.mult, ALU.add, [B_ss], [B_tmp])
        tt("pool", out_ap, tmp[:Pn, 0:n], mhalf[:Pn, 0:n], ALU.pow, [B_tmp, B_const], [B_o])

    prep_ld = [(AB[:, i * 4096:(i + 1) * 4096].bitcast(F32), Buf("wld%d" % i)) for i in range(3)]
    prep_cv = [(AB[:, 12288 + i * 2056:12288 + (i + 1) * 2056], Buf("wcv%d" % i)) for i in range(3)]

    def setup():
        dma(ident_f[:, :], c_ident[:, :], [], [B_const], "c0")
        dma(dmask[:, :, :], c_dmask[:, :, :], [], [B_const], "c0")
        dma(kwt[:, :], c_kw[:, :], [], [B_const], "c0")
        dma(cmask[:, :, :, :], c_cmask[:, :, :, :], [], [B_const], "c0")
        dma_nc(gains[:, :, :], gains_in.rearrange("g (k p) -> p g k", p=128), [], [B_const], "c0")
        dma_nc(subln[:, :], subln_in.rearrange("(k p) -> p k", p=128), [], [B_const], "c0")
        dma(nf_b[:, :], gains_in[5:6, :].to_broadcast([128, D]), [], [B_const], "c0")
        dma(lamt[:, :], lam_in.rearrange("(o n) -> o n", o=1).to_broadcast([128, 512]), [], [B_r1], "c0")
        S.add("pool", lambda h: h.memset(mhalf[:, :], -0.5), [], [B_const])
        S.add("dve", lambda h: h.tensor_copy(ident_b[:, :], ident_f[:, :]), [B_const], [B_const])
        ts("dve", subln[:, :], subln[:, :], 1.0 - LAM_INIT, None, ALU.mult, None, [B_const], [B_const])
        tt("dve", lamt[:, 0:128], lamt[:, 0:128], lamt[:, 128:256], ALU.mult, [B_r1], [B_r1])
        tt("dve", lamt[:, 256:384], lamt[:, 256:384], lamt[:, 384:512], ALU.mult, [B_r1], [B_r1])
        S.add("dve", lambda h: h.tensor_reduce(lamw[:, 0:1], lamt[:, 0:128], mybir.AxisListType.X, ALU.add),
              [B_r1], [B_const])
        S.add("dve", lambda h: h.tensor_reduce(lamw[:, 1:2], lamt[:, 256:384], mybir.AxisListType.X, ALU.add),
              [B_r1], [B_const])
        act(lamw[:, 0:2], lamw[:, 0:2], AF.Exp, [B_const], [B_const])
        tt("dve", lamw[:, 2:3], lamw[:, 1:2], lamw[:, 0:1], ALU.subtract, [B_const], [B_const])
        ts("dve", lamw[:, 2:3], lamw[:, 2:3], -LAM_INIT, None, ALU.add, None, [B_const], [B_const])

        xl, xc = [], []
        for i in range(0):
            for j in range(2):
                bb = Buf("xpl%d_%d" % (i, j))
                bb.overlaps.append(B_x[i])
                B_x[i].overlaps.append(bb)
                xl.append((x_t[i][:, 2 * j:2 * j + 2, :].rearrange("p s d -> p (s d)"), bb))
        for k, (rt, rb) in enumerate(()):
            for j in range(2):
                bb = Buf("rcv%d_%d" % (k, j))
                bb.overlaps.append(rb)
                rb.overlaps.append(bb)
                xc.append((rt[:, j * 2048:(j + 1) * 2048], bb))
        NSL = 3
        ldr = Ring(prep_ld)
        cvr = Ring(prep_cv)
        engs = Ring(["dve", "act"])
        ldk = Ring(["pl%d" % i for i in range(NSL)])
        stk = Ring(["ps%d" % i for i in range(NSL)])

        def prep(wsrc, R, C, dst, gain_fn):
            for kc in range(R // 128):
                for c0 in range(0, C, 2048):
                    cw = min(2048, C - c0)
                    (lt, lb), (ct, cb) = ldr.next(), cvr.next()
                    dma(lt[:, 0:cw], wsrc[kc * 128:(kc + 1) * 128, c0:c0 + cw], [], [lb], ldk.next())
                    gsc = gain_fn(kc)
                    e = engs.next()
                    if gsc is None:
                        copy_ev(e, ct[:, 0:cw], lt[:, 0:cw], [lb], [cb])
                    elif e == "act":
                        act(ct[:, 0:cw], lt[:, 0:cw], AF.Copy, [lb, B_const], [cb], scale=gsc)
                    else:
                        ts(e, ct[:, 0:cw], lt[:, 0:cw], gsc, None, ALU.mult, None, [lb, B_const], [cb])
                    S.add("act", lambda h, d_=dst[:, kc, c0:c0 + cw], s_=ct[:, 0:cw]: h.dma_start(out=d_, in_=s_),
                          reads=[cb], pw=[WSC_BUF], key=stk.next())

        prep(w_in_a, D, 5120, ws_in_a, lambda kc: gains[:, 0, kc:kc + 1])
        prep(w_out_a, 1536, D, ws_out_a, lambda kc: None)
        prep(w_kv, D, 2048, ws_kv, lambda kc: gains[:, 1, kc:kc + 1])
        prep(w_in_b, D, 3072, ws_in_b, lambda kc: gains[:, 2, kc:kc + 1])
        prep(w_out_b, 1536, D, ws_out_b, lambda kc: subln[:, (kc % 2):(kc % 2) + 1] if kc < 8 else None)
        for l in range(2):
            prep(w_mem[l], D, D, ws_mem[l], lambda kc, l=l: gains[:, 3 + l, kc:kc + 1])

    def norm_stats(x_ap, B_xb, nsub, Pn):
        ss, B_ss = sm("ss", 0, 4)
        rs, B_rs = sm("rs", 4, 4)
        junk = otm[:, :, :].rearrange("p j e -> p (j e)").bitcast(BF16)
        for s in range(nsub):
            act(junk[:Pn, 0:D], x_ap[:Pn, s, :], AF.Square, [B_xb], B_otm + [B_ss], accum=ss[:Pn, s:s + 1])
        rstd_from_ss(ss[:Pn, 0:nsub], B_ss, rs[:Pn, 0:nsub], B_rs, nsub, 1.0 / D, EPS, Pn)

    def norm_apply(x_ap, B_xb, nsub, Pn):
        rs, B_rs = sm("rs", 4, 4)
        for s in range(nsub):
            ts("dve", xn[:Pn, s, :], x_ap[:Pn, s, :], rs[:Pn, s:s + 1], None, ALU.mult, None,
               [B_xb, B_rs], [B_r1])
        for s in range(nsub):
            b = PS.next()
            pv = ps_bf(b).rearrange("p (c t) -> p c t", c=8)
            for kc in range(8):
                tr(pv[:, kc, 0:Pn], xn[:Pn, s, kc * 128:(kc + 1) * 128], ident_b[:Pn, :Pn],
                   [B_r1, B_const], [B_ps[b]])
            copy_ev(evq.next(), hT[:, :, s * Pn:(s + 1) * Pn], pv[:, :, 0:Pn], [B_ps[b]], [B_r2])

    def norm_to_hT(x_ap, B_xb, nsub, Pn):
        norm_stats(x_ap, B_xb, nsub, Pn)
        norm_apply(x_ap, B_xb, nsub, Pn)

    def proj_fm(wt, wb, nk, col0, NT, srcT, B_src):
        b = PS.next()
        for kc in range(nk):
            mm(psum[b][:, 0:NT], wt[:, kc, col0:col0 + 128], srcT[:, kc, 0:NT], kc == 0, kc == nk - 1,
               [wb, B_src], [B_ps[b]])
        return b

    def proj_tm(wt, wb, nk, s, Pn, srcT, B_src, ncol=512):
        b = PS.next()
        for kc in range(nk):
            mm(psum[b][:Pn, 0:ncol], srcT[:, kc, s * Pn:(s + 1) * Pn], wt[:, kc, 0:ncol], kc == 0, kc == nk - 1,
               [wb, B_src], [B_ps[b]])
        return b

    tanh_tmp = [sb("tanh%d" % i, [128, 512], BF16) for i in range(2)]
    B_tanh = [Buf("tanh0"), Buf("tanh1")]
    tq = Ring([0, 1])

    def gate_evac(b, Pn, out_ap, B_o):
        i = tq.next()
        act(tanh_tmp[i][:Pn, :], psum[b][:Pn, :], AF.Tanh, [B_ps[b]], [B_tanh[i]], scale=0.5)
        stt(out_ap, tanh_tmp[i][:Pn, :], 1.0, psum[b][:Pn, :], ALU.add, ALU.mult, [B_tanh[i], B_ps[b]], [B_o])

    def rope_pair(bA, bB, hd, NT, tab, tcos, tsin, dstT, B_dst):
        A = psum[bA][:, 0:NT]
        Bp = psum[bB][:, 0:NT]
        c = tab[:, tcos, 0:NT]
        s_ = tab[:, tsin, 0:NT]
        t = [ropetmp[i][:, 0:NT] for i in range(4)]
        tt("dve", t[0], A, c, ALU.mult, [B_ps[bA], B_rope], [B_ropetmp[0]])
        tt("dve", t[1], Bp, s_, ALU.mult, [B_ps[bB], B_rope], [B_ropetmp[1]])
        tt("dve", t[2], Bp, c, ALU.mult, [B_ps[bB], B_rope], [B_ropetmp[2]])
        tt("dve", t[3], A, s_, ALU.mult, [B_ps[bA], B_rope], [B_ropetmp[3]])
        tt("pool", dstT[:, 2 * hd, 0:NT], t[0], t[1], ALU.subtract, [B_ropetmp[0], B_ropetmp[1]], [B_dst])
        tt("pool", dstT[:, 2 * hd + 1, 0:NT], t[2], t[3], ALU.add, [B_ropetmp[2], B_ropetmp[3]], [B_dst])

    def in_proj_A(nsub, Pn):
        NT = nsub * Pn
        for g in range(10):
            wt, wb = ws.get(ws_in_a[:, :, g * 512:(g + 1) * 512], 8, 512)
            if g < 4:
                for hh in range(2):
                    hd = (g % 2) * 2 + hh
                    bA = proj_fm(wt, wb, 8, hh * 256, NT, hT, B_r2)
                    bB = proj_fm(wt, wb, 8, hh * 256 + 128, NT, hT, B_r2)
                    if g < 2:
                        rope_pair(bA, bB, hd, NT, rq_t, 2 * hd, 2 * hd + 1, qT, B_qT)
                    else:
                        rope_pair(bA, bB, hd, NT, rk_t, 0, 1, kT, B_kT)
            elif g < 6:
                for s in range(nsub):
                    b = proj_tm(wt, wb, 8, s, Pn, hT, B_r2)
                    copy_ev("act", vtok[:Pn, s, (g - 4) * 512:(g - 3) * 512], psum[b][:Pn, :], [B_ps[b]], [B_vtok])
            elif g < 8:
                for s in range(nsub):
                    b = proj_tm(wt, wb, 8, s, Pn, hT, B_r2)
                    gate_evac(b, Pn, sg[:Pn, s, (g - 6) * 512:(g - 5) * 512], B_sg)
            elif g == 8:
                for hd in range(4):
                    b = proj_fm(wt, wb, 8, hd * 128, NT, hT, B_r2)
                    copy_ev(evq.next(), qmT[:, hd, 0:NT], psum[b][:, 0:NT], [B_ps[b]], [B_qmT])
            else:
                for s in range(nsub):
                    b = proj_tm(wt, wb, 8, s, Pn, hT, B_r2)
                    gate_evac(b, Pn, sgm[:Pn, s, :], B_sgm)

    sbi = [0]

    def retention(nsub, Pn, GL, kwcol, sample):
        for s in range(nsub):
            cs = slice(s * Pn, (s + 1) * Pn)
            bs = PS.next()
            pv = psum[bs][:, :].rearrange("p (h l) -> p h l", h=4)
            for h in range(4):
                for dc in range(2):
                    mm(pv[:Pn, h, 0:Pn], kT[:, 2 * h + dc, cs], qT[:, 2 * h + dc, cs], dc == 0, dc == 1,
                       [B_kT, B_qT], [B_ps[bs]])
            tt("dve", scT[s][:Pn, :, 0:Pn], pv[:Pn, :, 0:Pn], dmask[:Pn, :, 0:Pn], ALU.mult,
               [B_ps[bs], B_const], [B_scT[s]])
            bk = PS.next()
            pk = ps_bf(bk)
            for blk in range(8):
                tr(pk[:Pn, blk * 128:(blk + 1) * 128], kT[:, blk, cs], ident_b[:, :], [B_kT, B_const], [B_ps[bk]])
            for h in range(4):
                act(kt_tok[s][:Pn, h * 256:(h + 1) * 256], pk[:Pn, h * 256:(h + 1) * 256], AF.Copy,
                    [B_ps[bk], B_const], [B_kttok[s]], scale=kwt[:Pn, kwcol + h:kwcol + h + 1])
        for s in range(nsub):
            cs = slice(s * Pn, (s + 1) * Pn)
            if sample:
                dma(S_f[:, :, :, :], st_in[s].rearrange("h (c p) e -> p h c e", p=128), [], B_Sf, "stld")
                for h in range(4):
                    copy_ev("pool", S_b[sbi[0] % 2][:, h, :, :], S_f[:, h, :, :], [B_Sf[h]], [B_Sb[sbi[0] % 2][h]])
            cur, nxt = sbi[0] % 2, (sbi[0] + 1) % 2
            sbi[0] += 1
            bus = []
            for h in range(4):
                bu = PS.next()
                bus.append(bu)
                uv = psum[bu][:, :].rearrange("p (c e) -> p c e", c=2)
                for dc in range(2):
                    mm(uv[:, dc, :], kt_tok[s][:Pn, (2 * h + dc) * 128:(2 * h + dc + 1) * 128],
                       vtok[:Pn, s, h * 256:(h + 1) * 256], True, True, [B_kttok[s], B_vtok], [B_ps[bu]])
            bo = [PS.next(), PS.next()]
            for h in range(4):
                ob = bo[h // 2]
                oview = psum[ob][:Pn, (h % 2) * 256:(h % 2 + 1) * 256]
                mm(oview, scT[s][:Pn, h, 0:Pn], vtok[:Pn, s, h * 256:(h + 1) * 256], True, False,
                   [B_scT[s], B_vtok], [B_ps[ob]])
                for dc in range(2):
                    mm(oview, qT[:, 2 * h + dc, cs], S_b[cur][:, h, dc, :], False, dc == 1,
                       [B_qT, B_Sb[cur][h]], [B_ps[ob]])
            for h in range(4):
                uv = psum[bus[h]][:, :].rearrange("p (c e) -> p c e", c=2)
                stt(S_b[nxt][:, h, :, :], S_f[:, h, :, :], GL[h], uv[:, :, :], ALU.mult, ALU.add,
                    [B_Sf[h], B_ps[bus[h]]], [B_Sb[nxt][h]])
            for h in range(4):
                uv = psum[bus[h]][:, :].rearrange("p (c e) -> p c e", c=2)
                stt(S_f[:, h, :, :], S_f[:, h, :, :], GL[h], uv[:, :, :], ALU.mult, ALU.add,
                    [B_Sf[h], B_ps[bus[h]]], [B_Sf[h]])
            if sample:
                dma(st_s[s].rearrange("h (c p) e -> p h c e", p=128), S_f[:, :, :, :], B_Sf, [], "stst")
            ss4, B_ss4 = sm("ss4", 8, 4)
            rs4, B_rs4 = sm("rs4", 12, 4)
            for h in range(4):
                ob = bo[h // 2]
                oview = psum[ob][:Pn, (h % 2) * 256:(h % 2 + 1) * 256]
                act(otm[:Pn, h % 2, :], oview, AF.Square, [B_ps[ob]], [B_otm[h % 2], B_ss4], accum=ss4[:Pn, h:h + 1])
            rstd_from_ss(ss4[:Pn, 0:4], B_ss4, rs4[:Pn, 0:4], B_rs4, 4, 4.0 / 256, 4.0 * EPS, Pn)
            for h in range(4):
                ob = bo[h // 2]
                oview = psum[ob][:Pn, (h % 2) * 256:(h % 2 + 1) * 256]
                stt(om[:Pn, s, h * 256:(h + 1) * 256], oview, rs4[:Pn, h:h + 1], sg[:Pn, s, h * 256:(h + 1) * 256],
                    ALU.mult, ALU.mult, [B_ps[ob], B_rs4, B_sg], [B_r2])

    def mem_attn(layer, nsub, Pn, slot_of_sub):
        for s in range(nsub):
            sl = slot_of_sub(s)
            for hp in range(2):
                bm = PS.next()
                mview = psum[bm][:, :].rearrange("p (j e) -> p j e", j=2)
                pts = []
                for j in range(2):
                    h = hp * 2 + j
                    bsx = PS.next()
                    sv = psum[bsx][:, :].rearrange("p (m t) -> p m t", m=4)
                    for mc in range(2):
                        mm(sv[:, mc, 0:Pn], memKT[:, sl, layer, h, mc * 128:(mc + 1) * 128],
                           qmT[:, h, s * Pn:(s + 1) * Pn], True, True, [B_mem[sl], B_qmT], [B_ps[bsx]])
                    pt, pb = mPT[(2 * s + hp * 2 + j) % 4]
                    act(pt[:, :, 0:Pn], sv[:, 0:2, 0:Pn], AF.Exp, [B_ps[bsx]], [pb], scale=SC_ATT)
                    pts.append((pt, pb))
                for j in range(2):
                    h = hp * 2 + j
                    pt, pb = pts[j]
                    for mc in range(2):
                        mm(mview[:Pn, j, 0:129], pt[:, mc, 0:Pn], memV[:, sl, layer, mc, h, 0:129],
                           mc == 0, mc == 1, [pb, B_mem[sl]], [B_ps[bm]])
                rr, B_rr = sm("mrr", 16, 2)
                for j in range(2):
                    S.add("dve", lambda hh, j=j, mview=mview: hh.reciprocal(rr[:Pn, j:j + 1], mview[:Pn, j, 128:129]),
                          [B_ps[bm]], [B_rr])
                for j in range(2):
                    h = hp * 2 + j
                    stt(om[:Pn, s, 1024 + h * 128:1024 + (h + 1) * 128], mview[:Pn, j, 0:128], rr[:Pn, j:j + 1],
                        sgm[:Pn, s, h * 128:(h + 1) * 128], ALU.mult, ALU.mult, [B_ps[bm], B_rr, B_sgm], [B_r2])

    def out_proj(wsrc, x_ap, B_xb, nsub, Pn):
        for s in range(nsub):
            b0, b1 = PS.next(), PS.next()
            p0 = ps_bf(b0).rearrange("p (c t) -> p c t", c=8)
            p1 = ps_bf(b1).rearrange("p (c t) -> p c t", c=8)
            for fc in range(12):
                pv, bb = (p0, b0) if fc < 8 else (p1, b1)
                tr(pv[:, fc % 8, 0:Pn], om[:Pn, s, fc * 128:(fc + 1) * 128], ident_b[:Pn, :Pn],
                   [B_r2, B_const], [B_ps[bb]])
            copy_ev(evq.next(), omT[:, 0:8, s * Pn:(s + 1) * Pn], p0[:, :, 0:Pn], [B_ps[b0]], [B_r1])
            copy_ev(evq.next(), omT[:, 8:12, s * Pn:(s + 1) * Pn], p1[:, 0:4, 0:Pn], [B_ps[b1]], [B_r1])
        for cg in range(2):
            banks = [PS.next() for _ in range(nsub)]
            for half in range(2):
                wt, wb = ws.get(wsrc[:, half * 6:(half + 1) * 6, cg * 512:(cg + 1) * 512], 6, 512)
                for s in range(nsub):
                    b = banks[s]
                    for f6 in range(6):
                        fc = half * 6 + f6
                        mm(psum[b][:Pn, 0:512], omT[:, fc, s * Pn:(s + 1) * Pn], wt[:, f6, 0:512], fc == 0, fc == 11,
                           [wb, B_r1], [B_ps[b]])
            for s in range(nsub):
                b = banks[s]
                tt("dve", x_ap[:Pn, s, cg * 512:(cg + 1) * 512], psum[b][:Pn, :], x_ap[:Pn, s, cg * 512:(cg + 1) * 512],
                   ALU.add, [B_ps[b], B_xb], [B_xb])

    stq = Ring([0, 1, 2, 3])

    def kv_proj(nsub, Pn, k_dst, v_dst, kt_dst_fn, v_dst_fn, B_ktd, B_vd):
        S.add("pool", lambda h: h.memset(Vn[:, :, :, 256:258], 1.0), [], [B_Vn])
        for g in range(4):
            wt, wb = ws.get(ws_kv[:, :, g * 512:(g + 1) * 512], 8, 512)
            for s in range(nsub):
                b = proj_tm(wt, wb, 8, s, Pn, hT, B_r2)
                i = stq.next()
                copy_ev("act", stage[i][:Pn, :], psum[b][:Pn, :], [B_ps[b]], [B_stage[i]])
                if g < 2:
                    S.add("act", lambda h, d_=k_dst(s)[:, g * 512:(g + 1) * 512], s_=stage[i][:Pn, :]:
                          h.dma_start(out=d_, in_=s_), reads=[B_stage[i]], key="st%d" % i)
                    bt = PS.next()
                    tv = psum[bt][:, :].rearrange("p (c t) -> p c t", c=4)
                    for blk in range(4):
                        tr(tv[:, blk, 0:Pn], stage[i][:Pn, blk * 128:(blk + 1) * 128], ident_f[:Pn, :Pn],
                           [B_stage[i], B_const], [B_ps[bt]])
                    copy_ev("dve", KTn[:, g * 4:(g + 1) * 4, s * Pn:(s + 1) * Pn], tv[:, :, 0:Pn], [B_ps[bt]], [B_KTn])
                else:
                    S.add("act", lambda h, d_=v_dst(s)[:, (g - 2) * 512:(g - 1) * 512], s_=stage[i][:Pn, :]:
                          h.dma_start(out=d_, in_=s_), reads=[B_stage[i]], key="st%d" % i)
                    copy_ev("pool", Vn[:Pn, s, (g - 2) * 2:(g - 1) * 2, 0:256],
                            stage[i][:Pn, :].rearrange("p (h e) -> p h e", h=2), [B_stage[i]], [B_Vn])
        kt_dst_fn()
        v_dst_fn()

    def in_proj_B(nsub, Pn):
        NT = nsub * Pn
        for g in range(6):
            wt, wb = ws.get(ws_in_b[:, :, g * 512:(g + 1) * 512], 8, 512)
            if g < 2:
                for blk in range(4):
                    b = proj_fm(wt, wb, 8, blk * 128, NT, hT, B_r2)
                    copy_ev(evq.next(), qT[:, g * 4 + blk, 0:NT], psum[b][:, 0:NT], [B_ps[b]], [B_qT])
            elif g < 4:
                for s in range(nsub):
                    b = proj_tm(wt, wb, 8, s, Pn, hT, B_r2)
                    gate_evac(b, Pn, sg[:Pn, s, (g - 2) * 512:(g - 1) * 512], B_sg)
            elif g == 4:
                for hd in range(4):
                    b = proj_fm(wt, wb, 8, hd * 128, NT, hT, B_r2)
                    copy_ev(evq.next(), qmT[:, hd, 0:NT], psum[b][:, 0:NT], [B_ps[b]], [B_qmT])
            else:
                for s in range(nsub):
                    b = proj_tm(wt, wb, 8, s, Pn, hT, B_r2)
                    gate_evac(b, Pn, sgm[:Pn, s, :], B_sgm)

    kvq = Ring([0, 1, 2])
    ptq = Ring([0, 1, 2, 3])

    def attn_tile(blocks):
        LA = 2
        sring = [0, 1, 2, 3]
        steps = []
        cl = []
        for bi, blk in enumerate(blocks):
            nst = sum(c["nkt"] for c in blk["chunks"])
            k = 0
            for c in blk["chunks"]:
                gi = len(cl)
                cl.append(c)
                koff = 0
                for t in range(c["nkt"]):
                    steps.append(dict(b=bi, g=gi, t=t, kp=c["kps"][t], koff=koff, mask=c["masks"][t],
                                      first=(k == 0), last=(k == nst - 1)))
                    koff += c["kps"][t]
                    k += 1
                c["last_step"] = len(steps) - 1
        total = len(steps)
        slot_of = {}
        slot_last = {0: -1, 1: -1, 2: -1}
        state = dict(pv_done=0, nstream=0)

        def load_chunk(gi, force):
            if gi >= len(cl) or gi in slot_of:
                return
            c = cl[gi]
            if "direct" in c:
                slot_of[gi] = None
                return
            ci = state["nstream"] % 3
            if slot_last[ci] >= state["pv_done"] and not force:
                return
            assert slot_last[ci] < state["pv_done"]
            state["nstream"] += 1
            slot_of[gi] = ci
            slot_last[ci] = c["last_step"]
            nk = sum(c["kps"])
            dma(KTc[ci][:, :, 0:nk], c["kt_src"], c["deps"], [B_KTc[ci]], "ktc%d" % ci)
            dma(Vc[ci][:, 0:c["nkt"], :], c["v_src"], c["deps"], [B_Vc[ci]], "vc%d" % ci)

        def emit_scores(st):
            load_chunk(st["g"], True)
            blk = blocks[st["b"]]
            c = cl[st["g"]]
            ci = slot_of[st["g"]]
            kp = st["kp"]
            NQ = blk["nqs"] * blk["Pn"]
            h = blk["h"]
            bs = sring[attn_tile.sr % len(sring)]
            attn_tile.sr += 1
            sv = psum[bs][:, :].rearrange("p (m q) -> p m q", m=2)
            for m in range(2):
                if ci is None:
                    lk = c["direct"][0](m, st["koff"], kp)
                    rb = c["direct"][2]
                else:
                    lk = KTc[ci][:, m, st["koff"]:st["koff"] + kp]
                    rb = [B_KTc[ci]]
                mm(sv[:kp, m, 0:NQ], lk, qT[:, 2 * h + m, blk["qc0"]:blk["qc0"] + NQ],
                   True, True, rb + [B_qT], [B_ps[bs]])
            pi = ptq.next()
            act(PT[pi][:kp, :, 0:NQ], sv[:kp, :, 0:NQ], AF.Exp, [B_ps[bs]], [B_PT[pi]], scale=SC_ATT)
            if st["mask"] is not None:
                tt("pool", PT[pi][:kp, :, 0:NQ], PT[pi][:kp, :, 0:NQ], cmask[:kp, st["mask"], :, 0:NQ], ALU.mult,
                   [B_PT[pi], B_const], [B_PT[pi]])
            st["pi"] = pi

        for i in range(min(LA, total)):
            emit_scores(steps[i])
        for i in range(total):
            if i + LA < total:
                emit_scores(steps[i + LA])
            nxt = steps[min(i + LA, total - 1)]["g"]
            load_chunk(nxt + 1, False)
            load_chunk(nxt + 2, False)
            st = steps[i]
            blk = blocks[st["b"]]
            c = cl[st["g"]]
            ci = slot_of[st["g"]]
            kp, pi, Pn, nqs = st["kp"], st["pi"], blk["Pn"], blk["nqs"]
            obank = [[4 + 2 * j, 5 + 2 * j] for j in range(nqs)]
            for j in range(nqs):
                for m in range(2):
                    ob = obank[j][m]
                    if ci is None:
                        rv = c["direct"][1](st["t"], kp)
                        rb = c["direct"][2]
                    else:
                        rv = Vc[ci][:kp, st["t"], 0:257]
                        rb = [B_Vc[ci]]
                    mm(psum[ob][:Pn, 0:257], PT[pi][:kp, m, j * Pn:(j + 1) * Pn], rv,
                       st["first"], st["last"], [B_PT[pi]] + rb, [B_ps[ob]])
            state["pv_done"] = i + 1
            if st["last"]:
                attn_epilogue(blk, obank)

    def attn_epilogue(blk, obank):
        Pn, nqs, h, sub0 = blk["Pn"], blk["nqs"], blk["h"], blk["sub0"]
        rrs = []
        for j in range(nqs):
            o1, o2 = obank[j]
            rr, B_rr = sm("arr%d" % j, 20 + 4 * j, 4)
            rrs.append((rr, B_rr))
            S.add("dve", lambda hh, rr=rr, o1=o1: hh.reciprocal(rr[:Pn, 0:1], psum[o1][:Pn, 256:257]), [B_ps[o1]], [B_rr])
            S.add("dve", lambda hh, rr=rr, o2=o2: hh.reciprocal(rr[:Pn, 1:2], psum[o2][:Pn, 256:257]), [B_ps[o2]], [B_rr])
            tt("dve", rr[:Pn, 2:3], rr[:Pn, 1:2], lamw[:Pn, 2:3], ALU.mult, [B_rr, B_const], [B_rr])
        for j in range(nqs):
            o1, o2 = obank[j]
            rr, B_rr = rrs[j]
            act(otm[:Pn, j, :], psum[o1][:Pn, 0:256], AF.Copy, [B_ps[o1], B_rr], [B_otm[j]], scale=rr[:Pn, 0:1])
        for j in range(nqs):
            o1, o2 = obank[j]
            rr, B_rr = rrs[j]
            stt(otm[:Pn, j, :], psum[o2][:Pn, 0:256], rr[:Pn, 2:3], otm[:Pn, j, :], ALU.mult, ALU.add,
                [B_ps[o2], B_rr, B_otm[j]], [B_otm[j]])
        for j in range(nqs):
            s = sub0 + j
            ssq, B_ssq = sm("assq%d" % j, 28 + 2 * j, 1)
            rsq, B_rsq = sm("arsq%d" % j, 32 + 2 * j, 1)
            junk = otmp[j][:Pn, 0:256]
            act(junk, otm[:Pn, j, :], AF.Square, [B_otm[j]], [B_otmp[j], B_ssq], accum=ssq[:Pn, 0:1])
            rstd_from_ss(ssq[:Pn, 0:1], B_ssq, rsq[:Pn, 0:1], B_rsq, 1, 4.0 / 256, 4.0 * EPS, Pn)
            stt(om[:Pn, s, h * 256:(h + 1) * 256], otm[:Pn, j, :], rsq[:Pn, 0:1], sg[:Pn, s, h * 256:(h + 1) * 256],
                ALU.mult, ALU.mult, [B_otm[j], B_rsq, B_sg], [B_r2])

    attn_tile.sr = 0

    mst = [AB[:, 12304 + i * 2048:12304 + (i + 1) * 2048].bitcast(F32) for i in range(2)]
    B_mst = [Buf("mst0"), Buf("mst1")]
    msq = Ring([0, 1])

    def mem_kv_prompt(bi):
        S.add("pool", lambda h: h.memset(memV[:, 0, :, :, :, 128:130], 2.0), [], [B_mem[0]])
        for mc in range(2):
            dma(x_t[0][:, mc, :], memp[bi, mc * 128:(mc + 1) * 128, :], [], [B_x[0]], "xld")
        norm_to_hT(x_t[0], B_x[0], 2, 128)
        for l in range(2):
            for g in range(2):
                wt, wb = ws.get(ws_mem[l][:, :, g * 512:(g + 1) * 512], 8, 512)
                for mc in range(2):
                    b = proj_tm(wt, wb, 8, mc, 128, hT, B_r2)
                    i = msq.next()
                    copy_ev("act", mst[i][:, 0:512], psum[b][:, :], [B_ps[b]], [B_mst[i]])
                    dst = (mk_p if g == 0 else mv_p)[l, bi, mc * 128:(mc + 1) * 128, :]
                    dma(dst, mst[i][:, 0:512], [B_mst[i]], [], "mst%d" % i)
                    if g == 1:
                        copy_ev("pool", memV[:, 0, l, mc, :, 0:128],
                                mst[i][:, 0:512].rearrange("p (h e) -> p h e", h=4), [B_mst[i]], [B_mem[0]])
                if g == 0:
                    for hd in range(4):
                        b = proj_fm(wt, wb, 8, hd * 128, 256, hT, B_r2)
                        copy_ev(evq.next(), memKT[:, 0, l, hd, :], psum[b][:, 0:256], [B_ps[b]], [B_mem[0]])

    def mem_kv_sample():
        for sl in range(2):
            S.add("pool", lambda h, sl=sl: h.memset(memV[:, sl, :, :, :, 128:130], 2.0), [], [B_mem[sl]])
            for l in range(2):
                for mc in range(2):
                    i = msq.next()
                    dma(mst[i][:, 0:512], cmk_in[l, sl, mc * 128:(mc + 1) * 128, :], [], [B_mst[i]], "mst%d" % i)
                    bt = PS.next()
                    tv = psum[bt][:, :].rearrange("p (c t) -> p c t", c=4)
                    for hd in range(4):
                        tr(tv[:, hd, :], mst[i][:, hd * 128:(hd + 1) * 128], ident_f[:, :], [B_mst[i], B_const], [B_ps[bt]])
                    copy_ev("dve", memKT[:, sl, l, :, mc * 128:(mc + 1) * 128], tv[:, :, :], [B_ps[bt]], [B_mem[sl]])
                    dma(mst[i][:, 512:1024], cmv_in[l, sl, mc * 128:(mc + 1) * 128, :], [], [B_mst[i]], "mst%d" % i)
                    copy_ev("pool", memV[:, sl, l, mc, :, 0:128],
                            mst[i][:, 512:1024].rearrange("p (h e) -> p h e", h=4), [B_mst[i]], [B_mem[sl]])

    cvK_f = sb("cvK_f", [128, 1024], F32)
    cvV_f = cvK_f
    cvK_b = sb("cvK_b", [128, 8, 128], BF16)
    cvV_b = sb("cvV_b", [128, 4, 258], BF16)
    B_cvKf, B_cvKb, B_cvVb = Buf("cvKf"), Buf("cvKb"), Buf("cvVb")
    B_cvVf = B_cvKf
    conv_todo = []

    def conv_init():
        del conv_todo[:]
        for b in range(2):
            for kt in range(PAST // 128):
                conv_todo.append((b, kt))
        S.add("pool", lambda h: h.memset(cvV_b[:, :, 256:258], 1.0), [], [B_cvVb])

    sgm_f = sgm[:, :, :].rearrange("p s t -> p (s t)").bitcast(F32)
    qmT_f = qmT[:, :, :].rearrange("p s t -> p (s t)").bitcast(F32)
    CVST = {"c": (cvK_f, B_cvKf), "s": (sgm_f, B_sgm), "q": (qmT_f, B_qmT)}

    def cv_next():
        return conv_todo.pop(0) if conv_todo else None

    def cv_loadK(item, st):
        if item is None:
            return
        b, kt = item
        t_, b_ = CVST[st]
        dma(t_[:, :], ck_in[b, kt * 128:(kt + 1) * 128, :], [], [b_], "cvl")

    def cv_loadV(item, st):
        if item is None:
            return
        b, kt = item
        t_, b_ = CVST[st]
        dma(t_[:, :], cv_in[b, kt * 128:(kt + 1) * 128, :], [], [b_], "cvl")

    def cv_transK(item, st):
        if item is None:
            return
        b, kt = item
        t_, b_ = CVST[st]
        for half in range(2):
            bt = PS.next()
            tv = psum[bt][:, :].rearrange("p (c t) -> p c t", c=4)
            for blk in range(4):
                tr(tv[:, blk, :], t_[:, (half * 4 + blk) * 128:(half * 4 + blk + 1) * 128], ident_f[:, :],
                   [b_, B_const], [B_ps[bt]])
            copy_ev(evq.next(), cvK_b[:, half * 4:(half + 1) * 4, :], tv[:, :, :], [B_ps[bt]], [B_cvKb])
        S.add("sp", lambda h, d_=kts_sc[b, :, :, kt * 128:(kt + 1) * 128].rearrange("m p k -> p m k"):
              h.dma_start(out=d_, in_=cvK_b[:, :, :]), reads=[B_cvKb], pw=[B_ktss[b][kt // 4]], key="cvs")

    def cv_castV(item, st):
        if item is None:
            return
        b, kt = item
        t_, b_ = CVST[st]
        copy_ev("pool", cvV_b[:, :, 0:256], t_[:, :].rearrange("p (h e) -> p h e", h=4), [b_], [B_cvVb])
        S.add("sp", lambda h, d_=vs_sc[b, kt, :, :, :]: h.dma_start(out=d_, in_=cvV_b[:, :, :]),
              reads=[B_cvVb], pw=[B_vss[b][kt // 4]], key="cvs")

    def conv_some(n):
        for _ in range(n):
            if not conv_todo:
                return
            b, kt = conv_todo.pop(0)
            dma(cvK_f[:, :], ck_in[b, kt * 128:(kt + 1) * 128, :], [], [B_cvKf], "cvl")
            for half in range(2):
                bt = PS.next()
                tv = psum[bt][:, :].rearrange("p (c t) -> p c t", c=4)
                for blk in range(4):
                    tr(tv[:, blk, :], cvK_f[:, (half * 4 + blk) * 128:(half * 4 + blk + 1) * 128], ident_f[:, :],
                       [B_cvKf, B_const], [B_ps[bt]])
                copy_ev(evq.next(), cvK_b[:, half * 4:(half + 1) * 4, :], tv[:, :, :], [B_ps[bt]], [B_cvKb])
            dma(cvV_f[:, :], cv_in[b, kt * 128:(kt + 1) * 128, :], [], [B_cvVf], "cvl")
            copy_ev("pool", cvV_b[:, :, 0:256], cvV_f[:, :].rearrange("p (h e) -> p h e", h=4), [B_cvVf], [B_cvVb])
            S.add("sp", lambda h, d_=kts_sc[b, :, :, kt * 128:(kt + 1) * 128].rearrange("m p k -> p m k"):
                  h.dma_start(out=d_, in_=cvK_b[:, :, :]), reads=[B_cvKb], pw=[B_ktss[b][kt // 4]], key="cvs")
            S.add("sp", lambda h, d_=vs_sc[b, kt, :, :, :]: h.dma_start(out=d_, in_=cvV_b[:, :, :]),
                  reads=[B_cvVb], pw=[B_vss[b][kt // 4]], key="cvs")

    deferred_A = []
    deferred_B = []

    def flush(lst):
        for f in lst:
            f()
        del lst[:]

    def final_store(x_ap, B_xb, nsub, Pn, dst, defer=False):
        ss, B_ss = sm("fss", 40, 4)
        rs, B_rs = sm("frs", 44, 4)
        junk = otm[:, :, :].rearrange("p j e -> p (j e)").bitcast(BF16)
        for s in range(nsub):
            act(junk[:Pn, 0:D], x_ap[:Pn, s, :], AF.Square, [B_xb], B_otm + [B_ss], accum=ss[:Pn, s:s + 1])
        rstd_from_ss(ss[:Pn, 0:nsub], B_ss, rs[:Pn, 0:nsub], B_rs, nsub, 1.0 / D, EPS, Pn)
        for s in range(nsub):
            stt(x_ap[:Pn, s, :], x_ap[:Pn, s, :], rs[:Pn, s:s + 1], nf_b[:Pn, :], ALU.mult, ALU.mult,
                [B_xb, B_rs, B_const], [B_xb])
        if defer:
            deferred_A.append(lambda: dma(dst, x_ap[:Pn, 0:nsub, :], [B_xb], [], "yst"))
        else:
            dma(dst, x_ap[:Pn, 0:nsub, :], [B_xb], [], "yst")

    def program():
        PS.i = 0
        sbi[0] = 0
        attn_tile.sr = 0
        del deferred_A[:]
        del deferred_B[:]
        setup()
        conv_init()
        for bi in range(NBP):
            par = bi % 2
            flush(deferred_A)
            mem_kv_prompt(bi)
            for h in range(4):
                S.add("pool", lambda hh, h=h: hh.memset(S_f[:, h, :, :], 0.0), [], [B_Sf[h]])
                S.add("pool", lambda hh, h=h, i=sbi[0] % 2: hh.memset(S_b[i][:, h, :, :], 0.0), [], [B_Sb[sbi[0] % 2][h]])
            xcur = 0
            for t in range(NTILE):
                tok0 = t * 512
                xt, bx = x_t[xcur], B_x[xcur]
                if t == 0 or not PIPE_NEXT:
                    dma(xt[:, :, :], xp[bi, tok0:tok0 + 512, :].rearrange("(s p) d -> p s d", p=128), [], [bx], "xld")
                    dma(rq_t[:, :, :], c_rq_p[:, :, tok0:tok0 + 512], [], [B_rope], "rope")
                    dma(rk_t[:, :, :], c_rk_p[:, :, tok0:tok0 + 512], [], [B_rope], "rope")
                    norm_to_hT(xt, bx, 4, 128)
                in_proj_A(4, 128)
                flush(deferred_A)
                retention(4, 128, GL_P, 0, False)
                cvi = [None] * 4
                if do_sample and INLOOP_CONV:
                    cvi = [cv_next() for _ in range(4)]
                    cv_loadK(cvi[0], "c")
                mem_attn(0, 4, 128, lambda s: 0)
                cv_transK(cvi[0], "c")
                cv_loadV(cvi[0], "s")
                cv_loadK(cvi[1], "q")
                out_proj(ws_out_a, xt, bx, 4, 128)
                cv_castV(cvi[0], "s")
                cv_transK(cvi[1], "q")
                cv_loadV(cvi[1], "c")
                if t == NTILE - 1:
                    dma(st_p[bi].rearrange("h (c p) e -> p h c e", p=128), S_f[:, :, :, :], B_Sf, [], "stst")
                has_next = (t + 1 < NTILE) and PIPE_NEXT
                if has_next:
                    xnx, bxn = x_t[1 - xcur], B_x[1 - xcur]
                    dma(xnx[:, :, :], xp[bi, tok0 + 512:tok0 + 1024, :].rearrange("(s p) d -> p s d", p=128),
                        [], [bxn], "xld")
                    dma(rq_t[:, :, :], c_rq_p[:, :, tok0 + 512:tok0 + 1024], [], [B_rope], "rope")
                    dma(rk_t[:, :, :], c_rk_p[:, :, tok0 + 512:tok0 + 1024], [], [B_rope], "rope")
                norm_to_hT(xt, bx, 4, 128)
                cv_castV(cvi[1], "c")

                def kt_store(par=par, tok0=tok0, t=t):
                    dma(kt_sc[par][:, :, tok0:tok0 + 512].rearrange("m p k -> p m k"), KTn[:, :, :],
                        [B_KTn], [B_ktsc[par][t]], "ktn")

                def v_store(par=par, tok0=tok0, t=t):
                    dma(v_sc[par][tok0 // 128:tok0 // 128 + 4, :, :, :].rearrange("s p h e -> p s h e"),
                        Vn[:, :, :, :], [B_Vn], [B_vsc[par][t]], "vn")

                kv_proj(4, 128,
                        lambda s, bi=bi, tok0=tok0: k_p[bi, tok0 + s * 128:tok0 + (s + 1) * 128, :],
                        lambda s, bi=bi, tok0=tok0: v_p[bi, tok0 + s * 128:tok0 + (s + 1) * 128, :],
                        lambda: deferred_B.append(kt_store), lambda: deferred_B.append(v_store), None, None)
                in_proj_B(4, 128)
                flush(deferred_B)
                mem_attn(1, 4, 128, lambda s: 0)
                cv_loadK(cvi[2], "c")
                cv_loadV(cvi[2], "s")
                cv_loadK(cvi[3], "q")
                blocks = []
                for qt in range(2):
                    q0 = tok0 + qt * 256
                    nkt_total = (q0 + 256) // 128
                    for h in range(4):
                        chunks = []
                        for cc in range((nkt_total + 3) // 4):
                            nkt = min(4, nkt_total - cc * 4)
                            masks = []
                            for kk in range(nkt):
                                kt_abs = cc * 4 + kk
                                masks.append(None if kt_abs * 128 < q0 else (kt_abs * 128 - q0) // 128)
                            ch = dict(nkt=nkt, kps=[128] * nkt, masks=masks)
                            if cc == t:
                                ch["direct"] = (
                                    lambda m, koff, kp, h=h: KTn[:, 2 * h + m, koff:koff + kp],
                                    lambda tt_, kp, h=h: Vn[:kp, tt_, h, 0:257],
                                    [B_KTn, B_Vn])
                            else:
                                ch["kt_src"] = kt_sc[par][2 * h:2 * h + 2, :, cc * 512:cc * 512 + nkt * 128].rearrange("m p k -> p m k")
                                ch["v_src"] = v_sc[par][cc * 4:cc * 4 + nkt, :, h, :].rearrange("t p e -> p t e")
                                ch["deps"] = [B_ktsc[par][cc], B_vsc[par][cc]]
                            chunks.append(ch)
                        blocks.append(dict(qc0=qt * 256, nqs=2, Pn=128, h=h, sub0=qt * 2, chunks=chunks))
                attn_tile(blocks)
                cv_transK(cvi[2], "c")
                cv_transK(cvi[3], "q")
                cv_castV(cvi[2], "s")
                if has_next:
                    norm_stats(xnx, bxn, 4, 128)
                out_proj(ws_out_b, xt, bx, 4, 128)
                cv_loadV(cvi[3], "c")
                if has_next:
                    norm_apply(xnx, bxn, 4, 128)
                final_store(xt, bx, 4, 128,
                            y_p[bi, tok0:tok0 + 512, :].rearrange("(s p) d -> p s d", p=128), defer=True)
                cv_castV(cvi[3], "c")
                xcur = 1 - xcur
        flush(deferred_A)
        if do_sample:
            sample_program()

    def sample_program():
        xt, bx = x_t[0], B_x[0]
        mem_kv_sample()
        conv_some(1000)
        dma(xt[:32, 0:2, :], xs.rearrange("s p d -> p s d"), [], [bx], "xld")
        dma(rq_t[:, :, 0:64], c_rq_s[:, :, :], [], [B_rope], "rope")
        dma(rk_t[:, :, 0:64], c_rk_s[:, :, :], [], [B_rope], "rope")
        norm_to_hT(xt, bx, 2, 32)
        in_proj_A(2, 32)
        retention(2, 32, GL_S, 4, True)
        mem_attn(0, 2, 32, lambda s: s)
        out_proj(ws_out_a, xt, bx, 2, 32)
        norm_to_hT(xt, bx, 2, 32)

        def kt_store():
            for b in range(2):
                dma(kts_sc[b, :, :, PAST:PAST + 32].rearrange("m p k -> p m k"), KTn[:, :, b * 32:(b + 1) * 32],
                    [B_KTn], [B_ktss[b][8]], "ktn")

        def v_store():
            for b in range(2):
                dma(vs_sc[b, 32, 0:32, :, :], Vn[:32, b, :, :], [B_Vn], [B_vss[b][8]], "vn")

        kv_proj(2, 32, lambda s: k_s[s, :, :], lambda s: v_s[s, :, :], lambda: None, lambda: None, None, None)
        in_proj_B(2, 32)
        mem_attn(1, 2, 32, lambda s: s)
        blocks = []
        for b in range(2):
            for h in range(4):
                chunks = []
                for cc in range(8):
                    chunks.append(dict(
                        kt_src=kts_sc[b, 2 * h:2 * h + 2, :, cc * 512:cc * 512 + 512].rearrange("m p k -> p m k"),
                        v_src=vs_sc[b, cc * 4:cc * 4 + 4, :, h, :].rearrange("t p e -> p t e"),
                        nkt=4, kps=[128] * 4, masks=[None] * 4, deps=[B_ktss[b][cc], B_vss[b][cc]]))
                chunks.append(dict(nkt=1, kps=[32], masks=[None], direct=(
                    lambda m, koff, kp, h=h, b=b: KTn[:, 2 * h + m, b * 32:b * 32 + 32],
                    lambda tt_, kp, h=h, b=b: Vn[:32, b, h, 0:257],
                    [B_KTn, B_Vn])))
                blocks.append(dict(qc0=b * 32, nqs=1, Pn=32, h=h, sub0=b, chunks=chunks))
        attn_tile(blocks)
        out_proj(ws_out_b, xt, bx, 2, 32)
        final_store(xt, bx, 2, 32, y_s.rearrange("s p d -> p s d"))

    ab_ranges = [(B_kT, 0, 4096), (B_vtok, 4096, 8192), (B_KTn, 4096, 8192), (B_Vn, 8192, 8192 + 4128)]
    ab_ranges += [(B_kttok[i], 8192 + i * 1024, 8192 + (i + 1) * 1024) for i in range(4)]
    ab_ranges += [(B_ropetmp[i], 12288 + i * 1024, 12288 + (i + 1) * 1024) for i in range(4)]
    ab_ranges += [(B_scT[i], 16384 + i * 512, 16384 + (i + 1) * 512) for i in range(4)]
    ab_ranges += [(B_stage[i], i * 1024, (i + 1) * 1024) for i in range(4)]
    ab_ranges += [(B_PT[i], 12320 + i * 512, 12320 + (i + 1) * 512) for i in range(4)]
    ab_ranges += [(B_KTc[i], 14368 + i * 1024, 14368 + (i + 1) * 1024) for i in range(3)]
    ab_ranges += [(B_Vc[i], 17440 + i * 1032, 17440 + (i + 1) * 1032) for i in range(3)]
    ab_ranges += [(B_otmp[i], 20536 + i * 488, 20536 + (i + 1) * 488) for i in range(2)]
    ab_ranges += [(prep_ld[i][1], i * 4096, (i + 1) * 4096) for i in range(3)]
    ab_ranges += [(prep_cv[i][1], 12288 + i * 2056, 12288 + (i + 1) * 2056) for i in range(3)]
    ab_ranges += [(B_mst[i], 12304 + i * 2048, 12304 + (i + 1) * 2048) for i in range(2)]
    for i in range(len(ab_ranges)):
        for j in range(i + 1, len(ab_ranges)):
            bi_, lo_i, hi_i = ab_ranges[i]
            bj_, lo_j, hi_j = ab_ranges[j]
            if lo_i < hi_j and lo_j < hi_i:
                bi_.overlaps.append(bj_)
                bj_.overlaps.append(bi_)

    S.dry = True
    ws.recording = True
    program()
    S.dry = False
    ws.recording = False
    program()
    S.emit(nc, es)
    es.close()
    return nc, consts


_CACHE = {}


def _get_prog(NBP, T, do_sample=True):
    key = (NBP, T, do_sample)
    if key not in _CACHE:
        _CACHE[key] = build(NBP, T, do_sample)
    return _CACHE[key]


def make_in_maps(inp, consts, NBP, T, pb_of_core):
    f = np.ascontiguousarray
    gains = np.stack([inp["norm_a"][0], inp["norm_kv"], inp["norm_b"][0], inp["norm_mem"][0],
                      inp["norm_mem"][1], inp["norm_f"]]).astype(np.float32)
    shared = dict(
        w_in_a=f(inp["w_in_a"][0]), w_out_a=f(inp["w_out_a"][0]), w_kv=f(inp["w_kv"]),
        w_in_b=f(inp["w_in_b"][0]), w_out_b=f(inp["w_out_b"][0]), w_mem=f(inp["w_mem_kv"]),
        gains=f(gains), subln=f(inp["subln_b"][0]), lam_in=f(inp["diff_lambda"][0].reshape(512)),
        c_rq_p=consts["rq_p"], c_rk_p=consts["rk_p"], c_rq_s=consts["rq_s"], c_rk_s=consts["rk_s"],
        c_dmask=consts["dmask"], c_kw=consts["kw"], c_cmask=consts["cmask"], c_ident=consts["ident"])
    maps = []
    for c in range(NCORES):
        pb = pb_of_core(c)
        m = dict(shared)
        m["xp"] = f(inp["x_prompt"][pb, :T])
        m["memp"] = f(inp["mem_prompt"][pb])
        sbs = [2 * c, 2 * c + 1]
        m["xs"] = f(inp["x_sample"][sbs])
        m["st_in"] = f(inp["state_ret"][0, sbs])
        m["ck_in"] = f(inp["cache_k"][sbs].reshape(2, PAST, D))
        m["cv_in"] = f(inp["cache_v"][sbs].reshape(2, PAST, D))
        m["cmk_in"] = f(inp["cache_mem_k"][:, sbs].reshape(2, 2, 256, 512))
        m["cmv_in"] = f(inp["cache_mem_v"][:, sbs].reshape(2, 2, 256, 512))
        maps.append(m)
    return maps


def kernel(**inp):
    inp = {k: np.asarray(v) for k, v in inp.items()}
    NBP, T = 4, SEQ
    nc, consts = _get_prog(NBP, T)
    maps = make_in_maps(inp, consts, NBP, T, lambda c: list(range(4 * c, 4 * c + 4)))
    res = run_bass_kernel_spmd(nc, maps, core_ids=list(range(NCORES))).results
    B = 32
    y_p = np.concatenate([r["y_p"] for r in res], 0)
    y_s = np.concatenate([r["y_s"] for r in res], 0)
    st_p = np.concatenate([r["st_p"] for r in res], 0)[None]
    k_p = np.concatenate([r["k_p"] for r in res], 0).reshape(B, T, 4, 2, 128)
    v_p = np.concatenate([r["v_p"] for r in res], 0).reshape(B, T, 4, 256)
    mk_p = np.concatenate([r["mk_p"] for r in res], 1).reshape(2, B, 256, 4, 128)
    mv_p = np.concatenate([r["mv_p"] for r in res], 1).reshape(2, B, 256, 4, 128)
    st_s = np.concatenate([r["st_s"] for r in res], 0)[None]
    k_s = np.concatenate([r["k_s"] for r in res], 0).reshape(16, 32, 4, 2, 128)
    v_s = np.concatenate([r["v_s"] for r in res], 0).reshape(16, 32, 4, 256)
    return (y_p, y_s, st_p, k_p, v_p, mk_p, mv_p, st_s, k_s, v_s)
```
